# Optimizing a Trainium2 kernel written in Bass

```python
import math
import jax, jax.numpy as jnp
from jax import lax
import numpy as np

D_MODEL = 2048
BATCH = 4
SEQ = 4096
DEPTH = 2

EPS = 1e-6
N_BRANCHES = 3
C_CONV = D_MODEL // 2
CONV_WIDTH = 31
MLA_HEADS = 8
NOPE_DIM = 128
ROPE_DIM = 64
V_DIM = 128
Q_LORA = D_MODEL // 4
KV_LORA = D_MODEL // 8
ROPE_BASE = 10000.0
MAX_POS_OFFSET = 1024
Q_BLOCK = 128
MLSTM_HEADS = 4
D_MLSTM = D_MODEL // 2
MLSTM_HEAD_DIM = D_MLSTM // MLSTM_HEADS
MLSTM_CONV_WIDTH = 4
MLSTM_CHUNK = 64
N_MEM = 256
XATTN_HEADS = 4
XATTN_HEAD_DIM = 128
FFN_HIDDEN = ((8 * D_MODEL + 3 * 256 - 1) // (3 * 256)) * 256
IN_COLS = 2 * C_CONV + Q_LORA + KV_LORA + ROPE_DIM + 2 * D_MLSTM + N_BRANCHES * D_MODEL

kernel_name = 'hybrid_conformer_mla_mlstm_block'


def rms_norm(x, g, eps=EPS):
    xf = x.astype(jnp.float32)
    y = xf * lax.rsqrt(jnp.mean(xf * xf, axis=-1, keepdims=True) + eps)
    return (y * g.astype(jnp.float32)).astype(x.dtype)


def layer_norm(x, g, b, eps=EPS):
    xf = x.astype(jnp.float32)
    mu = jnp.mean(xf, axis=-1, keepdims=True)
    var = jnp.mean(jnp.square(xf - mu), axis=-1, keepdims=True)
    y = (xf - mu) * lax.rsqrt(var + eps)
    return (y * g.astype(jnp.float32) + b.astype(jnp.float32)).astype(x.dtype)


def causal_depthwise_conv(x, w, b):
    width = w.shape[0]
    y = lax.conv_general_dilated(x, w[:, None, :], window_strides=(1,), padding=[(width - 1, 0)],
                                 dimension_numbers=('NWC', 'WIO', 'NWC'),
                                 feature_group_count=x.shape[-1])
    return y + b


def rope(x, positions):
    half = x.shape[-1] // 2
    inv_freq = ROPE_BASE ** (-jnp.arange(half, dtype=jnp.float32) / half)
    ang = positions.astype(jnp.float32)[:, :, None, None] * inv_freq
    cos, sin = jnp.cos(ang), jnp.sin(ang)
    xf = x.astype(jnp.float32)
    x1, x2 = xf[..., :half], xf[..., half:]
    return jnp.concatenate([x1 * cos - x2 * sin, x1 * sin + x2 * cos], axis=-1).astype(x.dtype)


def causal_block_attention(q, k, v, scale):
    B, S, H, Dq = q.shape
    nb = S // Q_BLOCK
    qb = q.reshape(B, nb, Q_BLOCK, H, Dq).transpose(1, 0, 2, 3, 4)
    kpos = jnp.arange(S)

    def one_block(args):
        q_blk, i = args
        s = jnp.einsum('bqhd,bkhd->bhqk', q_blk, k).astype(jnp.float32) * scale
        qpos = i * Q_BLOCK + jnp.arange(Q_BLOCK)
        s = jnp.where(kpos[None, :] <= qpos[:, None], s, -jnp.inf)
        p = jax.nn.softmax(s, axis=-1).astype(v.dtype)
        return jnp.einsum('bhqk,bkhd->bqhd', p, v)

    out = lax.map(one_block, (qb, jnp.arange(nb)))
    return out.transpose(1, 0, 2, 3, 4).reshape(B, S, H, v.shape[-1])


def conformer_conv_branch(conv_in, conv_dw, conv_dw_b, conv_ln_g, conv_ln_b):
    a, g = jnp.split(conv_in, 2, axis=-1)
    u = a * jax.nn.sigmoid(g)
    u = causal_depthwise_conv(u, conv_dw, conv_dw_b)
    return jax.nn.silu(layer_norm(u, conv_ln_g, conv_ln_b))


def mla_branch(cq, ckv, k_rope, positions, q_norm_g, kv_norm_g, w_q_up, w_kv_up, g_q, g_k):
    B, S, _ = cq.shape
    q = (rms_norm(cq, q_norm_g) @ w_q_up).reshape(B, S, MLA_HEADS, NOPE_DIM + ROPE_DIM)
    kv = (rms_norm(ckv, kv_norm_g) @ w_kv_up).reshape(B, S, MLA_HEADS, NOPE_DIM + V_DIM)
    k_nope, v = kv[..., :NOPE_DIM], kv[..., NOPE_DIM:]
    q_nope = rms_norm(q[..., :NOPE_DIM], g_q[:NOPE_DIM])
    q_rot = rope(rms_norm(q[..., NOPE_DIM:], g_q[NOPE_DIM:]), positions)
    k_nope = rms_norm(k_nope, g_k[:NOPE_DIM])
    k_rot = rope(rms_norm(k_rope, g_k[NOPE_DIM:])[:, :, None, :], positions)
    q = jnp.concatenate([q_nope, q_rot], axis=-1)
    k = jnp.concatenate([k_nope, jnp.broadcast_to(k_rot, (B, S, MLA_HEADS, ROPE_DIM))], axis=-1)
    o = causal_block_attention(q, k, v, (NOPE_DIM + ROPE_DIM) ** -0.5)
    return o.reshape(B, S, MLA_HEADS * V_DIM)


def mlstm_chunkwise(q, k, v, log_i, log_f):
    B, S, H, DH = q.shape
    L = MLSTM_CHUNK
    NC = S // L

    def to_chunks(t):
        return t.astype(jnp.float32).reshape(B, NC, L, H, DH).transpose(1, 0, 3, 2, 4)

    def gate_chunks(t):
        return t.astype(jnp.float32).reshape(B, NC, L, H).transpose(1, 0, 3, 2)

    qc, kc, vc = to_chunks(q), to_chunks(k) * (DH ** -0.5), to_chunks(v)
    lic, lfc = gate_chunks(log_i), gate_chunks(log_f)
    causal = jnp.tril(jnp.ones((L, L), dtype=bool))

    def step(carry, xs):
        C, n, m = carry
        qj, kj, vj, li, lf = xs
        b = jnp.cumsum(lf, axis=-1)
        Dlog = b[..., :, None] - b[..., None, :] + li[..., None, :]
        Dlog = jnp.where(causal, Dlog, -jnp.inf)
        inter = b + m[..., None]
        m_comb = jnp.maximum(inter, jnp.max(Dlog, axis=-1))
        w_intra = jnp.exp(Dlog - m_comb[..., None])
        w_inter = jnp.exp(inter - m_comb)
        s = jnp.einsum('bhid,bhjd->bhij', qj, kj) * w_intra
        num = w_inter[..., None] * jnp.einsum('bhid,bhde->bhie', qj, C) + jnp.einsum('bhij,bhje->bhie', s, vj)
        den = w_inter * jnp.einsum('bhid,bhd->bhi', qj, n) + jnp.sum(s, axis=-1)
        h = num / jnp.maximum(jnp.abs(den), jnp.exp(-m_comb))[..., None]
        bL = b[..., -1]
        g = bL[..., None] - b + li
        m_new = jnp.maximum(bL + m, jnp.max(g, axis=-1))
        wg = jnp.exp(g - m_new[..., None])
        decay = jnp.exp(bL + m - m_new)
        C_new = decay[..., None, None] * C + jnp.einsum('bhs,bhsd,bhse->bhde', wg, kj, vj)
        n_new = decay[..., None] * n + jnp.einsum('bhs,bhsd->bhd', wg, kj)
        return (C_new, n_new, m_new), h

    init = (jnp.zeros((B, H, DH, DH), jnp.float32), jnp.zeros((B, H, DH), jnp.float32),
            jnp.zeros((B, H), jnp.float32))
    _, hs = lax.scan(step, init, (qc, kc, vc, lic, lfc))
    return hs.transpose(1, 0, 3, 2, 4).reshape(B, S, H, DH).astype(q.dtype)


def mlstm_branch(xm, z, conv_w, conv_b, w_mq, w_mk, w_mv, w_if, b_if, gn_g, skip):
    B, S, _ = xm.shape
    xc = jax.nn.silu(causal_depthwise_conv(xm, conv_w, conv_b))
    xch = xc.reshape(B, S, MLSTM_HEADS, MLSTM_HEAD_DIM)
    xmh = xm.reshape(B, S, MLSTM_HEADS, MLSTM_HEAD_DIM)
    q = jnp.einsum('bshd,hde->bshe', xch, w_mq)
    k = jnp.einsum('bshd,hde->bshe', xch, w_mk)
    v = jnp.einsum('bshd,hde->bshe', xmh, w_mv)
    qkv = jnp.concatenate([q.reshape(B, S, D_MLSTM), k.reshape(B, S, D_MLSTM), v.reshape(B, S, D_MLSTM)], axis=-1)
    if_pre = (qkv @ w_if + b_if).astype(jnp.float32)
    log_i = if_pre[..., :MLSTM_HEADS]
    log_f = jax.nn.log_sigmoid(if_pre[..., MLSTM_HEADS:])
    h = mlstm_chunkwise(q, k, v, log_i, log_f)
    hf = h.astype(jnp.float32)
    mu = jnp.mean(hf, axis=-1, keepdims=True)
    var = jnp.mean(jnp.square(hf - mu), axis=-1, keepdims=True)
    hn = ((hf - mu) * lax.rsqrt(var + EPS) * gn_g.reshape(MLSTM_HEADS, MLSTM_HEAD_DIM).astype(jnp.float32)).astype(xm.dtype)
    hn = hn.reshape(B, S, D_MLSTM) + skip * xc
    return hn * jax.nn.silu(z)


def hybrid_mixer(h, positions, w_in, b_gate, conv_dw, conv_dw_b, conv_ln_g, conv_ln_b, w_conv_out,
                 mla_q_norm, mla_kv_norm, w_q_up, w_kv_up, mla_g_q, mla_g_k, w_mla_out,
                 mlstm_conv_w, mlstm_conv_b, w_mq, w_mk, w_mv, w_if, b_if, mlstm_gn_g, mlstm_skip,
                 w_mlstm_out, w_mix_out):
    B, S, _ = h.shape
    proj = h @ w_in
    sizes = (2 * C_CONV, Q_LORA, KV_LORA, ROPE_DIM, D_MLSTM, D_MLSTM, N_BRANCHES * D_MODEL)
    conv_in, cq, ckv, k_rope, xm, z, gate_pre = jnp.split(proj, np.cumsum(sizes)[:-1].tolist(), axis=-1)
    gates = jax.nn.sigmoid(gate_pre + b_gate).reshape(B, S, N_BRANCHES, D_MODEL)
    y_conv = conformer_conv_branch(conv_in, conv_dw, conv_dw_b, conv_ln_g, conv_ln_b) @ w_conv_out
    y_mla = mla_branch(cq, ckv, k_rope, positions, mla_q_norm, mla_kv_norm, w_q_up, w_kv_up,
                       mla_g_q, mla_g_k) @ w_mla_out
    y_mlstm = mlstm_branch(xm, z, mlstm_conv_w, mlstm_conv_b, w_mq, w_mk, w_mv, w_if, b_if,
                           mlstm_gn_g, mlstm_skip) @ w_mlstm_out
    merged = gates[:, :, 0] * y_conv + gates[:, :, 1] * y_mla + gates[:, :, 2] * y_mlstm
    return merged @ w_mix_out


def memory_cross_attention(h, mem_n, w_xq, w_xkv, g_q, g_k, w_xo):
    B, S, _ = h.shape
    M = mem_n.shape[1]
    q = rms_norm((h @ w_xq).reshape(B, S, XATTN_HEADS, XATTN_HEAD_DIM), g_q)
    kv = (mem_n @ w_xkv).reshape(B, M, 2, XATTN_HEADS, XATTN_HEAD_DIM)
    k = rms_norm(kv[:, :, 0], g_k)
    v = kv[:, :, 1]
    s = jnp.einsum('bshd,bmhd->bhsm', q, k).astype(jnp.float32) * (XATTN_HEAD_DIM ** -0.5)
    p = jax.nn.softmax(s, axis=-1).astype(v.dtype)
    o = jnp.einsum('bhsm,bmhd->bshd', p, v).reshape(B, S, XATTN_HEADS * XATTN_HEAD_DIM)
    return o @ w_xo


def swiglu_ffn(h, w_ffn_in, w_ffn_out):
    gate, up = jnp.split(h @ w_ffn_in, 2, axis=-1)
    return (jax.nn.silu(gate) * up) @ w_ffn_out


def setup_inputs(seed: int = 0) -> dict:
    key = jax.random.key(seed)
    ks = iter(jax.random.split(key, 64))
    f32 = jnp.float32
    Lr = DEPTH

    def w(shape, fan_in):
        return jax.random.normal(next(ks), shape, f32) * (fan_in ** -0.5)

    def gain(shape):
        return 1.0 + 0.02 * jax.random.normal(next(ks), shape, f32)

    def bias(shape, scale=0.02):
        return scale * jax.random.normal(next(ks), shape, f32)

    x = jax.random.normal(next(ks), (BATCH, SEQ, D_MODEL), f32)
    mem = jax.random.normal(next(ks), (BATCH, N_MEM, D_MODEL), f32)
    positions = (jax.random.randint(next(ks), (BATCH, 1), 0, MAX_POS_OFFSET, dtype=jnp.int32)
                 + jnp.arange(SEQ, dtype=jnp.int32)[None, :])
    norm_mix = gain((Lr, D_MODEL))
    w_in = w((Lr, D_MODEL, IN_COLS), D_MODEL)
    b_gate = bias((Lr, N_BRANCHES * D_MODEL))
    conv_dw = w((Lr, CONV_WIDTH, C_CONV), CONV_WIDTH)
    conv_dw_b = bias((Lr, C_CONV))
    conv_ln_g = gain((Lr, C_CONV))
    conv_ln_b = bias((Lr, C_CONV))
    w_conv_out = w((Lr, C_CONV, D_MODEL), C_CONV)
    mla_q_norm = gain((Lr, Q_LORA))
    mla_kv_norm = gain((Lr, KV_LORA))
    w_q_up = w((Lr, Q_LORA, MLA_HEADS * (NOPE_DIM + ROPE_DIM)), Q_LORA)
    w_kv_up = w((Lr, KV_LORA, MLA_HEADS * (NOPE_DIM + V_DIM)), KV_LORA)
    mla_g_q = gain((Lr, NOPE_DIM + ROPE_DIM))
    mla_g_k = gain((Lr, NOPE_DIM + ROPE_DIM))
    w_mla_out = w((Lr, MLA_HEADS * V_DIM, D_MODEL), MLA_HEADS * V_DIM)
    mlstm_conv_w = w((Lr, MLSTM_CONV_WIDTH, D_MLSTM), MLSTM_CONV_WIDTH)
    mlstm_conv_b = bias((Lr, D_MLSTM))
    w_mq = w((Lr, MLSTM_HEADS, MLSTM_HEAD_DIM, MLSTM_HEAD_DIM), MLSTM_HEAD_DIM)
    w_mk = w((Lr, MLSTM_HEADS, MLSTM_HEAD_DIM, MLSTM_HEAD_DIM), MLSTM_HEAD_DIM)
    w_mv = w((Lr, MLSTM_HEADS, MLSTM_HEAD_DIM, MLSTM_HEAD_DIM), MLSTM_HEAD_DIM)
    w_if = w((Lr, 3 * D_MLSTM, 2 * MLSTM_HEADS), 3 * D_MLSTM)
    b_i = bias((Lr, MLSTM_HEADS), 0.1)
    b_f = jnp.linspace(3.0, 6.0, MLSTM_HEADS, dtype=f32)[None, :] + bias((Lr, MLSTM_HEADS), 0.1)
    b_if = jnp.concatenate([b_i, b_f], axis=-1)
    mlstm_gn_g = gain((Lr, D_MLSTM))
    mlstm_skip = gain((Lr, D_MLSTM))
    w_mlstm_out = w((Lr, D_MLSTM, D_MODEL), D_MLSTM)
    w_mix_out = w((Lr, D_MODEL, D_MODEL), D_MODEL)
    norm_x = gain((Lr, D_MODEL))
    norm_mem = gain((Lr, D_MODEL))
    w_xq = w((Lr, D_MODEL, XATTN_HEADS * XATTN_HEAD_DIM), D_MODEL)
    w_xkv = w((Lr, D_MODEL, 2 * XATTN_HEADS * XATTN_HEAD_DIM), D_MODEL)
    xattn_g_q = gain((Lr, XATTN_HEAD_DIM))
    xattn_g_k = gain((Lr, XATTN_HEAD_DIM))
    w_xo = w((Lr, XATTN_HEADS * XATTN_HEAD_DIM, D_MODEL), XATTN_HEADS * XATTN_HEAD_DIM)
    norm_ffn = gain((Lr, D_MODEL))
    w_ffn_in = w((Lr, D_MODEL, 2 * FFN_HIDDEN), D_MODEL)
    w_ffn_out = w((Lr, FFN_HIDDEN, D_MODEL), FFN_HIDDEN)
    return {'x': x, 'mem': mem, 'positions': positions, 'norm_mix': norm_mix, 'w_in': w_in,
            'b_gate': b_gate, 'conv_dw': conv_dw, 'conv_dw_b': conv_dw_b, 'conv_ln_g': conv_ln_g,
            'conv_ln_b': conv_ln_b, 'w_conv_out': w_conv_out, 'mla_q_norm': mla_q_norm,
            'mla_kv_norm': mla_kv_norm, 'w_q_up': w_q_up, 'w_kv_up': w_kv_up, 'mla_g_q': mla_g_q,
            'mla_g_k': mla_g_k, 'w_mla_out': w_mla_out, 'mlstm_conv_w': mlstm_conv_w,
            'mlstm_conv_b': mlstm_conv_b, 'w_mq': w_mq, 'w_mk': w_mk, 'w_mv': w_mv, 'w_if': w_if,
            'b_if': b_if, 'mlstm_gn_g': mlstm_gn_g, 'mlstm_skip': mlstm_skip,
            'w_mlstm_out': w_mlstm_out, 'w_mix_out': w_mix_out, 'norm_x': norm_x,
            'norm_mem': norm_mem, 'w_xq': w_xq, 'w_xkv': w_xkv, 'xattn_g_q': xattn_g_q,
            'xattn_g_k': xattn_g_k, 'w_xo': w_xo, 'norm_ffn': norm_ffn, 'w_ffn_in': w_ffn_in,
            'w_ffn_out': w_ffn_out}


def reference(x, mem, positions, norm_mix, w_in, b_gate, conv_dw, conv_dw_b, conv_ln_g, conv_ln_b,
              w_conv_out, mla_q_norm, mla_kv_norm, w_q_up, w_kv_up, mla_g_q, mla_g_k, w_mla_out,
              mlstm_conv_w, mlstm_conv_b, w_mq, w_mk, w_mv, w_if, b_if, mlstm_gn_g, mlstm_skip,
              w_mlstm_out, w_mix_out, norm_x, norm_mem, w_xq, w_xkv, xattn_g_q, xattn_g_k, w_xo,
              norm_ffn, w_ffn_in, w_ffn_out):
    for l in range(DEPTH):
        h = rms_norm(x, norm_mix[l])
        x = x + hybrid_mixer(h, positions, w_in[l], b_gate[l], conv_dw[l], conv_dw_b[l], conv_ln_g[l],
                             conv_ln_b[l], w_conv_out[l], mla_q_norm[l], mla_kv_norm[l], w_q_up[l],
                             w_kv_up[l], mla_g_q[l], mla_g_k[l], w_mla_out[l], mlstm_conv_w[l],
                             mlstm_conv_b[l], w_mq[l], w_mk[l], w_mv[l], w_if[l], b_if[l],
                             mlstm_gn_g[l], mlstm_skip[l], w_mlstm_out[l], w_mix_out[l])
        h = rms_norm(x, norm_x[l])
        mem_n = rms_norm(mem, norm_mem[l])
        x = x + memory_cross_attention(h, mem_n, w_xq[l], w_xkv[l], xattn_g_q[l], xattn_g_k[l], w_xo[l])
        h = rms_norm(x, norm_ffn[l])
        x = x + swiglu_ffn(h, w_ffn_in[l], w_ffn_out[l])
    return x
```

```python
import math
from contextlib import ExitStack

import numpy as np
import concourse.bass as bass
import concourse.mybir as mybir
from concourse.bass_utils import run_bass_kernel_spmd

F32 = mybir.dt.float32
BF16 = mybir.dt.bfloat16
I32 = mybir.dt.int32
AF = mybir.ActivationFunctionType
ALU = mybir.AluOpType

ENGS = ['pe', 'act', 'dve', 'pool', 'sp']

DM = 2048
S_OWN = 2048
S_ALL = 4096
G = 1024
NT = G // 128
NSB = G // 512
EPS = 1e-6
FFN_H = 5632
IN_COLS = 11072
DEPTH = 2


class Op:
    __slots__ = ('eng', 'fn', 'dma', 'deps_c', 'deps_d', 'idx', 'signal', 'count', 'sem', 'val', 'cc')


class Prog:
    def __init__(self):
        self.ops = {e: [] for e in ENGS}
        self.last_w = {}
        self.rd_c = {}
        self.rd_d = {}

    def add(self, eng, fn, reads=(), writes=(), dma=False, cc=False):
        o = Op()
        o.eng = eng; o.fn = fn; o.dma = dma; o.cc = cc
        o.idx = len(self.ops[eng]); o.signal = False
        o.count = 0; o.sem = None; o.val = 0
        dc = {}
        dd = set()

        def need(p):
            if p.dma:
                dd.add(p)
            else:
                if p.eng == 'pe' and eng == 'pe' and not dma:
                    return
                if dc.get(p.eng, -1) < p.idx:
                    dc[p.eng] = p.idx
        for k in reads:
            p = self.last_w.get(k)
            if p is not None:
                need(p)
        for k in writes:
            p = self.last_w.get(k)
            if p is not None:
                need(p)
            for (pe, pidx) in self.rd_c.get(k, {}).items():
                need(self.ops[pe][pidx])
            for r in self.rd_d.get(k, ()):
                need(r)
        o.deps_c = dc
        o.deps_d = dd
        self.ops[eng].append(o)
        for k in reads:
            if dma:
                self.rd_d.setdefault(k, set()).add(o)
            else:
                self.rd_c.setdefault(k, {})[eng] = o.idx
        for k in writes:
            self.last_w[k] = o
            self.rd_c[k] = {}
            self.rd_d[k] = set()
        return o

    def barrier(self):
        o = Op()
        o.eng = 'sp'; o.fn = (lambda e: e.nop()); o.dma = False; o.cc = False
        o.idx = len(self.ops['sp']); o.signal = False
        o.count = 0; o.sem = None; o.val = 0
        o.deps_c = {e: len(self.ops[e]) - 1 for e in ENGS if len(self.ops[e]) > 0}
        start = getattr(self, '_bar_idx', {e: 0 for e in ENGS})
        dd = set()
        for e in ENGS:
            for p in self.ops[e][start[e]:]:
                if p.dma:
                    dd.add(p)
        o.deps_d = dd
        self.ops['sp'].append(o)
        for e in ENGS:
            if e == 'sp':
                continue
            q = Op()
            q.eng = e; q.fn = (lambda en: en.nop()); q.dma = False; q.cc = False
            q.idx = len(self.ops[e]); q.signal = False
            q.count = 0; q.sem = None; q.val = 0
            q.deps_c = {'sp': o.idx}
            q.deps_d = set()
            self.ops[e].append(q)
        self._bar_idx = {e: len(self.ops[e]) for e in ENGS}
        self.last_w = {}
        self.rd_c = {}
        self.rd_d = {}

    def emit(self, nc, es, npool=20):
        ops = self.ops
        for e in ENGS:
            for o in ops[e]:
                for (pe, pidx) in o.deps_c.items():
                    ops[pe][pidx].signal = True
        sems = {e: es.enter_context(nc.semaphore("s_" + e)) for e in ENGS}
        for e in ENGS:
            c = 0
            for o in ops[e]:
                if (not o.dma) and o.signal:
                    c += 1
                    o.count = c
        for e in ENGS:
            nd = sum(1 for o in ops[e] if o.dma)
            if nd == 0:
                continue
            n = min(npool, nd)
            pool = [es.enter_context(nc.semaphore("d_%s_%d" % (e, i))) for i in range(n)]
            i = 0
            ccsem = None
            ncc = 0
            for o in ops[e]:
                if o.dma and o.cc:
                    if ccsem is None:
                        ccsem = es.enter_context(nc.semaphore("cc_%s" % e))
                    ncc += 1
                    o.sem = ccsem
                    o.val = ncc
                elif o.dma:
                    o.sem = pool[i % n]
                    o.val = 16 * (i // n + 1)
                    i += 1
        block = es.enter_context(nc.Block())

        def run(engname, eng):
            waited = {}
            for o in ops[engname]:
                waits = []
                for (pe, pidx) in o.deps_c.items():
                    waits.append((sems[pe], ops[pe][pidx].count))
                for d in o.deps_d:
                    waits.append((d.sem, d.val))
                if o.dma and o.cc and o.val > 1:
                    waits.append((o.sem, o.val - 1))
                elif o.dma and (not o.cc) and o.val > 16:
                    waits.append((o.sem, o.val - 16))
                waits.sort(key=lambda sv: -sv[1])
                for (s, v) in waits:
                    key = id(s)
                    if waited.get(key, 0) >= v:
                        continue
                    eng.wait_ge(s, v)
                    waited[key] = v
                ins = o.fn(eng)
                if o.dma and o.cc:
                    ins.then_inc(o.sem)
                elif o.dma:
                    ins.then_inc(o.sem, 16)
                elif o.signal:
                    ins.then_inc(sems[engname], 1)

        @block.sync
        def _(e):
            run('sp', e)

        @block.tensor
        def _(e):
            run('pe', e)

        @block.scalar
        def _(e):
            run('act', e)

        @block.vector
        def _(e):
            run('dve', e)

        @block.gpsimd
        def _(e):
            run('pool', e)


class Builder:
    def __init__(self, nc, layers, debug=()):
        self.nc = nc
        self.P = Prog()
        self.layers = layers
        self.debug = set(debug)
        self.uid = 0
        self.gs = ExitStack()

    def dma(self, eng, out, in_, r, w, **kw):
        self.P.add(eng, lambda e: e.dma_start(out=out, in_=in_, **kw), r, w, dma=True)

    def act(self, out, in_, func, r, w, **kw):
        self.P.add('act', lambda e: e.activation(out=out, in_=in_, func=func, **kw), r, w)

    def mm(self, out, lhsT, rhs, start, stop, r, w):
        self.P.add('pe', lambda e: e.matmul(out, lhsT=lhsT, rhs=rhs, start=start, stop=stop), r, w)

    def tr(self, out, in_, ident, r, w):
        self.P.add('pe', lambda e: e.transpose(out=out, in_=in_, identity=ident), r, w)

    def tt(self, eng, out, in0, in1, op, r, w):
        self.P.add(eng, lambda e: e.tensor_tensor(out=out, in0=in0, in1=in1, op=op), r, w)

    def ts(self, eng, out, in0, s1, s2, op0, op1, r, w):
        if op1 is None:
            self.P.add(eng, lambda e: e.tensor_scalar(out=out, in0=in0, scalar1=s1, scalar2=None, op0=op0), r, w)
        else:
            self.P.add(eng, lambda e: e.tensor_scalar(out=out, in0=in0, scalar1=s1, scalar2=s2, op0=op0, op1=op1), r, w)

    def stt(self, eng, out, in0, scalar, in1, op0, op1, r, w):
        self.P.add(eng, lambda e: e.scalar_tensor_tensor(out=out, in0=in0, scalar=scalar, in1=in1, op0=op0, op1=op1), r, w)

    def cp(self, eng, out, in_, r, w):
        self.P.add(eng, lambda e: e.tensor_copy(out=out, in_=in_), r, w)

    def memset(self, eng, ap, val, w):
        self.P.add(eng, lambda e: e.memset(ap, val), (), w)

    def recip(self, out, in_, r, w):
        self.P.add('dve', lambda e: e.reciprocal(out=out, in_=in_), r, w)

    def sb(self, es, name, shape, dt):
        self.uid += 1
        return es.enter_context(self.nc.sbuf_tensor("%s_%d" % (name, self.uid), shape, dt))

    def dram(self, name, shape, dt):
        kind = "ExternalOutput" if name in self.debug else "Internal"
        if kind == "Internal":
            return self.nc.dram_tensor(name, shape, dt).ap()
        return self.nc.dram_tensor(name, shape, dt, kind=kind).ap()

    def rstd(self, out, in_, n, r, w):
        self.ts('dve', out, in_, 1.0 / n, EPS, ALU.mult, ALU.add, r, w)
        self.act(out, out, AF.Sqrt, w, w)
        self.recip(out, out, w, w)

    def setup(self):
        nc = self.nc
        g = self.gs
        self.cst_d = nc.dram_tensor("cst", [128, 512], F32, kind="ExternalInput").ap()
        self.pos_d = nc.dram_tensor("pos", [1, S_ALL], I32, kind="ExternalInput").ap()
        self.mem_d = nc.dram_tensor("mem", [256, DM], F32, kind="ExternalInput").ap()
        self.cst = self.sb(g, "cst", [128, 512], F32)
        self.idb = self.sb(g, "idb", [128, 128], BF16)
        self.trib = self.sb(g, "trib", [128, 128], BF16)
        self.onesf = self.sb(g, "onesf", [128, 128], F32)
        self.onesb = self.sb(g, "onesb", [128, 128], BF16)
        self.psum = g.enter_context(nc.psum_tensor("ps", [128, 4096], F32))
        self.dma('sp', self.cst[:], self.cst_d, (), ['cst'])
        self.cp('dve', self.idb[:], self.cst[:, 0:128], ['cst'], ['idb'])
        self.cp('dve', self.trib[:], self.cst[:, 128:256], ['cst'], ['trib'])
        self.memset('dve', self.onesf[:], 1.0, ['onesf'])
        self.memset('dve', self.onesb[:], 1.0, ['onesb'])
        self.idf = self.cst[:, 0:128]
        self.trif = self.cst[:, 128:256]
        self.c_if2pi = self.cst[0:64, 256:257]
        self.c_sign = self.cst[0:64, 257:258]
        self.c_flag = self.cst[:, 258:259]
        self.c_cbias = self.cst[:, 259:260]
        self.c_zero = self.cst[:, 260:261]
        self.Cst = self.sb(g, "Cst", [128, 8, 257], F32)
        self.Cbf = self.sb(g, "Cbf", [128, 8, 257], BF16)
        self.u_s = self.dram("u_s", [1024, S_ALL], F32)
        self.cq_s = self.dram("cq_s", [512, S_OWN], F32)
        self.ckv_s = self.dram("ckv_s", [256, S_ALL], F32)
        self.kr_s = self.dram("kr_s", [2, 64, S_ALL], F32)
        self.xm_s = self.dram("xm_s", [1024, 3 + S_ALL], F32)
        self.z_s = self.dram("z_s", [1024, S_OWN], F32)
        self.xc_s = self.dram("xc_s", [1024, S_OWN], F32)
        self.cb_s = self.dram("cb_s", [1024, S_OWN], BF16)
        self.o_s = self.dram("o_s", [1024, S_OWN], BF16)
        self.hn_s = self.dram("hn_s", [1024, S_OWN], BF16)
        self.kn_s = self.dram("kn_s", [8, 128, S_ALL], BF16)
        self.v_s = self.dram("v_s", [S_ALL, 1024], BF16)
        self.krot_s = self.dram("krot_s", [64, S_ALL], BF16)
        self.x1_s = self.dram("x1_s", [S_OWN, DM], F32)
        self.x2_s = self.dram("x2_s", [S_OWN, DM], F32)
        self.P.barrier()

    def ps(self, bank, n=512, off=0):
        return self.psum[:, bank * 512 + off: bank * 512 + off + n]

    def norm_transpose(self, es, src, row0, ntiles, gain_d, hT, hkey):
        gn = self.sb(es, "gain", [128, DM], F32)
        self.dma('sp', gn[:], gain_d.partition_broadcast(128), (), ['gain'])
        xts = [self.sb(es, "xt%d" % i, [128, DM], F32) for i in range(2)]
        hbs = [self.sb(es, "hb%d" % i, [128, DM], BF16) for i in range(2)]
        junk = self.sb(es, "junk", [128, DM], F32)
        st = self.sb(es, "nst", [128, 2 * ntiles], F32)
        self.memset('dve', st[:], 0.0, ['nst'])
        for t in range(ntiles):
            xt = xts[t % 2]; hb = hbs[t % 2]
            kx = ('xt', t % 2); kh = ('hb', t % 2)
            self.dma('sp', xt[:], src[row0 + t * 128: row0 + (t + 1) * 128, :], (), [kx])
            self.act(junk[:], xt[:], AF.Square, [kx, 'nst'], ['junk', 'nst'], accum_out=st[:, 2 * t:2 * t + 1])
            self.rstd(st[:, 2 * t + 1:2 * t + 2], st[:, 2 * t:2 * t + 1], DM, ['nst'], ['nst'])
            self.stt('dve', hb[:], xt[:], st[:, 2 * t + 1:2 * t + 2], gn[:], ALU.mult, ALU.mult, [kx, 'nst', 'gain'], [kh])
            pb = (t % 2) * 2
            pv = self.psum[:, pb * 512:(pb + 2) * 512].bitcast(BF16)
            for c in range(16):
                self.tr(pv[:, c * 128:(c + 1) * 128], hb[:, c * 128:(c + 1) * 128], self.idb[:], [kh, 'idb'], [('ps', pb), ('ps', pb + 1)])
            eng = 'act' if t % 2 == 0 else 'dve'
            outv = hT[:, :, t * 128:(t + 1) * 128]
            inv = pv.rearrange("p (c t) -> p c t", c=16)
            if eng == 'act':
                self.act(outv, inv, AF.Copy, [('ps', pb), ('ps', pb + 1)], [(hkey, t // 4)])
            else:
                self.cp('dve', outv, inv, [('ps', pb), ('ps', pb + 1)], [(hkey, t // 4)])

    def wload(self, dst, src, key):
        self.dma('pool', dst, src, (), [key])

    def proj_fm(self, wt, wkey, fsl, M, nk, rhs_fn, rkeys, bank0, nsb):
        for s in range(nsb):
            for k in range(nk):
                self.mm(self.ps(bank0 + s)[0:M, :], wt[:, k, fsl:fsl + M], rhs_fn(k, s), k == 0, k == nk - 1,
                        [wkey] + rkeys(k, s), [('ps', bank0 + s)])

    def layer(self, l, W, x_own, x_ctx, x_out):
        P = self.P
        nc = self.nc
        with ExitStack() as ls:
            def col(name, src, nch, rows=128):
                t = self.sb(ls, name, [rows, nch], F32)
                self.dma('sp', t[:], src.rearrange("(c p) -> p c", p=rows), (), [name], allow_slow_non_contiguous=True)
                return t
            pv = {}
            pv['bgate'] = col('bgate', W['b_gate'][l], 48)
            pv['cdb'] = col('cdb', W['conv_dw_b'][l], 8)
            pv['clg'] = col('clg', W['conv_ln_g'][l], 8)
            pv['clb'] = col('clb', W['conv_ln_b'][l], 8)
            pv['qng'] = col('qng', W['mla_q_norm'][l], 4)
            pv['kvg'] = col('kvg', W['mla_kv_norm'][l], 2)
            pv['mcb'] = col('mcb', W['mlstm_conv_b'][l], 8)
            pv['gng'] = col('gng', W['mlstm_gn_g'][l], 8)
            pv['skp'] = col('skp', W['mlstm_skip'][l], 8)
            pv['xgq'] = col('xgq', W['xattn_g_q'][l], 1)
            pv['xgk'] = col('xgk', W['xattn_g_k'][l], 1)
            cdw = self.sb(ls, 'cdw', [128, 8, 31], F32)
            for c in range(8):
                self.dma('sp', cdw[:, c, :], W['conv_dw'][l][:, c * 128:(c + 1) * 128].rearrange("j p -> p j"), (), ['cdw'], allow_slow_non_contiguous=True)
            mcw = self.sb(ls, 'mcw', [128, 8, 4], F32)
            for c in range(8):
                self.dma('sp', mcw[:, c, :], W['mlstm_conv_w'][l][:, c * 128:(c + 1) * 128].rearrange("j p -> p j"), (), ['mcw'], allow_slow_non_contiguous=True)
            gq = self.sb(ls, 'gq', [128, 4], F32)
            gk = self.sb(ls, 'gk', [128, 4], F32)
            for (t, nm, key) in ((gq, 'mla_g_q', 'gq'), (gk, 'mla_g_k', 'gk')):
                src = W[nm][l]
                self.dma('sp', t[:, 0:1], src[0:128].rearrange("(p o) -> p o", o=1), (), [key], allow_slow_non_contiguous=True)
                self.dma('sp', t[0:64, 1:2], src[128:192].rearrange("(p o) -> p o", o=1), (), [key], allow_slow_non_contiguous=True)
                self.dma('sp', t[0:32, 2:3], src[160:192].rearrange("(p o) -> p o", o=1), (), [key], allow_slow_non_contiguous=True)
                self.dma('sp', t[32:64, 2:3], src[128:160].rearrange("(p o) -> p o", o=1), (), [key], allow_slow_non_contiguous=True)
            bif = self.sb(ls, 'bif', [128, 8], F32)
            self.dma('sp', bif[:], W['b_if'][l].partition_broadcast(128), (), ['bif'])
            wif = self.sb(ls, 'wif', [128, 24, 8], BF16)
            self.wload(wif[:], W['w_if'][l].rearrange("(c p) e -> p c e", p=128), 'wif')
            kx = self.sb(ls, 'kx', [128, 4, 256], BF16)
            vx = self.sb(ls, 'vx', [128, 2, 512], BF16)
            self.stage_mem(l, W, pv, kx, vx)
            self.memset('dve', self.Cst[:], 0.0, ['Cst'])
            self.memset('dve', self.Cbf[:], 0.0, ['Cbf'])
            with ExitStack() as es:
                z = self.sb(es, "zpad", [128, 8, 3], F32)
                self.memset('dve', z[:], 0.0, ['zpad'])
                self.dma('sp', self.xm_s[:, 0:3].rearrange("(c p) j -> p c j", p=128), z[:], ['zpad'], ['xm_s'])
                P.barrier()

            for gi in range(4):
                own = gi >= 2
                p0 = gi * G
                o0 = (gi - 2) * G
                src = x_own if own else x_ctx
                r0 = o0 if own else p0
                with ExitStack() as gsx:
                    hT = self.sb(gsx, "hT", [128, 16, G], BF16)
                    with ExitStack() as es:
                        self.norm_transpose(es, src, r0, NT, W['norm_mix'][l], hT, 'hT')
                        P.barrier()
                    self.stage_proj(l, W, gi, hT)
                    if own:
                        self.stage_conv(l, W, pv, cdw, gi)
                    self.stage_kv(l, W, pv, gk, gi)
                    if own:
                        self.stage_attn(l, W, pv, gq, gi)
                    self.stage_mlstm(l, W, pv, mcw, bif, wif, gi)
                    if own:
                        self.stage_merge(l, W, pv, gi, hT, src)
                if own:
                    self.stage_xattn(l, W, pv, kx, vx, gi)
                    self.stage_ffn(l, W, gi, x_out)
            P.barrier()

    def stage_mem(self, l, W, pv, kx, vx):
        P = self.P
        with ExitStack() as es:
            mT = self.sb(es, "mT", [128, 16, 256], BF16)
            with ExitStack() as es2:
                self.norm_transpose(es2, self.mem_d, 0, 2, W['norm_mem'][l], mT, 'mT')
                P.barrier()
            wv = W['w_xkv'][l].rearrange("(c p) f -> p c f", p=128)
            wk = self.sb(es, "wxkv", [128, 16, 1024], BF16)
            self.wload(wk[:, :, 0:512], wv[:, :, 0:512], 'wxkv0')
            self.wload(wk[:, :, 512:1024], wv[:, :, 512:1024], 'wxkv1')
            kf = self.sb(es, "kf", [128, 256], F32)
            sq = self.sb(es, "sq", [128, 256], F32)
            rs = self.sb(es, "rs", [128, 256], F32)
            for h in range(4):
                b = h % 2
                for k in range(16):
                    self.mm(self.ps(b, 256), wk[:, k, h * 128:(h + 1) * 128], mT[:, k, :], k == 0, k == 15,
                            ['wxkv0', ('mT', 0)], [('ps', b)])
                self.act(kf[:], self.ps(b, 256), AF.Copy, [('ps', b)], ['kf'])
                self.act(sq[:], self.ps(b, 256), AF.Square, [('ps', b)], ['sq'])
                self.mm(self.ps(2 + b, 256), self.onesf[:], sq[:], True, True, ['onesf', 'sq'], [('ps', 2 + b)])
                self.rstd(rs[:], self.ps(2 + b, 256), 128, [('ps', 2 + b)], ['rs'])
                self.stt('dve', kx[:, h, :], kf[:], pv['xgk'][:, 0:1], rs[:], ALU.mult, ALU.mult, ['kf', 'rs', 'xgk'], ['kx'])
            for mc in range(2):
                b = 4 + mc
                for k in range(16):
                    self.mm(self.ps(b), mT[:, k, mc * 128:(mc + 1) * 128], wk[:, k, 512:1024], k == 0, k == 15,
                            ['wxkv1', ('mT', 0)], [('ps', b)])
                self.act(vx[:, mc, :], self.ps(b), AF.Copy, [('ps', b)], ['vx'])
            P.barrier()

    def stage_proj(self, l, W, gi, hT):
        P = self.P
        own = gi >= 2
        p0 = gi * G
        o0 = (gi - 2) * G
        wv = W['w_in'][l].rearrange("(c p) f -> p c f", p=128)
        with ExitStack() as es:
            wb = [self.sb(es, "wb%d" % i, [128, 16, 512], BF16) for i in range(3)]
            stg = [self.sb(es, "stg%d" % i, [128, G], F32) for i in range(4)]
            sig = [self.sb(es, "sig%d" % i, [128, G], F32) for i in range(2)]
            cnt = {'w': 0, 's': 0, 'b': 0, 'g': 0}

            def nextw():
                i = cnt['w'] % 3; cnt['w'] += 1
                return wb[i], ('wb', i)

            def nexts():
                i = cnt['s'] % 4; cnt['s'] += 1
                return stg[i], ('stg', i)

            def nextb():
                b = (cnt['b'] % 4) * 2; cnt['b'] += 1
                return b

            def rhs_fn(k, s):
                return hT[:, k, s * 512:(s + 1) * 512]

            def rkeys(k, s):
                return [('hT', s)]

            def simple_seg(c0, width, M, dst_fn, func):
                for f0 in range(0, width, 512):
                    fw = min(512, width - f0)
                    wt, wkey = nextw()
                    self.wload(wt[:, :, 0:fw], wv[:, :, c0 + f0:c0 + f0 + fw], wkey)
                    for j in range(0, fw, M):
                        b = nextb()
                        self.proj_fm(wt, wkey, j, M, 16, rhs_fn, rkeys, b, NSB)
                        st_, skey = nexts()
                        self.act(st_[0:M, :], self.psum[0:M, b * 512:b * 512 + G], func, [('ps', b), ('ps', b + 1)], [skey])
                        dst_fn(f0 + j, st_, skey)

            if own:
                for f0 in range(0, 1024, 512):
                    wa, ka = nextw()
                    self.wload(wa[:], wv[:, :, f0:f0 + 512], ka)
                    wg, kg = nextw()
                    self.wload(wg[:], wv[:, :, 1024 + f0:1024 + f0 + 512], kg)
                    for j in range(0, 512, 128):
                        ba = nextb()
                        self.proj_fm(wa, ka, j, 128, 16, rhs_fn, rkeys, ba, NSB)
                        bg = nextb()
                        self.proj_fm(wg, kg, j, 128, 16, rhs_fn, rkeys, bg, NSB)
                        i = cnt['g'] % 2; cnt['g'] += 1
                        self.act(sig[i][:], self.psum[:, bg * 512:bg * 512 + G], AF.Sigmoid, [('ps', bg), ('ps', bg + 1)], [('sig', i)])
                        st_, skey = nexts()
                        self.tt('dve', st_[:], self.psum[:, ba * 512:ba * 512 + G], sig[i][:], ALU.mult,
                                [('ps', ba), ('ps', ba + 1), ('sig', i)], [skey])
                        ch = (f0 + j) // 128
                        self.dma('sp', self.u_s[ch * 128:(ch + 1) * 128, p0:p0 + G], st_[:], [skey], ['u_s'])
                simple_seg(2048, 512, 128,
                           lambda f, st_, sk: self.dma('sp', self.cq_s[f:f + 128, o0:o0 + G], st_[:], [sk], ['cq_s']), AF.Copy)
                simple_seg(3904, 1024, 128,
                           lambda f, st_, sk: self.dma('sp', self.z_s[f:f + 128, o0:o0 + G], st_[:], [sk], ['z_s']), AF.Silu)
            elif gi == 1:
                for f0 in range(0, 1024, 512):
                    wa, ka = nextw()
                    self.wload(wa[:], wv[:, :, f0:f0 + 512], ka)
                    wg, kg = nextw()
                    self.wload(wg[:], wv[:, :, 1024 + f0:1024 + f0 + 512], kg)
                    for j in range(0, 512, 128):
                        ba = nextb()
                        bg = nextb()
                        for k in range(16):
                            self.mm(self.ps(ba, 128), wa[:, k, j:j + 128], hT[:, k, G - 128:G], k == 0, k == 15, [ka, ('hT', 1)], [('ps', ba)])
                        for k in range(16):
                            self.mm(self.ps(bg, 128), wg[:, k, j:j + 128], hT[:, k, G - 128:G], k == 0, k == 15, [kg, ('hT', 1)], [('ps', bg)])
                        i = cnt['g'] % 2; cnt['g'] += 1
                        self.act(sig[i][:, 0:128], self.ps(bg, 128), AF.Sigmoid, [('ps', bg)], [('sig', i)], scale=1.0)
                        st_, skey = nexts()
                        self.stt('dve', st_[:, 0:128], self.ps(ba, 128), self.c_flag, sig[i][:, 0:128], ALU.mult, ALU.mult,
                                 [('ps', ba), ('sig', i), 'cst'], [skey])
                        ch = (f0 + j) // 128
                        self.dma('sp', self.u_s[ch * 128:(ch + 1) * 128, p0 + G - 128:p0 + G], st_[:, 0:128], [skey], ['u_s'])
            simple_seg(2560, 256, 128,
                       lambda f, st_, sk: self.dma('sp', self.ckv_s[f:f + 128, p0:p0 + G], st_[:], [sk], ['ckv_s']), AF.Copy)
            wt, wkey = nextw()
            self.wload(wt[:, :, 0:64], wv[:, :, 2816:2880], wkey)
            self.wload(wt[:, :, 64:96], wv[:, :, 2848:2880], wkey)
            self.wload(wt[:, :, 96:128], wv[:, :, 2816:2848], wkey)
            for v_ in range(2):
                b = nextb()
                self.proj_fm(wt, wkey, v_ * 64, 64, 16, rhs_fn, rkeys, b, NSB)
                st_, skey = nexts()
                self.act(st_[0:64, :], self.psum[0:64, b * 512:b * 512 + G], AF.Copy, [('ps', b), ('ps', b + 1)], [skey])
                self.dma('sp', self.kr_s[v_, :, p0:p0 + G], st_[0:64, :], [skey], ['kr_s'])
            simple_seg(2880, 1024, 128,
                       lambda f, st_, sk: self.dma('sp', self.xm_s[f:f + 128, 3 + p0:3 + p0 + G], st_[:], [sk], ['xm_s']), AF.Copy)
            P.barrier()

    def stage_conv(self, l, W, pv, cdw, gi):
        P = self.P
        p0 = gi * G
        o0 = (gi - 2) * G
        with ExitStack() as es:
            ub = [self.sb(es, "ub%d" % i, [128, 30 + G], F32) for i in range(2)]
            cv = self.sb(es, "cv", [128, 8, G], F32)
            sq = [self.sb(es, "csq%d" % i, [128, G], F32) for i in range(2)]
            mean = self.sb(es, "mean", [128, G], F32)
            rs = self.sb(es, "crs", [128, G], F32)
            tmp = [self.sb(es, "ctmp%d" % i, [128, G], F32) for i in range(2)]
            cbo = [self.sb(es, "cbo%d" % i, [128, G], BF16) for i in range(2)]
            for c in range(8):
                u = ub[c % 2]; uk = ('ub', c % 2)
                self.dma('sp', u[:], self.u_s[c * 128:(c + 1) * 128, p0 - 30:p0 + G], ['u_s'], [uk])
                eng = 'dve' if c % 2 == 0 else 'pool'
                eng = 'dve'
                self.ts(eng, cv[:, c, :], u[:, 0:G], cdw[:, c, 0:1], pv['cdb'][:, c:c + 1], ALU.mult, ALU.add,
                        [uk, 'cdw', 'cdb'], [('cv', c)])
                for j in range(1, 31):
                    self.stt(eng, cv[:, c, :], u[:, j:j + G], cdw[:, c, j:j + 1], cv[:, c, :], ALU.mult, ALU.add,
                             [uk, 'cdw', ('cv', c)], [('cv', c)])
                s = sq[c % 2]; sk = ('csq', c % 2)
                self.act(s[:], cv[:, c, :], AF.Square, [('cv', c)], [sk])
                for sbk in range(NSB):
                    self.mm(self.ps(sbk), self.onesf[:], cv[:, c, sbk * 512:(sbk + 1) * 512], c == 0, c == 7, ['onesf', ('cv', c)], [('ps', sbk)])
                    self.mm(self.ps(2 + sbk), self.onesf[:], s[:, sbk * 512:(sbk + 1) * 512], c == 0, c == 7, ['onesf', sk], [('ps', 2 + sbk)])
            r_s1 = [('ps', 0), ('ps', 1)]
            r_s2 = [('ps', 2), ('ps', 3)]
            self.act(mean[:], self.psum[:, 0:G], AF.Copy, r_s1, ['mean'], scale=1.0 / 1024)
            t0 = tmp[0]
            self.tt('dve', t0[:], mean[:], mean[:], ALU.mult, ['mean'], [('ctmp', 0)])
            self.stt('dve', rs[:], self.psum[:, 1024:1024 + G], 1.0 / 1024, t0[:], ALU.mult, ALU.subtract, r_s2 + [('ctmp', 0)], ['crs'])
            self.ts('dve', rs[:], rs[:], EPS, None, ALU.add, None, ['crs'], ['crs'])
            self.act(rs[:], rs[:], AF.Sqrt, ['crs'], ['crs'])
            self.recip(rs[:], rs[:], ['crs'], ['crs'])
            for c in range(8):
                t = tmp[c % 2]; tk = ('ctmp', c % 2)
                self.tt('dve', t[:], cv[:, c, :], mean[:], ALU.subtract, [('cv', c), 'mean'], [tk])
                self.tt('dve', t[:], t[:], rs[:], ALU.mult, [tk, 'crs'], [tk])
                o = cbo[c % 2]; ok = ('cbo', c % 2)
                self.act(o[:], t[:], AF.Silu, [tk, 'clg', 'clb'], [ok], scale=pv['clg'][:, c:c + 1], bias=pv['clb'][:, c:c + 1])
                self.dma('sp', self.cb_s[c * 128:(c + 1) * 128, o0:o0 + G], o[:], [ok], ['cb_s'])
            P.barrier()

    def rope_tables(self, es, p0):
        pi_ = self.sb(es, "rp_i", [64, G], I32)
        t = self.sb(es, "rp_t", [64, G], F32)
        kf = self.sb(es, "rp_k", [64, G], F32)
        cosT = self.sb(es, "rp_cos", [64, G], F32)
        sinT = self.sb(es, "rp_sin", [64, G], F32)
        self.dma('sp', pi_[:], self.pos_d[:, p0:p0 + G].partition_broadcast(64), (), ['rp_i'])
        self.cp('dve', t[:], pi_[:], ['rp_i'], ['rp_t'])
        self.ts('dve', t[:], t[:], self.c_if2pi, None, ALU.mult, None, ['rp_t', 'cst'], ['rp_t'])
        for which, dst, key in ((0, sinT, 'rp_sin'), (1, cosT, 'rp_cos')):
            if which == 1:
                self.ts('dve', t[:], t[:], 0.25, None, ALU.add, None, ['rp_t'], ['rp_t'])
            self.cp('dve', pi_[:], t[:], ['rp_t'], ['rp_i'])
            self.cp('dve', kf[:], pi_[:], ['rp_i'], ['rp_k'])
            self.tt('dve', kf[:], t[:], kf[:], ALU.subtract, ['rp_t', 'rp_k'], ['rp_k'])
            self.ts('dve', dst[:], kf[:], 0.5, None, ALU.is_gt, None, ['rp_k'], [key])
            self.tt('dve', kf[:], kf[:], dst[:], ALU.subtract, ['rp_k', key], ['rp_k'])
            self.act(dst[:], kf[:], AF.Sin, ['rp_k'], [key], scale=2 * math.pi)
        self.ts('dve', sinT[:], sinT[:], self.c_sign, None, ALU.mult, None, ['rp_sin', 'cst'], ['rp_sin'])
        t1 = self.sb(es, "rt1", [64, G], F32)
        t2 = self.sb(es, "rt2", [64, G], F32)
        return cosT, sinT, t1, t2

    def stage_kv(self, l, W, pv, gk, gi):
        P = self.P
        p0 = gi * G
        with ExitStack() as es:
            tabs = self.rope_tables(es, p0)
            ck = self.sb(es, "ck", [128, 2, G], F32)
            sq = self.sb(es, "ksq", [128, G], F32)
            rs = self.sb(es, "krs", [128, G], F32)
            ckn = self.sb(es, "ckn", [128, 2, G], BF16)
            wkv = self.sb(es, "wkv", [128, 2, 2048], BF16)
            wvv = self.sb(es, "wvv", [128, 2, 8, 128], BF16)
            self.wload(wkv[:], W['w_kv_up'][l].rearrange("(c p) f -> p c f", p=128), 'wkv')
            wv5 = W['w_kv_up'][l].rearrange("(c p) (h two d) -> p c h two d", p=128, two=2, d=128)
            for c in range(2):
                self.wload(wvv[:, c, :, :], wv5[:, c, :, 1, :], 'wvv')
            self.dma('sp', ck[:], self.ckv_s[:, p0:p0 + G].rearrange("(c p) t -> p c t", p=128), ['ckv_s'], ['ck'])
            for c in range(2):
                self.act(sq[:], ck[:, c, :], AF.Square, ['ck'], ['ksq'])
                for s in range(NSB):
                    self.mm(self.ps(s), self.onesf[:], sq[:, s * 512:(s + 1) * 512], c == 0, c == 1, ['onesf', 'ksq'], [('ps', s)])
            self.rstd(rs[:], self.psum[:, 0:G], 256, [('ps', 0), ('ps', 1)], ['krs'])
            for c in range(2):
                self.stt('dve', ckn[:, c, :], ck[:, c, :], pv['kvg'][:, c:c + 1], rs[:], ALU.mult, ALU.mult, ['ck', 'krs', 'kvg'], ['ckn'])
            kf = [self.sb(es, "kff%d" % i, [128, G], F32) for i in range(2)]
            sq2 = [self.sb(es, "ksq2%d" % i, [128, G], F32) for i in range(2)]
            rs2 = [self.sb(es, "krs2%d" % i, [128, G], F32) for i in range(2)]
            kno = [self.sb(es, "kno%d" % i, [128, G], BF16) for i in range(2)]
            for h in range(8):
                i = h % 2
                b = 2 + i * 2
                for s in range(NSB):
                    for c in range(2):
                        self.mm(self.ps(b + s), wkv[:, c, h * 256:h * 256 + 128], ckn[:, c, s * 512:(s + 1) * 512], c == 0, c == 1,
                                ['wkv', 'ckn'], [('ps', b + s)])
                rb = [('ps', b), ('ps', b + 1)]
                self.act(kf[i][:], self.psum[:, b * 512:b * 512 + G], AF.Copy, rb, [('kff', i)])
                self.act(sq2[i][:], self.psum[:, b * 512:b * 512 + G], AF.Square, rb, [('ksq2', i)])
                for s in range(NSB):
                    self.mm(self.ps(6 + s), self.onesf[:], sq2[i][:, s * 512:(s + 1) * 512], True, True, ['onesf', ('ksq2', i)], [('ps', 6 + s)])
                self.rstd(rs2[i][:], self.psum[:, 6 * 512:6 * 512 + G], 128, [('ps', 6), ('ps', 7)], [('krs2', i)])
                self.stt('dve', kno[i][:], kf[i][:], gk[:, 0:1], rs2[i][:], ALU.mult, ALU.mult, [('kff', i), ('krs2', i), 'gk'], [('kno', i)])
                self.dma('sp', self.kn_s[h, :, p0:p0 + G], kno[i][:], [('kno', i)], ['kn_s'])
            vo = [self.sb(es, "vo%d" % i, [128, 1024], BF16) for i in range(2)]
            for t in range(NT):
                i = t % 2
                b = 2 + i * 2
                for hb in range(2):
                    for c in range(2):
                        self.mm(self.ps(b + hb), ckn[:, c, t * 128:(t + 1) * 128], wvv[:, c, hb * 4:(hb + 1) * 4, :].rearrange("p h d -> p (h d)"),
                                c == 0, c == 1, ['wvv', 'ckn'], [('ps', b + hb)])
                self.act(vo[i][:], self.psum[:, b * 512:b * 512 + 1024], AF.Copy, [('ps', b), ('ps', b + 1)], [('vo', i)])
                self.dma('sp', self.v_s[p0 + t * 128:p0 + (t + 1) * 128, :], vo[i][:], [('vo', i)], ['v_s'])
            kr = self.sb(es, "krr", [64, 2, G], F32)
            self.dma('sp', kr[:], self.kr_s[:, :, p0:p0 + G].rearrange("v p t -> p v t"), ['kr_s'], ['krr'])
            sq3 = self.sb(es, "ksq3", [64, G], F32)
            rs3 = self.sb(es, "krs3", [64, G], F32)
            kro = self.sb(es, "kro", [64, G], BF16)
            self.act(sq3[:], kr[:, 0, :], AF.Square, ['krr'], ['ksq3'])
            for s in range(NSB):
                self.mm(self.ps(s)[0:64, :], self.onesf[0:64, 0:64], sq3[:, s * 512:(s + 1) * 512], True, True, ['onesf', 'ksq3'], [('ps', s)])
            self.rstd(rs3[:], self.psum[0:64, 0:G], 64, [('ps', 0), ('ps', 1)], ['krs3'])
            self.rope_apply(kr[:, 0, :], kr[:, 1, :], rs3, gk, tabs, kro[:], ['krr', 'krs3', 'gk'], 'kro')
            self.dma('sp', self.krot_s[:, p0:p0 + G], kro[:], ['kro'], ['krot_s'])
            P.barrier()

    def rope_apply(self, a, b, rs, gvec, tabs, out, in_keys, okey):
        cosT, sinT, t1, t2 = tabs
        self.stt('dve', t1[:], a, gvec[0:64, 1:2], rs[:], ALU.mult, ALU.mult, in_keys, ['rt1'])
        self.tt('dve', t1[:], t1[:], cosT[:], ALU.mult, ['rt1', 'rp_cos'], ['rt1'])
        self.stt('dve', t2[:], b, gvec[0:64, 2:3], rs[:], ALU.mult, ALU.mult, in_keys, ['rt2'])
        self.tt('dve', t2[:], t2[:], sinT[:], ALU.mult, ['rt2', 'rp_sin'], ['rt2'])
        self.tt('dve', out, t1[:], t2[:], ALU.add, ['rt1', 'rt2'], [okey])

    def stage_attn(self, l, W, pv, gq, gi):
        P = self.P
        p0 = gi * G
        o0 = (gi - 2) * G
        scale = 192.0 ** -0.5
        with ExitStack() as es:
            qn = self.sb(es, "qn", [128, 8, G], BF16)
            qr = self.sb(es, "qr", [64, 8, G], BF16)
            with ExitStack() as e2:
                tabs = self.rope_tables(e2, p0)
                cq = self.sb(e2, "cq", [128, 4, G], F32)
                sq = self.sb(e2, "qsq", [128, G], F32)
                rs = self.sb(e2, "qrs", [128, G], F32)
                cqn = self.sb(e2, "cqn", [128, 4, G], BF16)
                wq = self.sb(e2, "wq", [128, 4, 1536], BF16)
                wqp = self.sb(e2, "wqp", [128, 4, 8, 64], BF16)
                wsrc = W['w_q_up'][l].rearrange("(c p) f -> p c f", p=128)
                self.wload(wq[:], wsrc, 'wq')
                w3 = W['w_q_up'][l].rearrange("(c p) (h e) -> p c h e", p=128, e=192)
                for c in range(4):
                    self.wload(wqp[:, c, :, 0:32], w3[:, c, :, 160:192], 'wqp')
                    self.wload(wqp[:, c, :, 32:64], w3[:, c, :, 128:160], 'wqp')
                self.dma('sp', cq[:], self.cq_s[:, o0:o0 + G].rearrange("(c p) t -> p c t", p=128), ['cq_s'], ['cq'])
                for c in range(4):
                    self.act(sq[:], cq[:, c, :], AF.Square, ['cq'], ['qsq'])
                    for s in range(NSB):
                        self.mm(self.ps(s), self.onesf[:], sq[:, s * 512:(s + 1) * 512], c == 0, c == 3, ['onesf', 'qsq'], [('ps', s)])
                self.rstd(rs[:], self.psum[:, 0:G], 512, [('ps', 0), ('ps', 1)], ['qrs'])
                for c in range(4):
                    self.stt('dve', cqn[:, c, :], cq[:, c, :], pv['qng'][:, c:c + 1], rs[:], ALU.mult, ALU.mult, ['cq', 'qrs', 'qng'], ['cqn'])
                qf = self.sb(e2, "qf", [128, G], F32)
                sq2 = self.sb(e2, "qsq2", [128, G], F32)
                rs2 = self.sb(e2, "qrs2", [128, G], F32)
                qa = self.sb(e2, "qa", [64, G], F32)
                qb = self.sb(e2, "qb", [64, G], F32)
                sq3 = self.sb(e2, "qsq3", [64, G], F32)
                rs3 = self.sb(e2, "qrs3", [64, G], F32)
                for h in range(8):
                    for s in range(NSB):
                        for c in range(4):
                            self.mm(self.ps(s), wq[:, c, h * 192:h * 192 + 128], cqn[:, c, s * 512:(s + 1) * 512], c == 0, c == 3,
                                    ['wq', 'cqn'], [('ps', s)])
                    rb = [('ps', 0), ('ps', 1)]
                    self.act(qf[:], self.psum[:, 0:G], AF.Copy, rb, ['qf'])
                    self.act(sq2[:], self.psum[:, 0:G], AF.Square, rb, ['qsq2'])
                    for s in range(NSB):
                        self.mm(self.ps(2 + s), self.onesf[:], sq2[:, s * 512:(s + 1) * 512], True, True, ['onesf', 'qsq2'], [('ps', 2 + s)])
                    self.rstd(rs2[:], self.psum[:, 1024:1024 + G], 128, [('ps', 2), ('ps', 3)], ['qrs2'])
                    self.stt('dve', qn[:, h, :], qf[:], gq[:, 0:1], rs2[:], ALU.mult, ALU.mult, ['qf', 'qrs2', 'gq'], [('qn', h)])
                    for s in range(NSB):
                        for c in range(4):
                            self.mm(self.ps(4 + s)[0:64, :], wq[:, c, h * 192 + 128:h * 192 + 192], cqn[:, c, s * 512:(s + 1) * 512], c == 0, c == 3,
                                    ['wq', 'cqn'], [('ps', 4 + s)])
                        for c in range(4):
                            self.mm(self.ps(6 + s)[0:64, :], wqp[:, c, h, :], cqn[:, c, s * 512:(s + 1) * 512], c == 0, c == 3,
                                    ['wqp', 'cqn'], [('ps', 6 + s)])
                    self.act(qa[:], self.psum[0:64, 4 * 512:4 * 512 + G], AF.Copy, [('ps', 4), ('ps', 5)], ['qa'])
                    self.act(qb[:], self.psum[0:64, 6 * 512:6 * 512 + G], AF.Copy, [('ps', 6), ('ps', 7)], ['qb'])
                    self.act(sq3[:], qa[:], AF.Square, ['qa'], ['qsq3'])
                    for s in range(NSB):
                        self.mm(self.ps(4 + s)[0:64, :], self.onesf[0:64, 0:64], sq3[:, s * 512:(s + 1) * 512], True, True, ['onesf', 'qsq3'], [('ps', 4 + s)])
                    self.rstd(rs3[:], self.psum[0:64, 4 * 512:4 * 512 + G], 64, [('ps', 4), ('ps', 5)], ['qrs3'])
                    self.rope_apply(qa[:], qb[:], rs3, gq, tabs, qr[:, h, :], ['qa', 'qb', 'qrs3', 'gq'], ('qr', h))
                P.barrier()
            nkeys = p0 + G
            nkb_all = nkeys // 128
            krT = self.sb(es, "krT", [64, S_ALL], BF16)
            self.dma('sp', krT[:, 0:nkeys], self.krot_s[:, 0:nkeys], ['krot_s'], ['krT'])
            knT = [self.sb(es, "knT%d" % i, [128, S_ALL], BF16) for i in range(2)]
            vT = [self.sb(es, "vT%d" % i, [128, 32, 128], BF16) for i in range(2)]
            pT = [self.sb(es, "pT%d" % i, [128, 512], BF16) for i in range(3)]
            rden = self.sb(es, "rden", [128, 512], F32)
            oo = [self.sb(es, "oo%d" % i, [128, 512], BF16) for i in range(2)]
            it = 0
            npt = 0
            for h in range(8):
                hi = h % 2
                self.dma('sp', knT[hi][:, 0:nkeys], self.kn_s[h, :, 0:nkeys], ['kn_s'], [('knT', hi)])
                self.dma('sp', vT[hi][:, 0:nkb_all, :], self.v_s[0:nkeys, h * 128:(h + 1) * 128].rearrange("(kb p) d -> p kb d", p=128),
                         ['v_s'], [('vT', hi)])
                for qs in range(NSB):
                    q0 = p0 + qs * 512
                    qo = qs * 512
                    nkb = (q0 + 512) // 128
                    bO = 4 + (it % 2) * 2
                    bD = bO + 1
                    it += 1
                    for kb in range(nkb):
                        j = kb - q0 // 128
                        c0 = max(0, j) * 128
                        n = 512 - c0
                        bS = (kb % 2) * 2 if False else (npt % 3)
                        self.mm(self.ps(bS, n, c0), knT[hi][:, kb * 128:(kb + 1) * 128], qn[:, h, qo + c0:qo + 512], True, False,
                                [('knT', hi), ('qn', h)], [('ps', bS)])
                        self.mm(self.ps(bS, n, c0), krT[:, kb * 128:(kb + 1) * 128], qr[:, h, qo + c0:qo + 512], False, True,
                                ['krT', ('qr', h)], [('ps', bS)])
                        pi = npt % 3; npt += 1
                        bias = self.c_cbias if kb < 16 else self.c_zero
                        self.act(pT[pi][:, c0:512], self.ps(bS, n, c0), AF.Exp, [('ps', bS), 'cst'], [('pT', pi)], scale=scale, bias=bias)
                        if j >= 0:
                            self.tt('pool', pT[pi][:, c0:c0 + 128], pT[pi][:, c0:c0 + 128], self.trib[:], ALU.mult, [('pT', pi), 'trib'], [('pT', pi)])
                        self.mm(self.ps(bO, n, c0), vT[hi][:, kb, :], pT[pi][:, c0:512], kb == 0, kb == nkb - 1, [('vT', hi), ('pT', pi)], [('ps', bO)])
                        self.mm(self.ps(bD, n, c0), self.onesb[:], pT[pi][:, c0:512], kb == 0, kb == nkb - 1, ['onesb', ('pT', pi)], [('ps', bD)])
                    self.recip(rden[:], self.ps(bD), [('ps', bD)], ['rden'])
                    o = oo[it % 2]; ok = ('oo', it % 2)
                    self.tt('dve', o[:], self.ps(bO), rden[:], ALU.mult, [('ps', bO), 'rden'], [ok])
                    self.dma('sp', self.o_s[h * 128:(h + 1) * 128, o0 + qo:o0 + qo + 512], o[:], [ok], ['o_s'])
            P.barrier()

    def stage_mlstm(self, l, W, pv, mcw, bif, wif, gi):
        P = self.P
        own = gi >= 2
        p0 = gi * G
        o0 = (gi - 2) * G
        with ExitStack() as es:
            qT = self.sb(es, "qT", [128, 8, G], BF16)
            kT = self.sb(es, "kT", [128, 8, G], BF16)
            kTM = self.sb(es, "kTM", [128, NT, 1024], BF16)
            vp = self.sb(es, "vp", [128, NT, 4, 257], BF16)
            gsc = self.sb(es, "gsc", [128, NT, 16], F32)
            with ExitStack() as eb:
                xcb = self.sb(eb, "xcb", [128, 8, G], BF16)
                xmbf = self.sb(eb, "xmbf", [128, 8, G], BF16)
                with ExitStack() as e1:
                    xmb = self.sb(e1, "xmb", [128, 8, 3 + G], F32)
                    xcr = [self.sb(e1, "xc%d" % i, [128, G], F32) for i in range(2)]
                    self.dma('sp', xmb[:], self.xm_s[:, p0:p0 + 3 + G].rearrange("(c p) t -> p c t", p=128), ['xm_s'], ['xmb'])
                    if gi == 2:
                        self.ts('dve', xmb[:, :, 0:3], xmb[:, :, 0:3], self.c_flag, None, ALU.mult, None, ['xmb', 'cst'], ['xmb'])
                    for c in range(8):
                        xc = xcr[c % 2]; xk = ('xc', c % 2)
                        self.ts('dve', xc[:], xmb[:, c, 0:G], mcw[:, c, 0:1], pv['mcb'][:, c:c + 1], ALU.mult, ALU.add, ['xmb', 'mcw', 'mcb'], [xk])
                        for j in range(1, 4):
                            self.stt('dve', xc[:], xmb[:, c, j:j + G], mcw[:, c, j:j + 1], xc[:], ALU.mult, ALU.add, ['xmb', 'mcw', xk], [xk])
                        self.act(xc[:], xc[:], AF.Silu, [xk], [xk])
                        self.cp('pool', xcb[:, c, :], xc[:], [xk], [('xcb', c)])
                        self.cp('pool', xmbf[:, c, :], xmb[:, c, 3:3 + G], ['xmb'], [('xmbf', c)])
                        if own:
                            self.dma('sp', self.xc_s[c * 128:(c + 1) * 128, o0:o0 + G], xc[:], [xk], ['xc_s'])
                    P.barrier()
                with ExitStack() as e2:
                    vTf = self.sb(e2, "vTf", [128, 8, G], BF16)
                    wm = self.sb(e2, "wm", [128, 3, 8, 256], BF16)
                    vf = [self.sb(e2, "vf%d" % i, [128, 1024], F32) for i in range(2)]
                    gt = self.sb(e2, "gt", [128, NT, 16], F32)
                    for i, nm in enumerate(('w_mq', 'w_mk', 'w_mv')):
                        self.wload(wm[:, i, :, :], W[nm][l].rearrange("h (c p) e -> p (h c) e", p=128), ('wm', i))
                    nb = 0
                    for (wi, srcb, skey, dst, dkey) in ((0, xcb, 'xcb', qT, 'qT'), (1, xcb, 'xcb', kT, 'kT'), (2, xmbf, 'xmbf', vTf, 'vTf')):
                        for hh in range(4):
                            for ec in range(2):
                                b = (nb % 4) * 2; nb += 1
                                for s in range(NSB):
                                    for dc in range(2):
                                        self.mm(self.ps(b + s), wm[:, wi, hh * 2 + dc, ec * 128:(ec + 1) * 128], srcb[:, hh * 2 + dc, s * 512:(s + 1) * 512],
                                                dc == 0, dc == 1, [('wm', wi), (skey, hh * 2 + dc)], [('ps', b + s)])
                                if nb % 2 == 0:
                                    self.act(dst[:, hh * 2 + ec, :], self.psum[:, b * 512:b * 512 + G], AF.Copy, [('ps', b), ('ps', b + 1)], [(dkey, hh * 2 + ec)])
                                else:
                                    self.cp('dve', dst[:, hh * 2 + ec, :], self.psum[:, b * 512:b * 512 + G], [('ps', b), ('ps', b + 1)], [(dkey, hh * 2 + ec)])
                    allq = [('qT', i) for i in range(8)]
                    allk = [('kT', i) for i in range(8)]
                    allv = [('vTf', i) for i in range(8)]
                    for t in range(NT):
                        tsl = slice(t * 128, (t + 1) * 128)
                        b = (t % 2) * 2
                        for hh in range(4):
                            for dc in range(2):
                                self.mm(self.ps(b + hh // 2, 256, (hh % 2) * 256), xcb[:, hh * 2 + dc, tsl], wm[:, 1, hh * 2 + dc, :], dc == 0, dc == 1,
                                        [('wm', 1), ('xcb', hh * 2 + dc)], [('ps', b + hh // 2)])
                        self.act(kTM[:, t, :], self.psum[:, b * 512:b * 512 + 1024], AF.Copy, [('ps', b), ('ps', b + 1)], [('kTM', t)])
                        b2 = 4
                        for hh in range(4):
                            for dc in range(2):
                                self.mm(self.ps(b2 + hh // 2, 256, (hh % 2) * 256), xmbf[:, hh * 2 + dc, tsl], wm[:, 2, hh * 2 + dc, :], dc == 0, dc == 1,
                                        [('wm', 2), ('xmbf', hh * 2 + dc)], [('ps', b2 + hh // 2)])
                        vfi = vf[t % 2]; vk = ('vf', t % 2)
                        self.cp('dve', vfi[:], self.psum[:, b2 * 512:b2 * 512 + 1024], [('ps', b2), ('ps', b2 + 1)], [vk])
                        bg = 6 + (t % 2)
                        gk_ = ('ps', bg)
                        for i, (srcT, keys) in enumerate(((qT, allq), (kT, allk), (vTf, allv))):
                            for c in range(8):
                                self.mm(self.ps(bg, 8, 0), srcT[:, c, tsl], wif[:, i * 8 + c, :], i == 0 and c == 0, i == 2 and c == 7,
                                        ['wif', keys[c]], [gk_])
                        g = gt[:, t, :]
                        gkey = ('gt', t)
                        self.tt('dve', g[:, 0:8], self.ps(bg, 8, 0), bif[:], ALU.add, [gk_, 'bif'], [gkey])
                        self.act(g[:, 8:12], g[:, 4:8], AF.Exp, [gkey], [gkey], scale=-1.0)
                        self.act(g[:, 8:12], g[:, 8:12], AF.Ln, [gkey], [gkey], bias=1.0)
                        self.ts('dve', g[:, 8:12], g[:, 8:12], -1.0, None, ALU.mult, None, [gkey], [gkey])
                        self.mm(self.ps(bg, 4, 16), self.trif, g[:, 8:12], True, True, ['cst', gkey], [gk_])
                        self.mm(self.ps(bg, 4, 32), self.onesf[:], g[:, 8:12], True, True, ['onesf', gkey], [gk_])
                        sc = gsc[:, t, :]
                        skey = ('gsc', t)
                        self.tt('dve', g[:, 12:16], g[:, 0:4], self.ps(bg, 4, 16), ALU.subtract, [gkey, gk_], [gkey])
                        self.act(sc[:, 0:4], g[:, 12:16], AF.Exp, [gkey], [skey])
                        self.ts('dve', sc[:, 0:4], sc[:, 0:4], 1.0 / 16, None, ALU.mult, None, [skey], [skey])
                        self.act(sc[:, 4:8], self.ps(bg, 4, 16), AF.Exp, [gk_], [skey])
                        self.act(sc[:, 8:12], self.ps(bg, 4, 32), AF.Exp, [gk_], [skey])
                        for hh in range(4):
                            self.ts('dve', vp[:, t, hh, 0:256], vfi[:, hh * 256:(hh + 1) * 256], sc[:, hh:hh + 1], None, ALU.mult, None, [vk, skey], [('vp', t)])
                        self.cp('dve', vp[:, t, :, 256:257], sc[:, 0:4].rearrange("p (h o) -> p h o", o=1), [skey], [('vp', t)])
                    P.barrier()
            with ExitStack() as e3:
                if own:
                    sT = [self.sb(e3, "sT%d" % i, [128, 128], BF16) for i in range(2)]
                    nd = [self.sb(e3, "nd%d" % i, [128, 257], F32) for i in range(2)]
                    hst = self.sb(e3, "hst", [128, NT * 4, 8], F32)
                    hh_ = [self.sb(e3, "hh%d" % i, [128, 256], F32) for i in range(2)]
                    junk = self.sb(e3, "mjunk", [128, 256], F32)
                    zt = self.sb(e3, "zt", [128, 8, G], F32)
                    xcs = self.sb(e3, "xcs", [128, 8, G], F32)
                    hno = self.sb(e3, "hno", [128, 8, G], BF16)
                    self.dma('sp', zt[:], self.z_s[:, o0:o0 + G].rearrange("(c p) t -> p c t", p=128), ['z_s'], ['zt'])
                    self.dma('sp', xcs[:], self.xc_s[:, o0:o0 + G].rearrange("(c p) t -> p c t", p=128), ['xc_s'], [('xcs', c) for c in range(8)])
                    for c in range(8):
                        self.ts('pool', xcs[:, c, :], xcs[:, c, :], pv['skp'][:, c:c + 1], None, ALU.mult, None, [('xcs', c), 'skp'], [('xcs', c)])
                    self.memset('dve', hst[:], 0.0, ['hst'])
                it = 0
                for t in range(NT):
                    tsl = slice(t * 128, (t + 1) * 128)
                    sc = gsc[:, t, :]
                    skey = ('gsc', t)
                    for hh in range(4):
                        i2 = it % 2; it += 1
                        if own:
                            bS = 0
                            for dc in range(2):
                                self.mm(self.ps(bS, 128, i2 * 128), kT[:, hh * 2 + dc, tsl], qT[:, hh * 2 + dc, tsl], dc == 0, dc == 1,
                                        [('kT', hh * 2 + dc), ('qT', hh * 2 + dc)], [('psS', i2)])
                            self.tt('dve', sT[i2][:], self.ps(bS, 128, i2 * 128), self.trif, ALU.mult, [('psS', i2), 'cst'], [('sT', i2)])
                            bN = 1 + i2
                            for dc in range(2):
                                self.mm(self.ps(bN, 257), qT[:, hh * 2 + dc, tsl], self.Cbf[:, hh * 2 + dc, :], dc == 0, False,
                                        [('qT', hh * 2 + dc), ('Cbf', hh)], [('ps', bN)])
                            self.mm(self.ps(bN, 257), sT[i2][:], vp[:, t, hh, :], False, True, [('sT', i2), ('vp', t)], [('ps', bN)])
                            n_ = nd[i2]; nk = ('nd', i2)
                            self.act(n_[:], self.ps(bN, 257), AF.Copy, [('ps', bN), skey], [nk], scale=sc[:, 4 + hh:5 + hh])
                            hs = hst[:, t * 4 + hh, :]
                            hk = ('hst', t * 4 + hh)
                            self.act(hs[:, 0:1], n_[:, 256:257], AF.Abs, [nk, 'hst'], [hk])
                            self.ts('dve', hs[:, 0:1], hs[:, 0:1], 1.0, None, ALU.max, None, [hk], [hk])
                            self.recip(hs[:, 1:2], hs[:, 0:1], [hk], [hk])
                            hv = hh_[i2]; hvk = ('hh', i2)
                            self.act(hv[:], n_[:, 0:256], AF.Copy, [nk, hk], [hvk, hk], scale=hs[:, 1:2], accum_out=hs[:, 2:3])
                            self.act(junk[:], hv[:], AF.Square, [hvk, hk], ['mjunk', hk], accum_out=hs[:, 3:4])
                            self.ts('dve', hs[:, 4:5], hs[:, 2:3], 1.0 / 256, None, ALU.mult, None, [hk], [hk])
                            self.tt('dve', hs[:, 5:6], hs[:, 4:5], hs[:, 4:5], ALU.mult, [hk], [hk])
                            self.stt('dve', hs[:, 6:7], hs[:, 3:4], 1.0 / 256, hs[:, 5:6], ALU.mult, ALU.subtract, [hk], [hk])
                            self.ts('dve', hs[:, 6:7], hs[:, 6:7], EPS, None, ALU.add, None, [hk], [hk])
                            self.act(hs[:, 6:7], hs[:, 6:7], AF.Sqrt, [hk], [hk])
                            self.recip(hs[:, 7:8], hs[:, 6:7], [hk], [hk])
                            self.ts('dve', hv[:], hv[:], hs[:, 4:5], hs[:, 7:8], ALU.subtract, ALU.mult, [hvk, hk], [hvk])
                            bT = 7
                            for dc in range(2):
                                self.tr(self.ps(bT, 128, (i2 * 2 + dc) * 128), hv[:, dc * 128:(dc + 1) * 128], self.idf, [hvk, 'cst'], [('psT', i2)])
                            for dc in range(2):
                                c = hh * 2 + dc
                                self.stt('dve', xcs[:, c, tsl], self.ps(bT, 128, (i2 * 2 + dc) * 128), pv['gng'][:, c:c + 1], xcs[:, c, tsl],
                                         ALU.mult, ALU.add, [('psT', i2), 'gng', ('xcs', c)], [('xcs', c)])
                                self.tt('pool', hno[:, c, tsl], xcs[:, c, tsl], zt[:, c, tsl], ALU.mult, [('xcs', c), 'zt'], [('hno', c)])
                        for dc in range(2):
                            c = hh * 2 + dc
                            bU = 3 + (it % 2) * 2 + dc
                            self.mm(self.ps(bU, 257), kTM[:, t, hh * 256 + dc * 128:hh * 256 + (dc + 1) * 128], vp[:, t, hh, :], True, True,
                                    [('kTM', t), ('vp', t)], [('ps', bU)])
                            self.ts('dve', self.Cst[:, c, :], self.Cst[:, c, :], sc[:, 8 + hh:9 + hh], None, ALU.mult, None, [('Cst', hh), skey], [('Cst', hh)])
                            self.stt('dve', self.Cst[:, c, :], self.ps(bU, 257), sc[:, 8 + hh:9 + hh], self.Cst[:, c, :], ALU.mult, ALU.add,
                                     [('ps', bU), ('Cst', hh), skey], [('Cst', hh)])
                            self.act(self.Cbf[:, c, :], self.Cst[:, c, :], AF.Copy, [('Cst', hh)], [('Cbf', hh)])
                if gi == 1:
                    for hh in range(4):
                        for dc in range(2):
                            c = hh * 2 + dc
                            self.ts('dve', self.Cst[:, c, :], self.Cst[:, c, :], self.c_flag, None, ALU.mult, None, [('Cst', hh), 'cst'], [('Cst', hh)])
                            self.act(self.Cbf[:, c, :], self.Cst[:, c, :], AF.Copy, [('Cst', hh)], [('Cbf', hh)])
                if own:
                    for c in range(8):
                        self.dma('sp', self.hn_s[c * 128:(c + 1) * 128, o0:o0 + G], hno[:, c, :], [('hno', c)], ['hn_s'])
                P.barrier()

    def stage_merge(self, l, W, pv, gi, hT, x_src):
        P = self.P
        o0 = (gi - 2) * G
        wv = W['w_in'][l].rearrange("(c p) f -> p c f", p=128)
        wouts = [W[n][l].rearrange("(c p) f -> p c f", p=128) for n in ('w_conv_out', 'w_mla_out', 'w_mlstm_out')]
        srcs = [self.cb_s, self.o_s, self.hn_s]
        with ExitStack() as es:
            mg = self.sb(es, "mg", [128, 16, G], BF16)
            with ExitStack() as e2:
                br = [self.sb(e2, "br%d" % j, [128, 8, G], BF16) for j in range(3)]
                for j in range(3):
                    self.dma('sp', br[j][:], srcs[j][:, o0:o0 + G].rearrange("(c p) t -> p c t", p=128), [('cb_s', 'o_s', 'hn_s')[j]], [('br', j)])
                wg = [self.sb(e2, "wg%d" % i, [128, 16, 256], BF16) for i in range(4)]
                wy = [self.sb(e2, "wy%d" % i, [128, 8, 256], BF16) for i in range(4)]
                sg = [self.sb(e2, "sg%d" % i, [128, 512], F32) for i in range(2)]
                macc = [self.sb(e2, "macc%d" % i, [128, 512], F32) for i in range(2)]
                tmp = [self.sb(e2, "mtmp%d" % i, [128, 512], F32) for i in range(2)]
                nw = 0
                nb = 0
                ns = 0
                for d0 in range(0, DM, 256):
                    wgs = []
                    for j in range(3):
                        i = nw % 4; nw += 1
                        self.wload(wg[i][:], wv[:, :, 4928 + j * DM + d0:4928 + j * DM + d0 + 256], ('wg', i))
                        self.wload(wy[i][:], wouts[j][:, :, d0:d0 + 256], ('wy', i))
                        wgs.append(i)
                    for dd in range(2):
                        dc = d0 // 128 + dd
                        for s in range(NSB):
                            ssl = slice(s * 512, (s + 1) * 512)
                            ma = macc[ns % 2]; mk = ('macc', ns % 2); ns += 1
                            for j in range(3):
                                i = wgs[j]
                                bG = (nb % 4) * 2; bY = bG + 1; nb += 1
                                for k in range(16):
                                    self.mm(self.ps(bG), wg[i][:, k, dd * 128:(dd + 1) * 128], hT[:, k, ssl], k == 0, k == 15,
                                            [('wg', i), ('hT', s)], [('ps', bG)])
                                for k in range(8):
                                    self.mm(self.ps(bY), wy[i][:, k, dd * 128:(dd + 1) * 128], br[j][:, k, ssl], k == 0, k == 7,
                                            [('wy', i), ('br', j)], [('ps', bY)])
                                sgi = sg[nb % 2]; sk = ('sg', nb % 2)
                                self.act(sgi[:], self.ps(bG), AF.Sigmoid, [('ps', bG), 'bgate'], [sk], bias=pv['bgate'][:, j * 16 + dc:j * 16 + dc + 1])
                                if j == 0:
                                    self.tt('dve', ma[:], sgi[:], self.ps(bY), ALU.mult, [sk, ('ps', bY)], [mk])
                                else:
                                    tm = tmp[j % 2]; tk = ('mtmp', j % 2)
                                    self.tt('dve', tm[:], sgi[:], self.ps(bY), ALU.mult, [sk, ('ps', bY)], [tk])
                                    if j == 1:
                                        self.tt('pool', ma[:], ma[:], tm[:], ALU.add, [mk, tk], [mk])
                                    else:
                                        self.tt('pool', mg[:, dc, ssl], ma[:], tm[:], ALU.add, [mk, tk], [('mg', s)])
                P.barrier()
            self.resid_proj(es, mg, 'mg', 16, W['w_mix_out'][l], x_src, o0, self.x1_s, o0, G)
            P.barrier()

    def resid_proj(self, es, aT, akey, nk, w_d, x_src, xr0, x_dst, dr0, ntok, kchunk=16):
        wv = w_d.rearrange("(c p) f -> p c f", p=128)
        nkb = (nk + kchunk - 1) // kchunk
        wb = [self.sb(es, "rw%d" % i, [128, kchunk, 512], BF16) for i in range(3)]
        xt = [self.sb(es, "rx%d" % i, [128, 512], F32) for i in range(3)]
        ntl = ntok // 128
        assert ntl <= 8
        nw = 0
        nx = 0
        for fc in range(4):
            fsl = slice(fc * 512, (fc + 1) * 512)
            for kb in range(nkb):
                k0 = kb * kchunk
                kn = min(kchunk, nk - k0)
                i = nw % 3; nw += 1
                self.wload(wb[i][:, 0:kn, :], wv[:, k0:k0 + kn, fsl], ('rw', i))
                for t in range(ntl):
                    for k in range(kn):
                        self.mm(self.ps(t), aT[:, k0 + k, t * 128:(t + 1) * 128], wb[i][:, k, :], (k0 + k) == 0, (k0 + k) == nk - 1,
                                [('rw', i), (akey, t // 4)], [('ps', t)])
            for t in range(ntl):
                xi = nx % 3; nx += 1
                self.dma('sp', xt[xi][:], x_src[xr0 + t * 128:xr0 + (t + 1) * 128, fsl], (), [('rx', xi)])
                self.tt('dve', xt[xi][:], xt[xi][:], self.ps(t), ALU.add, [('rx', xi), ('ps', t)], [('rx', xi)])
                self.dma('sp', x_dst[dr0 + t * 128:dr0 + (t + 1) * 128, fsl], xt[xi][:], [('rx', xi)], ['xdst'])

    def stage_xattn(self, l, W, pv, kx, vx, gi):
        P = self.P
        o0 = (gi - 2) * G
        scale = 128.0 ** -0.5
        with ExitStack() as es:
            ox = self.sb(es, "ox", [128, 4, G], BF16)
            with ExitStack() as e1:
                hT = self.sb(e1, "hTx", [128, 16, G], BF16)
                with ExitStack() as e2:
                    self.norm_transpose(e2, self.x1_s, o0, NT, W['norm_x'][l], hT, 'hTx')
                    P.barrier()
                wq = self.sb(e1, "wxq", [128, 16, 512], BF16)
                self.wload(wq[:], W['w_xq'][l].rearrange("(c p) f -> p c f", p=128), 'wxq')
                qx = self.sb(e1, "qx", [128, 4, G], BF16)
                qf = self.sb(e1, "xqf", [128, G], F32)
                sq = self.sb(e1, "xsq", [128, G], F32)
                rs = self.sb(e1, "xrs", [128, G], F32)
                for h in range(4):
                    b = (h % 2) * 2
                    for s in range(NSB):
                        for k in range(16):
                            self.mm(self.ps(b + s), wq[:, k, h * 128:(h + 1) * 128], hT[:, k, s * 512:(s + 1) * 512], k == 0, k == 15,
                                    ['wxq', ('hTx', s)], [('ps', b + s)])
                    rb = [('ps', b), ('ps', b + 1)]
                    self.act(qf[:], self.psum[:, b * 512:b * 512 + G], AF.Copy, rb, ['xqf'])
                    self.act(sq[:], self.psum[:, b * 512:b * 512 + G], AF.Square, rb, ['xsq'])
                    for s in range(NSB):
                        self.mm(self.ps(4 + s), self.onesf[:], sq[:, s * 512:(s + 1) * 512], True, True, ['onesf', 'xsq'], [('ps', 4 + s)])
                    self.rstd(rs[:], self.psum[:, 4 * 512:4 * 512 + G], 128, [('ps', 4), ('ps', 5)], ['xrs'])
                    self.stt('dve', qx[:, h, :], qf[:], pv['xgq'][:, 0:1], rs[:], ALU.mult, ALU.mult, ['xqf', 'xrs', 'xgq'], [('qx', h)])
                pT = [self.sb(e1, "xpT%d" % i, [128, 512], BF16) for i in range(3)]
                rden = self.sb(e1, "xrden", [128, 512], F32)
                npt = 0
                it = 0
                for h in range(4):
                    for s in range(NSB):
                        ssl = slice(s * 512, (s + 1) * 512)
                        bO = 4 + (it % 2) * 2; bD = bO + 1; it += 1
                        for mc in range(2):
                            bS = npt % 3
                            pi = npt % 3; npt += 1
                            self.mm(self.ps(bS), kx[:, h, mc * 128:(mc + 1) * 128], qx[:, h, ssl], True, True, ['kx', ('qx', h)], [('ps', bS)])
                            self.act(pT[pi][:], self.ps(bS), AF.Exp, [('ps', bS)], [('xpT', pi)], scale=scale)
                            self.mm(self.ps(bO), vx[:, mc, h * 128:(h + 1) * 128], pT[pi][:], mc == 0, mc == 1, ['vx', ('xpT', pi)], [('ps', bO)])
                            self.mm(self.ps(bD), self.onesb[:], pT[pi][:], mc == 0, mc == 1, ['onesb', ('xpT', pi)], [('ps', bD)])
                        self.recip(rden[:], self.ps(bD), [('ps', bD)], ['xrden'])
                        self.tt('dve', ox[:, h, ssl], self.ps(bO), rden[:], ALU.mult, [('ps', bO), 'xrden'], [('ox', s)])
                P.barrier()
            self.resid_proj(es, ox, 'ox', 4, W['w_xo'][l], self.x1_s, o0, self.x2_s, o0, G)
            P.barrier()

    def stage_ffn(self, l, W, gi, x_out):
        P = self.P
        NJ = FFN_H // 128
        wv = W['w_ffn_in'][l].rearrange("(c p) f -> p c f", p=128)
        for half in range(G // 512):
            o0 = (gi - 2) * G + half * 512
            with ExitStack() as es:
                aT = self.sb(es, "aT", [128, NJ, 512], BF16)
                with ExitStack() as e1:
                    hT = self.sb(e1, "hTf", [128, 16, 512], BF16)
                    with ExitStack() as e2:
                        self.norm_transpose(e2, self.x2_s, o0, 4, W['norm_ffn'][l], hT, 'hTf')
                        P.barrier()
                    wg = [self.sb(e1, "fwg%d" % i, [128, 16, 512], BF16) for i in range(2)]
                    wu = [self.sb(e1, "fwu%d" % i, [128, 16, 512], BF16) for i in range(2)]
                    sg = [self.sb(e1, "fsg%d" % i, [128, 512], F32) for i in range(2)]
                    nb = 0
                    for jb in range(NJ // 4):
                        i = jb % 2
                        self.wload(wg[i][:], wv[:, :, jb * 512:(jb + 1) * 512], ('fwg', i))
                        self.wload(wu[i][:], wv[:, :, FFN_H + jb * 512:FFN_H + (jb + 1) * 512], ('fwu', i))
                        for jj in range(4):
                            j = jb * 4 + jj
                            bG = (nb % 4) * 2; bU = bG + 1; nb += 1
                            for k in range(16):
                                self.mm(self.ps(bG), wg[i][:, k, jj * 128:(jj + 1) * 128], hT[:, k, :], k == 0, k == 15, [('fwg', i), ('hTf', 0)], [('ps', bG)])
                            for k in range(16):
                                self.mm(self.ps(bU), wu[i][:, k, jj * 128:(jj + 1) * 128], hT[:, k, :], k == 0, k == 15, [('fwu', i), ('hTf', 0)], [('ps', bU)])
                            s = sg[nb % 2]; sk = ('fsg', nb % 2)
                            self.act(s[:], self.ps(bG), AF.Silu, [('ps', bG)], [sk])
                            self.tt('dve', aT[:, j, :], s[:], self.ps(bU), ALU.mult, [sk, ('ps', bU)], [('aT', 0)])
                    P.barrier()
                self.resid_proj(es, aT, 'aT', NJ, W['w_ffn_out'][l], self.x2_s, o0, x_out, o0, 512, kchunk=22)
                P.barrier()


WNAMES = ['norm_mix', 'w_in', 'b_gate', 'conv_dw', 'conv_dw_b', 'conv_ln_g', 'conv_ln_b', 'w_conv_out',
          'mla_q_norm', 'mla_kv_norm', 'w_q_up', 'w_kv_up', 'mla_g_q', 'mla_g_k', 'w_mla_out',
          'mlstm_conv_w', 'mlstm_conv_b', 'w_mq', 'w_mk', 'w_mv', 'w_if', 'b_if', 'mlstm_gn_g', 'mlstm_skip',
          'w_mlstm_out', 'w_mix_out', 'norm_x', 'norm_mem', 'w_xq', 'w_xkv', 'xattn_g_q', 'xattn_g_k', 'w_xo',
          'norm_ffn', 'w_ffn_in', 'w_ffn_out']


def build_program(shapes, layers, debug=()):
    nc = bass.Bass("TRN2", target_bir_lowering=False)
    B = Builder(nc, layers, debug)
    W = {}
    for n in WNAMES:
        shp = [len(layers)] + list(shapes[n][1:])
        W[n] = nc.dram_tensor(n, shp, F32, kind="ExternalInput").ap()
    x_own = nc.dram_tensor("x_own", [S_OWN, DM], F32, kind="ExternalInput").ap()
    x_ctx = nc.dram_tensor("x_ctx", [S_OWN, DM], F32, kind="ExternalInput").ap()
    y = nc.dram_tensor("y", [S_OWN, DM], F32, kind="ExternalOutput").ap()
    B.setup()
    if len(layers) == 1:
        B.layer(0, W, x_own, x_ctx, y)
    else:
        xl0 = nc.dram_tensor("xl0", [S_OWN, DM], F32).ap()
        ctx1 = nc.dram_tensor("ctx1", [S_OWN, DM], F32).ap()
        CH = 256
        bin_ = [nc.dram_tensor("ccin%d" % i, [CH, DM], F32).ap() for i in range(2)]
        bout = [nc.dram_tensor("ccout%d" % i, [2 * CH, DM], F32).ap() for i in range(2)]
        B.layer(0, W, x_own, x_ctx, xl0)
        rg = [[0, 1], [2, 3], [4, 5], [6, 7]]
        for i in range(S_OWN // CH):
            b = i % 2
            B.P.add('sp', lambda e, i=i, b=b: e.dma_start(out=bin_[b], in_=xl0[i * CH:(i + 1) * CH, :]), (), [('ccin', b)], dma=True)
            B.P.add('pool', lambda e, b=b: e.collective_compute("AllGather", ALU.bypass, replica_groups=rg, ins=[bin_[b]], outs=[bout[b]]),
                    [('ccin', b)], [('ccout', b)], dma=True, cc=True)
            B.P.add('sp', lambda e, i=i, b=b: e.dma_start(out=ctx1[i * CH:(i + 1) * CH, :], in_=bout[b][0:CH, :]), [('ccout', b)], ['ctx1'], dma=True)
        B.P.barrier()
        B.layer(1, W, xl0, ctx1, y)
    B.P.barrier()
    es = ExitStack()
    B.P.emit(nc, es)
    es.close()
    B.gs.close()
    return nc


def make_cst(half):
    c = np.zeros((128, 512), np.float32)
    c[:, 0:128] = np.eye(128, dtype=np.float32)
    r = np.arange(128)
    c[:, 128:256] = (r[None, :] >= r[:, None]).astype(np.float32)
    inv = (10000.0 ** (-(np.arange(32, dtype=np.float64)) / 32.0)) / (2 * math.pi)
    c[0:64, 256] = np.concatenate([inv, inv]).astype(np.float32)
    c[0:32, 257] = -1.0
    c[32:64, 257] = 1.0
    c[:, 258] = 1.0 if half == 1 else 0.0
    c[:, 259] = 0.0 if half == 1 else -30000.0
    c[:, 260] = 0.0
    return c


_PROG_CACHE = {}


def run_layer(l, inputs, x_cur, debug=(), cores=None):
    shapes = {n: inputs[n].shape for n in WNAMES}
    key = ('layer', tuple(debug))
    if key not in _PROG_CACHE:
        _PROG_CACHE[key] = build_program(shapes, [0], debug)
    nc = _PROG_CACHE[key]
    if cores is None:
        cores = list(range(8))
    wl = {n: np.ascontiguousarray(inputs[n][l:l + 1]) for n in WNAMES}
    in_maps = []
    for c in cores:
        b, half = c // 2, c % 2
        m = dict(wl)
        m['x_own'] = np.ascontiguousarray(x_cur[b, half * S_OWN:(half + 1) * S_OWN])
        m['x_ctx'] = np.ascontiguousarray(x_cur[b, 0:S_OWN])
        pos = inputs['positions'][b].astype(np.int32)
        if half == 1:
            pa = pos
        else:
            pa = np.concatenate([pos[0:S_OWN], pos[0:S_OWN]])
        m['pos'] = np.ascontiguousarray(pa[None, :])
        m['mem'] = np.ascontiguousarray(inputs['mem'][b])
        m['cst'] = make_cst(half)
        in_maps.append(m)
    res = run_bass_kernel_spmd(nc, in_maps, core_ids=list(range(len(cores))))
    return res.results


def run_fused(inputs, cores=None):
    shapes = {n: inputs[n].shape for n in WNAMES}
    key = ('fused',)
    if key not in _PROG_CACHE:
        _PROG_CACHE[key] = build_program(shapes, [0, 1])
    nc = _PROG_CACHE[key]
    if cores is None:
        cores = list(range(8))
    wl = {n: np.ascontiguousarray(inputs[n], dtype=np.float32) for n in WNAMES}
    x = inputs['x']
    in_maps = []
    for c in cores:
        b, half = c // 2, c % 2
        m = dict(wl)
        m['x_own'] = np.ascontiguousarray(x[b, half * S_OWN:(half + 1) * S_OWN], dtype=np.float32)
        m['x_ctx'] = np.ascontiguousarray(x[b, 0:S_OWN], dtype=np.float32)
        pos = inputs['positions'][b].astype(np.int32)
        if half == 1:
            pa = pos
        else:
            pa = np.concatenate([pos[0:S_OWN], pos[0:S_OWN]])
        m['pos'] = np.ascontiguousarray(pa[None, :])
        m['mem'] = np.ascontiguousarray(inputs['mem'][b], dtype=np.float32)
        m['cst'] = make_cst(half)
        in_maps.append(m)
    res = run_bass_kernel_spmd(nc, in_maps, core_ids=list(range(len(cores))))
    return res.results


def kernel(**inputs):
    inputs = {k: np.asarray(v) for k, v in inputs.items()}
    outs = run_fused(inputs)
    x = inputs['x']
    out = np.empty(x.shape, np.float32)
    for c in range(8):
        b, half = c // 2, c % 2
        out[b, half * S_OWN:(half + 1) * S_OWN] = outs[c]['y']
    return out
```

```python
import math
from contextlib import ExitStack

import numpy as np
import concourse.bass as bass
import concourse.mybir as mybir
from concourse.bass_utils import run_bass_kernel_spmd

F32 = mybir.dt.float32
BF16 = mybir.dt.bfloat16
I32 = mybir.dt.int32
AF = mybir.ActivationFunctionType
ALU = mybir.AluOpType

ENGS = ['pe', 'act', 'dve', 'pool', 'sp']

DM = 2048
S_OWN = 2048
S_ALL = 4096
G = 1024
NT = G // 128
NSB = G // 512
EPS = 1e-6
FFN_H = 5632
IN_COLS = 11072
DEPTH = 2


class Op:
    __slots__ = ('eng', 'fn', 'dma', 'deps_c', 'deps_d', 'idx', 'signal', 'count', 'sem', 'val', 'cc')


class Prog:
    def __init__(self):
        self.ops = {e: [] for e in ENGS}
        self.last_w = {}
        self.rd_c = {}
        self.rd_d = {}

    def add(self, eng, fn, reads=(), writes=(), dma=False, cc=False):
        o = Op()
        o.eng = eng; o.fn = fn; o.dma = dma; o.cc = cc
        o.idx = len(self.ops[eng]); o.signal = False
        o.count = 0; o.sem = None; o.val = 0
        dc = {}
        dd = set()

        def need(p):
            if p.dma:
                dd.add(p)
            else:
                if p.eng == 'pe' and eng == 'pe' and not dma:
                    return
                if dc.get(p.eng, -1) < p.idx:
                    dc[p.eng] = p.idx
        for k in reads:
            p = self.last_w.get(k)
            if p is not None:
                need(p)
        for k in writes:
            p = self.last_w.get(k)
            if p is not None:
                need(p)
            for (pe, pidx) in self.rd_c.get(k, {}).items():
                need(self.ops[pe][pidx])
            for r in self.rd_d.get(k, ()):
                need(r)
        o.deps_c = dc
        o.deps_d = dd
        self.ops[eng].append(o)
        for k in reads:
            if dma:
                self.rd_d.setdefault(k, set()).add(o)
            else:
                self.rd_c.setdefault(k, {})[eng] = o.idx
        for k in writes:
            self.last_w[k] = o
            self.rd_c[k] = {}
            self.rd_d[k] = set()
        return o

    def barrier(self):
        o = Op()
        o.eng = 'sp'; o.fn = (lambda e: e.nop()); o.dma = False; o.cc = False
        o.idx = len(self.ops['sp']); o.signal = False
        o.count = 0; o.sem = None; o.val = 0
        o.deps_c = {e: len(self.ops[e]) - 1 for e in ENGS if len(self.ops[e]) > 0}
        start = getattr(self, '_bar_idx', {e: 0 for e in ENGS})
        dd = set()
        for e in ENGS:
            for p in self.ops[e][start[e]:]:
                if p.dma:
                    dd.add(p)
        o.deps_d = dd
        self.ops['sp'].append(o)
        for e in ENGS:
            if e == 'sp':
                continue
            q = Op()
            q.eng = e; q.fn = (lambda en: en.nop()); q.dma = False; q.cc = False
            q.idx = len(self.ops[e]); q.signal = False
            q.count = 0; q.sem = None; q.val = 0
            q.deps_c = {'sp': o.idx}
            q.deps_d = set()
            self.ops[e].append(q)
        self._bar_idx = {e: len(self.ops[e]) for e in ENGS}
        self.last_w = {}
        self.rd_c = {}
        self.rd_d = {}

    def emit(self, nc, es, npool=20):
        ops = self.ops
        for e in ENGS:
            for o in ops[e]:
                for (pe, pidx) in o.deps_c.items():
                    ops[pe][pidx].signal = True
        sems = {e: es.enter_context(nc.semaphore("s_" + e)) for e in ENGS}
        for e in ENGS:
            c = 0
            for o in ops[e]:
                if (not o.dma) and o.signal:
                    c += 1
                    o.count = c
        for e in ENGS:
            nd = sum(1 for o in ops[e] if o.dma)
            if nd == 0:
                continue
            n = min(npool, nd)
            pool = [es.enter_context(nc.semaphore("d_%s_%d" % (e, i))) for i in range(n)]
            i = 0
            ccsem = None
            ncc = 0
            for o in ops[e]:
                if o.dma and o.cc:
                    if ccsem is None:
                        ccsem = es.enter_context(nc.semaphore("cc_%s" % e))
                    ncc += 1
                    o.sem = ccsem
                    o.val = ncc
                elif o.dma:
                    o.sem = pool[i % n]
                    o.val = 16 * (i // n + 1)
                    i += 1
        block = es.enter_context(nc.Block())

        def run(engname, eng):
            waited = {}
            for o in ops[engname]:
                waits = []
                for (pe, pidx) in o.deps_c.items():
                    waits.append((sems[pe], ops[pe][pidx].count))
                for d in o.deps_d:
                    waits.append((d.sem, d.val))
                if o.dma and o.cc and o.val > 1:
                    waits.append((o.sem, o.val - 1))
                elif o.dma and (not o.cc) and o.val > 16:
                    waits.append((o.sem, o.val - 16))
                waits.sort(key=lambda sv: -sv[1])
                for (s, v) in waits:
                    key = id(s)
                    if waited.get(key, 0) >= v:
                        continue
                    eng.wait_ge(s, v)
                    waited[key] = v
                ins = o.fn(eng)
                if o.dma and o.cc:
                    ins.then_inc(o.sem)
                elif o.dma:
                    ins.then_inc(o.sem, 16)
                elif o.signal:
                    ins.then_inc(sems[engname], 1)

        @block.sync
        def _(e):
            run('sp', e)

        @block.tensor
        def _(e):
            run('pe', e)

        @block.scalar
        def _(e):
            run('act', e)

        @block.vector
        def _(e):
            run('dve', e)

        @block.gpsimd
        def _(e):
            run('pool', e)


class Builder:
    def __init__(self, nc, layers, debug=()):
        self.nc = nc
        self.P = Prog()
        self.layers = layers
        self.debug = set(debug)
        self.uid = 0
        self.gs = ExitStack()

    def dma(self, eng, out, in_, r, w, **kw):
        self.P.add(eng, lambda e: e.dma_start(out=out, in_=in_, **kw), r, w, dma=True)

    def act(self, out, in_, func, r, w, **kw):
        self.P.add('act', lambda e: e.activation(out=out, in_=in_, func=func, **kw), r, w)

    def mm(self, out, lhsT, rhs, start, stop, r, w):
        self.P.add('pe', lambda e: e.matmul(out, lhsT=lhsT, rhs=rhs, start=start, stop=stop), r, w)

    def tr(self, out, in_, ident, r, w):
        self.P.add('pe', lambda e: e.transpose(out=out, in_=in_, identity=ident), r, w)

    def tt(self, eng, out, in0, in1, op, r, w):
        self.P.add(eng, lambda e: e.tensor_tensor(out=out, in0=in0, in1=in1, op=op), r, w)

    def ts(self, eng, out, in0, s1, s2, op0, op1, r, w):
        if op1 is None:
            self.P.add(eng, lambda e: e.tensor_scalar(out=out, in0=in0, scalar1=s1, scalar2=None, op0=op0), r, w)
        else:
            self.P.add(eng, lambda e: e.tensor_scalar(out=out, in0=in0, scalar1=s1, scalar2=s2, op0=op0, op1=op1), r, w)

    def stt(self, eng, out, in0, scalar, in1, op0, op1, r, w):
        self.P.add(eng, lambda e: e.scalar_tensor_tensor(out=out, in0=in0, scalar=scalar, in1=in1, op0=op0, op1=op1), r, w)

    def cp(self, eng, out, in_, r, w):
        self.P.add(eng, lambda e: e.tensor_copy(out=out, in_=in_), r, w)

    def memset(self, eng, ap, val, w):
        self.P.add(eng, lambda e: e.memset(ap, val), (), w)

    def recip(self, out, in_, r, w):
        self.P.add('dve', lambda e: e.reciprocal(out=out, in_=in_), r, w)

    def sb(self, es, name, shape, dt):
        self.uid += 1
        return es.enter_context(self.nc.sbuf_tensor("%s_%d" % (name, self.uid), shape, dt))

    def dram(self, name, shape, dt):
        kind = "ExternalOutput" if name in self.debug else "Internal"
        if kind == "Internal":
            return self.nc.dram_tensor(name, shape, dt).ap()
        return self.nc.dram_tensor(name, shape, dt, kind=kind).ap()

    def rstd(self, out, in_, n, r, w):
        self.ts('dve', out, in_, 1.0 / n, EPS, ALU.mult, ALU.add, r, w)
        self.act(out, out, AF.Sqrt, w, w)
        self.recip(out, out, w, w)

    def setup(self):
        nc = self.nc
        g = self.gs
        self.cst_d = nc.dram_tensor("cst", [128, 512], F32, kind="ExternalInput").ap()
        self.pos_d = nc.dram_tensor("pos", [1, S_ALL], I32, kind="ExternalInput").ap()
        self.mem_d = nc.dram_tensor("mem", [256, DM], F32, kind="ExternalInput").ap()
        self.cst = self.sb(g, "cst", [128, 512], F32)
        self.idb = self.sb(g, "idb", [128, 128], BF16)
        self.trib = self.sb(g, "trib", [128, 128], BF16)
        self.onesf = self.sb(g, "onesf", [128, 128], F32)
        self.onesb = self.sb(g, "onesb", [128, 128], BF16)
        self.psum = g.enter_context(nc.psum_tensor("ps", [128, 4096], F32))
        self.dma('sp', self.cst[:], self.cst_d, (), ['cst'])
        self.cp('dve', self.idb[:], self.cst[:, 0:128], ['cst'], ['idb'])
        self.cp('dve', self.trib[:], self.cst[:, 128:256], ['cst'], ['trib'])
        self.memset('dve', self.onesf[:], 1.0, ['onesf'])
        self.memset('dve', self.onesb[:], 1.0, ['onesb'])
        self.idf = self.cst[:, 0:128]
        self.trif = self.cst[:, 128:256]
        self.c_if2pi = self.cst[0:64, 256:257]
        self.c_sign = self.cst[0:64, 257:258]
        self.c_flag = self.cst[:, 258:259]
        self.c_cbias = self.cst[:, 259:260]
        self.c_zero = self.cst[:, 260:261]
        self.Cst = self.sb(g, "Cst", [128, 8, 257], F32)
        self.Cbf = self.sb(g, "Cbf", [128, 8, 257], BF16)
        self.u_s = self.dram("u_s", [1024, S_ALL], F32)
        self.cq_s = self.dram("cq_s", [512, S_OWN], F32)
        self.ckv_s = self.dram("ckv_s", [256, S_ALL], F32)
        self.kr_s = self.dram("kr_s", [2, 64, S_ALL], F32)
        self.xm_s = self.dram("xm_s", [1024, 3 + S_ALL], F32)
        self.z_s = self.dram("z_s", [1024, S_OWN], F32)
        self.xc_s = self.dram("xc_s", [1024, S_OWN], F32)
        self.cb_s = self.dram("cb_s", [1024, S_OWN], BF16)
        self.o_s = self.dram("o_s", [1024, S_OWN], BF16)
        self.hn_s = self.dram("hn_s", [1024, S_OWN], BF16)
        self.kn_s = self.dram("kn_s", [8, 128, S_ALL], BF16)
        self.v_s = self.dram("v_s", [S_ALL, 1024], BF16)
        self.krot_s = self.dram("krot_s", [64, S_ALL], BF16)
        self.x1_s = self.dram("x1_s", [S_OWN, DM], F32)
        self.x2_s = self.dram("x2_s", [S_OWN, DM], F32)
        self.P.barrier()

    def ps(self, bank, n=512, off=0):
        return self.psum[:, bank * 512 + off: bank * 512 + off + n]

    def norm_transpose(self, es, src, row0, ntiles, gain_d, hT, hkey):
        gn = self.sb(es, "gain", [128, DM], F32)
        self.dma('sp', gn[:], gain_d.partition_broadcast(128), (), ['gain'])
        xts = [self.sb(es, "xt%d" % i, [128, DM], F32) for i in range(2)]
        hbs = [self.sb(es, "hb%d" % i, [128, DM], BF16) for i in range(2)]
        junk = self.sb(es, "junk", [128, DM], F32)
        st = self.sb(es, "nst", [128, 2 * ntiles], F32)
        self.memset('dve', st[:], 0.0, ['nst'])
        def stage1(t):
            xt = xts[t % 2]; hb = hbs[t % 2]
            kx = ('xt', t % 2); kh = ('hb', t % 2)
            self.dma('sp', xt[:], src[row0 + t * 128: row0 + (t + 1) * 128, :], (), [kx])
            self.act(junk[:], xt[:], AF.Square, [kx, 'nst'], ['junk', 'nst'], accum_out=st[:, 2 * t:2 * t + 1])
            self.rstd(st[:, 2 * t + 1:2 * t + 2], st[:, 2 * t:2 * t + 1], DM, ['nst'], ['nst'])
            self.stt('dve', hb[:], xt[:], st[:, 2 * t + 1:2 * t + 2], gn[:], ALU.mult, ALU.mult, [kx, 'nst', 'gain'], [kh])

        def stage2(t):
            hb = hbs[t % 2]
            kh = ('hb', t % 2)
            pb = (t % 2) * 2
            pv = self.psum[:, pb * 512:(pb + 2) * 512].bitcast(BF16)
            for c in range(16):
                self.tr(pv[:, c * 128:(c + 1) * 128], hb[:, c * 128:(c + 1) * 128], self.idb[:], [kh, 'idb'], [('ps', pb), ('ps', pb + 1)])
            outv = hT[:, :, t * 128:(t + 1) * 128]
            inv = pv.rearrange("p (c t) -> p c t", c=16)
            if t % 2 == 0:
                self.act(outv, inv, AF.Copy, [('ps', pb), ('ps', pb + 1)], [(hkey, t // 4)])
            else:
                self.cp('dve', outv, inv, [('ps', pb), ('ps', pb + 1)], [(hkey, t // 4)])

        for t in range(ntiles + 1):
            if t < ntiles:
                stage1(t)
            if t >= 1:
                stage2(t - 1)

    def wload(self, dst, src, key):
        self.dma('pool', dst, src, (), [key])

    def proj_fm(self, wt, wkey, fsl, M, nk, rhs_fn, rkeys, bank0, nsb):
        for s in range(nsb):
            for k in range(nk):
                self.mm(self.ps(bank0 + s)[0:M, :], wt[:, k, fsl:fsl + M], rhs_fn(k, s), k == 0, k == nk - 1,
                        [wkey] + rkeys(k, s), [('ps', bank0 + s)])

    def layer(self, l, W, x_own, x_ctx, x_out):
        P = self.P
        nc = self.nc
        with ExitStack() as ls:
            def col(name, src, nch, rows=128):
                t = self.sb(ls, name, [rows, nch], F32)
                self.dma('sp', t[:], src.rearrange("(c p) -> p c", p=rows), (), [name], allow_slow_non_contiguous=True)
                return t
            pv = {}
            pv['bgate'] = col('bgate', W['b_gate'][l], 48)
            pv['cdb'] = col('cdb', W['conv_dw_b'][l], 8)
            pv['clg'] = col('clg', W['conv_ln_g'][l], 8)
            pv['clb'] = col('clb', W['conv_ln_b'][l], 8)
            pv['qng'] = col('qng', W['mla_q_norm'][l], 4)
            pv['kvg'] = col('kvg', W['mla_kv_norm'][l], 2)
            pv['mcb'] = col('mcb', W['mlstm_conv_b'][l], 8)
            pv['gng'] = col('gng', W['mlstm_gn_g'][l], 8)
            pv['skp'] = col('skp', W['mlstm_skip'][l], 8)
            pv['xgq'] = col('xgq', W['xattn_g_q'][l], 1)
            pv['xgk'] = col('xgk', W['xattn_g_k'][l], 1)
            cdw = self.sb(ls, 'cdw', [128, 8, 31], F32)
            for c in range(8):
                self.dma('sp', cdw[:, c, :], W['conv_dw'][l][:, c * 128:(c + 1) * 128].rearrange("j p -> p j"), (), ['cdw'], allow_slow_non_contiguous=True)
            mcw = self.sb(ls, 'mcw', [128, 8, 4], F32)
            for c in range(8):
                self.dma('sp', mcw[:, c, :], W['mlstm_conv_w'][l][:, c * 128:(c + 1) * 128].rearrange("j p -> p j"), (), ['mcw'], allow_slow_non_contiguous=True)
            gq = self.sb(ls, 'gq', [128, 4], F32)
            gk = self.sb(ls, 'gk', [128, 4], F32)
            for (t, nm, key) in ((gq, 'mla_g_q', 'gq'), (gk, 'mla_g_k', 'gk')):
                src = W[nm][l]
                self.dma('sp', t[:, 0:1], src[0:128].rearrange("(p o) -> p o", o=1), (), [key], allow_slow_non_contiguous=True)
                self.dma('sp', t[0:64, 1:2], src[128:192].rearrange("(p o) -> p o", o=1), (), [key], allow_slow_non_contiguous=True)
                self.dma('sp', t[0:32, 2:3], src[160:192].rearrange("(p o) -> p o", o=1), (), [key], allow_slow_non_contiguous=True)
                self.dma('sp', t[32:64, 2:3], src[128:160].rearrange("(p o) -> p o", o=1), (), [key], allow_slow_non_contiguous=True)
            bif = self.sb(ls, 'bif', [128, 8], F32)
            self.dma('sp', bif[:], W['b_if'][l].partition_broadcast(128), (), ['bif'])
            wif = self.sb(ls, 'wif', [128, 24, 8], BF16)
            self.wload(wif[:], W['w_if'][l].rearrange("(c p) e -> p c e", p=128), 'wif')
            kx = self.sb(ls, 'kx', [128, 4, 256], BF16)
            vx = self.sb(ls, 'vx', [128, 2, 512], BF16)
            self.stage_mem(l, W, pv, kx, vx)
            self.memset('dve', self.Cst[:], 0.0, ['Cst'])
            self.memset('dve', self.Cbf[:], 0.0, ['Cbf'])
            with ExitStack() as es:
                z = self.sb(es, "zpad", [128, 8, 3], F32)
                self.memset('dve', z[:], 0.0, ['zpad'])
                self.dma('sp', self.xm_s[:, 0:3].rearrange("(c p) j -> p c j", p=128), z[:], ['zpad'], ['xm_s'])
                P.barrier()

            for gi in range(4):
                own = gi >= 2
                p0 = gi * G
                o0 = (gi - 2) * G
                src = x_own if own else x_ctx
                r0 = o0 if own else p0
                with ExitStack() as gsx:
                    hT = self.sb(gsx, "hT", [128, 16, G], BF16)
                    with ExitStack() as es:
                        self.norm_transpose(es, src, r0, NT, W['norm_mix'][l], hT, 'hT')
                        P.barrier()
                    self.stage_proj(l, W, gi, hT)
                    if own:
                        self.stage_conv(l, W, pv, cdw, gi)
                    self.stage_kv(l, W, pv, gk, gi)
                    if own:
                        self.stage_attn(l, W, pv, gq, gi)
                    self.stage_mlstm(l, W, pv, mcw, bif, wif, gi)
                    if own:
                        self.stage_merge(l, W, pv, gi, hT, src)
                if own:
                    self.stage_xattn(l, W, pv, kx, vx, gi)
                    self.stage_ffn(l, W, gi, x_out)
            P.barrier()

    def stage_mem(self, l, W, pv, kx, vx):
        P = self.P
        with ExitStack() as es:
            mT = self.sb(es, "mT", [128, 16, 256], BF16)
            with ExitStack() as es2:
                self.norm_transpose(es2, self.mem_d, 0, 2, W['norm_mem'][l], mT, 'mT')
                P.barrier()
            wv = W['w_xkv'][l].rearrange("(c p) f -> p c f", p=128)
            wk = self.sb(es, "wxkv", [128, 16, 1024], BF16)
            self.wload(wk[:, :, 0:512], wv[:, :, 0:512], 'wxkv0')
            self.wload(wk[:, :, 512:1024], wv[:, :, 512:1024], 'wxkv1')
            kf = self.sb(es, "kf", [128, 256], F32)
            sq = self.sb(es, "sq", [128, 256], F32)
            rs = self.sb(es, "rs", [128, 256], F32)
            for h in range(4):
                b = h % 2
                for k in range(16):
                    self.mm(self.ps(b, 256), wk[:, k, h * 128:(h + 1) * 128], mT[:, k, :], k == 0, k == 15,
                            ['wxkv0', ('mT', 0)], [('ps', b)])
                self.act(kf[:], self.ps(b, 256), AF.Copy, [('ps', b)], ['kf'])
                self.act(sq[:], self.ps(b, 256), AF.Square, [('ps', b)], ['sq'])
                self.mm(self.ps(2 + b, 256), self.onesf[:], sq[:], True, True, ['onesf', 'sq'], [('ps', 2 + b)])
                self.rstd(rs[:], self.ps(2 + b, 256), 128, [('ps', 2 + b)], ['rs'])
                self.stt('dve', kx[:, h, :], kf[:], pv['xgk'][:, 0:1], rs[:], ALU.mult, ALU.mult, ['kf', 'rs', 'xgk'], ['kx'])
            for mc in range(2):
                b = 4 + mc
                for k in range(16):
                    self.mm(self.ps(b), mT[:, k, mc * 128:(mc + 1) * 128], wk[:, k, 512:1024], k == 0, k == 15,
                            ['wxkv1', ('mT', 0)], [('ps', b)])
                self.act(vx[:, mc, :], self.ps(b), AF.Copy, [('ps', b)], ['vx'])
            P.barrier()

    def stage_proj(self, l, W, gi, hT):
        P = self.P
        own = gi >= 2
        p0 = gi * G
        o0 = (gi - 2) * G
        wv = W['w_in'][l].rearrange("(c p) f -> p c f", p=128)
        with ExitStack() as es:
            wb = [self.sb(es, "wb%d" % i, [128, 16, 512], BF16) for i in range(3)]
            stg = [self.sb(es, "stg%d" % i, [128, G], F32) for i in range(4)]
            sig = [self.sb(es, "sig%d" % i, [128, G], F32) for i in range(2)]
            cnt = {'w': 0, 's': 0, 'b': 0, 'g': 0}

            def nextw():
                i = cnt['w'] % 3; cnt['w'] += 1
                return wb[i], ('wb', i)

            def nexts():
                i = cnt['s'] % 4; cnt['s'] += 1
                return stg[i], ('stg', i)

            def nextb():
                b = (cnt['b'] % 4) * 2; cnt['b'] += 1
                return b

            def rhs_fn(k, s):
                return hT[:, k, s * 512:(s + 1) * 512]

            def rkeys(k, s):
                return [('hT', s)]

            def simple_seg(c0, width, M, dst_fn, func):
                for f0 in range(0, width, 512):
                    fw = min(512, width - f0)
                    wt, wkey = nextw()
                    self.wload(wt[:, :, 0:fw], wv[:, :, c0 + f0:c0 + f0 + fw], wkey)
                    for j in range(0, fw, M):
                        b = nextb()
                        self.proj_fm(wt, wkey, j, M, 16, rhs_fn, rkeys, b, NSB)
                        st_, skey = nexts()
                        self.act(st_[0:M, :], self.psum[0:M, b * 512:b * 512 + G], func, [('ps', b), ('ps', b + 1)], [skey])
                        dst_fn(f0 + j, st_, skey)

            if own:
                for f0 in range(0, 1024, 512):
                    wa, ka = nextw()
                    self.wload(wa[:], wv[:, :, f0:f0 + 512], ka)
                    wg, kg = nextw()
                    self.wload(wg[:], wv[:, :, 1024 + f0:1024 + f0 + 512], kg)
                    for j in range(0, 512, 128):
                        ba = nextb()
                        self.proj_fm(wa, ka, j, 128, 16, rhs_fn, rkeys, ba, NSB)
                        bg = nextb()
                        self.proj_fm(wg, kg, j, 128, 16, rhs_fn, rkeys, bg, NSB)
                        i = cnt['g'] % 2; cnt['g'] += 1
                        self.act(sig[i][:], self.psum[:, bg * 512:bg * 512 + G], AF.Sigmoid, [('ps', bg), ('ps', bg + 1)], [('sig', i)])
                        st_, skey = nexts()
                        self.tt('dve', st_[:], self.psum[:, ba * 512:ba * 512 + G], sig[i][:], ALU.mult,
                                [('ps', ba), ('ps', ba + 1), ('sig', i)], [skey])
                        ch = (f0 + j) // 128
                        self.dma('sp', self.u_s[ch * 128:(ch + 1) * 128, p0:p0 + G], st_[:], [skey], ['u_s'])
                simple_seg(2048, 512, 128,
                           lambda f, st_, sk: self.dma('sp', self.cq_s[f:f + 128, o0:o0 + G], st_[:], [sk], ['cq_s']), AF.Copy)
                simple_seg(3904, 1024, 128,
                           lambda f, st_, sk: self.dma('sp', self.z_s[f:f + 128, o0:o0 + G], st_[:], [sk], ['z_s']), AF.Silu)
            elif gi == 1:
                for f0 in range(0, 1024, 512):
                    wa, ka = nextw()
                    self.wload(wa[:], wv[:, :, f0:f0 + 512], ka)
                    wg, kg = nextw()
                    self.wload(wg[:], wv[:, :, 1024 + f0:1024 + f0 + 512], kg)
                    for j in range(0, 512, 128):
                        ba = nextb()
                        bg = nextb()
                        for k in range(16):
                            self.mm(self.ps(ba, 128), wa[:, k, j:j + 128], hT[:, k, G - 128:G], k == 0, k == 15, [ka, ('hT', 1)], [('ps', ba)])
                        for k in range(16):
                            self.mm(self.ps(bg, 128), wg[:, k, j:j + 128], hT[:, k, G - 128:G], k == 0, k == 15, [kg, ('hT', 1)], [('ps', bg)])
                        i = cnt['g'] % 2; cnt['g'] += 1
                        self.act(sig[i][:, 0:128], self.ps(bg, 128), AF.Sigmoid, [('ps', bg)], [('sig', i)], scale=1.0)
                        st_, skey = nexts()
                        self.stt('dve', st_[:, 0:128], self.ps(ba, 128), self.c_flag, sig[i][:, 0:128], ALU.mult, ALU.mult,
                                 [('ps', ba), ('sig', i), 'cst'], [skey])
                        ch = (f0 + j) // 128
                        self.dma('sp', self.u_s[ch * 128:(ch + 1) * 128, p0 + G - 128:p0 + G], st_[:, 0:128], [skey], ['u_s'])
            simple_seg(2560, 256, 128,
                       lambda f, st_, sk: self.dma('sp', self.ckv_s[f:f + 128, p0:p0 + G], st_[:], [sk], ['ckv_s']), AF.Copy)
            wt, wkey = nextw()
            self.wload(wt[:, :, 0:64], wv[:, :, 2816:2880], wkey)
            self.wload(wt[:, :, 64:96], wv[:, :, 2848:2880], wkey)
            self.wload(wt[:, :, 96:128], wv[:, :, 2816:2848], wkey)
            for v_ in range(2):
                b = nextb()
                self.proj_fm(wt, wkey, v_ * 64, 64, 16, rhs_fn, rkeys, b, NSB)
                st_, skey = nexts()
                self.act(st_[0:64, :], self.psum[0:64, b * 512:b * 512 + G], AF.Copy, [('ps', b), ('ps', b + 1)], [skey])
                self.dma('sp', self.kr_s[v_, :, p0:p0 + G], st_[0:64, :], [skey], ['kr_s'])
            simple_seg(2880, 1024, 128,
                       lambda f, st_, sk: self.dma('sp', self.xm_s[f:f + 128, 3 + p0:3 + p0 + G], st_[:], [sk], ['xm_s']), AF.Copy)
            P.barrier()

    def stage_conv(self, l, W, pv, cdw, gi):
        P = self.P
        p0 = gi * G
        o0 = (gi - 2) * G
        with ExitStack() as es:
            ub = [self.sb(es, "ub%d" % i, [128, 30 + G], F32) for i in range(2)]
            cv = self.sb(es, "cv", [128, 8, G], F32)
            sq = [self.sb(es, "csq%d" % i, [128, G], F32) for i in range(2)]
            mean = self.sb(es, "mean", [128, G], F32)
            rs = self.sb(es, "crs", [128, G], F32)
            tmp = [self.sb(es, "ctmp%d" % i, [128, G], F32) for i in range(2)]
            cbo = [self.sb(es, "cbo%d" % i, [128, G], BF16) for i in range(2)]
            for c in range(8):
                u = ub[c % 2]; uk = ('ub', c % 2)
                self.dma('sp', u[:], self.u_s[c * 128:(c + 1) * 128, p0 - 30:p0 + G], ['u_s'], [uk])
                eng = 'dve' if c % 2 == 0 else 'pool'
                eng = 'dve'
                self.ts(eng, cv[:, c, :], u[:, 0:G], cdw[:, c, 0:1], pv['cdb'][:, c:c + 1], ALU.mult, ALU.add,
                        [uk, 'cdw', 'cdb'], [('cv', c)])
                for j in range(1, 31):
                    self.stt(eng, cv[:, c, :], u[:, j:j + G], cdw[:, c, j:j + 1], cv[:, c, :], ALU.mult, ALU.add,
                             [uk, 'cdw', ('cv', c)], [('cv', c)])
                s = sq[c % 2]; sk = ('csq', c % 2)
                self.act(s[:], cv[:, c, :], AF.Square, [('cv', c)], [sk])
                for sbk in range(NSB):
                    self.mm(self.ps(sbk), self.onesf[:], cv[:, c, sbk * 512:(sbk + 1) * 512], c == 0, c == 7, ['onesf', ('cv', c)], [('ps', sbk)])
                    self.mm(self.ps(2 + sbk), self.onesf[:], s[:, sbk * 512:(sbk + 1) * 512], c == 0, c == 7, ['onesf', sk], [('ps', 2 + sbk)])
            r_s1 = [('ps', 0), ('ps', 1)]
            r_s2 = [('ps', 2), ('ps', 3)]
            self.act(mean[:], self.psum[:, 0:G], AF.Copy, r_s1, ['mean'], scale=1.0 / 1024)
            t0 = tmp[0]
            self.tt('dve', t0[:], mean[:], mean[:], ALU.mult, ['mean'], [('ctmp', 0)])
            self.stt('dve', rs[:], self.psum[:, 1024:1024 + G], 1.0 / 1024, t0[:], ALU.mult, ALU.subtract, r_s2 + [('ctmp', 0)], ['crs'])
            self.ts('dve', rs[:], rs[:], EPS, None, ALU.add, None, ['crs'], ['crs'])
            self.act(rs[:], rs[:], AF.Sqrt, ['crs'], ['crs'])
            self.recip(rs[:], rs[:], ['crs'], ['crs'])
            for c in range(8):
                t = tmp[c % 2]; tk = ('ctmp', c % 2)
                self.tt('dve', t[:], cv[:, c, :], mean[:], ALU.subtract, [('cv', c), 'mean'], [tk])
                self.tt('dve', t[:], t[:], rs[:], ALU.mult, [tk, 'crs'], [tk])
                o = cbo[c % 2]; ok = ('cbo', c % 2)
                self.act(o[:], t[:], AF.Silu, [tk, 'clg', 'clb'], [ok], scale=pv['clg'][:, c:c + 1], bias=pv['clb'][:, c:c + 1])
                self.dma('sp', self.cb_s[c * 128:(c + 1) * 128, o0:o0 + G], o[:], [ok], ['cb_s'])
            P.barrier()

    def rope_tables(self, es, p0):
        pi_ = self.sb(es, "rp_i", [64, G], I32)
        t = self.sb(es, "rp_t", [64, G], F32)
        kf = self.sb(es, "rp_k", [64, G], F32)
        cosT = self.sb(es, "rp_cos", [64, G], F32)
        sinT = self.sb(es, "rp_sin", [64, G], F32)
        self.dma('sp', pi_[:], self.pos_d[:, p0:p0 + G].partition_broadcast(64), (), ['rp_i'])
        self.cp('dve', t[:], pi_[:], ['rp_i'], ['rp_t'])
        self.ts('dve', t[:], t[:], self.c_if2pi, None, ALU.mult, None, ['rp_t', 'cst'], ['rp_t'])
        for which, dst, key in ((0, sinT, 'rp_sin'), (1, cosT, 'rp_cos')):
            if which == 1:
                self.ts('dve', t[:], t[:], 0.25, None, ALU.add, None, ['rp_t'], ['rp_t'])
            self.cp('dve', pi_[:], t[:], ['rp_t'], ['rp_i'])
            self.cp('dve', kf[:], pi_[:], ['rp_i'], ['rp_k'])
            self.tt('dve', kf[:], t[:], kf[:], ALU.subtract, ['rp_t', 'rp_k'], ['rp_k'])
            self.ts('dve', dst[:], kf[:], 0.5, None, ALU.is_gt, None, ['rp_k'], [key])
            self.tt('dve', kf[:], kf[:], dst[:], ALU.subtract, ['rp_k', key], ['rp_k'])
            self.act(dst[:], kf[:], AF.Sin, ['rp_k'], [key], scale=2 * math.pi)
        self.ts('dve', sinT[:], sinT[:], self.c_sign, None, ALU.mult, None, ['rp_sin', 'cst'], ['rp_sin'])
        t1 = self.sb(es, "rt1", [64, G], F32)
        t2 = self.sb(es, "rt2", [64, G], F32)
        return cosT, sinT, t1, t2

    def stage_kv(self, l, W, pv, gk, gi):
        P = self.P
        p0 = gi * G
        with ExitStack() as es:
            tabs = self.rope_tables(es, p0)
            ck = self.sb(es, "ck", [128, 2, G], F32)
            sq = self.sb(es, "ksq", [128, G], F32)
            rs = self.sb(es, "krs", [128, G], F32)
            ckn = self.sb(es, "ckn", [128, 2, G], BF16)
            wkv = self.sb(es, "wkv", [128, 2, 2048], BF16)
            wvv = self.sb(es, "wvv", [128, 2, 8, 128], BF16)
            self.wload(wkv[:], W['w_kv_up'][l].rearrange("(c p) f -> p c f", p=128), 'wkv')
            wv5 = W['w_kv_up'][l].rearrange("(c p) (h two d) -> p c h two d", p=128, two=2, d=128)
            for c in range(2):
                self.wload(wvv[:, c, :, :], wv5[:, c, :, 1, :], 'wvv')
            self.dma('sp', ck[:], self.ckv_s[:, p0:p0 + G].rearrange("(c p) t -> p c t", p=128), ['ckv_s'], ['ck'])
            for c in range(2):
                self.act(sq[:], ck[:, c, :], AF.Square, ['ck'], ['ksq'])
                for s in range(NSB):
                    self.mm(self.ps(s), self.onesf[:], sq[:, s * 512:(s + 1) * 512], c == 0, c == 1, ['onesf', 'ksq'], [('ps', s)])
            self.rstd(rs[:], self.psum[:, 0:G], 256, [('ps', 0), ('ps', 1)], ['krs'])
            for c in range(2):
                self.stt('dve', ckn[:, c, :], ck[:, c, :], pv['kvg'][:, c:c + 1], rs[:], ALU.mult, ALU.mult, ['ck', 'krs', 'kvg'], ['ckn'])
            kf = [self.sb(es, "kff%d" % i, [128, G], F32) for i in range(2)]
            sq2 = [self.sb(es, "ksq2%d" % i, [128, G], F32) for i in range(2)]
            rs2 = [self.sb(es, "krs2%d" % i, [128, G], F32) for i in range(2)]
            kno = [self.sb(es, "kno%d" % i, [128, G], BF16) for i in range(2)]
            for h in range(8):
                i = h % 2
                b = 2 + i * 2
                for s in range(NSB):
                    for c in range(2):
                        self.mm(self.ps(b + s), wkv[:, c, h * 256:h * 256 + 128], ckn[:, c, s * 512:(s + 1) * 512], c == 0, c == 1,
                                ['wkv', 'ckn'], [('ps', b + s)])
                rb = [('ps', b), ('ps', b + 1)]
                self.act(kf[i][:], self.psum[:, b * 512:b * 512 + G], AF.Copy, rb, [('kff', i)])
                self.act(sq2[i][:], self.psum[:, b * 512:b * 512 + G], AF.Square, rb, [('ksq2', i)])
                for s in range(NSB):
                    self.mm(self.ps(6 + s), self.onesf[:], sq2[i][:, s * 512:(s + 1) * 512], True, True, ['onesf', ('ksq2', i)], [('ps', 6 + s)])
                self.rstd(rs2[i][:], self.psum[:, 6 * 512:6 * 512 + G], 128, [('ps', 6), ('ps', 7)], [('krs2', i)])
                self.stt('dve', kno[i][:], kf[i][:], gk[:, 0:1], rs2[i][:], ALU.mult, ALU.mult, [('kff', i), ('krs2', i), 'gk'], [('kno', i)])
                self.dma('sp', self.kn_s[h, :, p0:p0 + G], kno[i][:], [('kno', i)], ['kn_s'])
            vo = [self.sb(es, "vo%d" % i, [128, 1024], BF16) for i in range(2)]
            for t in range(NT):
                i = t % 2
                b = 2 + i * 2
                for hb in range(2):
                    for c in range(2):
                        self.mm(self.ps(b + hb), ckn[:, c, t * 128:(t + 1) * 128], wvv[:, c, hb * 4:(hb + 1) * 4, :].rearrange("p h d -> p (h d)"),
                                c == 0, c == 1, ['wvv', 'ckn'], [('ps', b + hb)])
                self.act(vo[i][:], self.psum[:, b * 512:b * 512 + 1024], AF.Copy, [('ps', b), ('ps', b + 1)], [('vo', i)])
                self.dma('sp', self.v_s[p0 + t * 128:p0 + (t + 1) * 128, :], vo[i][:], [('vo', i)], ['v_s'])
            kr = self.sb(es, "krr", [64, 2, G], F32)
            self.dma('sp', kr[:], self.kr_s[:, :, p0:p0 + G].rearrange("v p t -> p v t"), ['kr_s'], ['krr'])
            sq3 = self.sb(es, "ksq3", [64, G], F32)
            rs3 = self.sb(es, "krs3", [64, G], F32)
            kro = self.sb(es, "kro", [64, G], BF16)
            self.act(sq3[:], kr[:, 0, :], AF.Square, ['krr'], ['ksq3'])
            for s in range(NSB):
                self.mm(self.ps(s)[0:64, :], self.onesf[0:64, 0:64], sq3[:, s * 512:(s + 1) * 512], True, True, ['onesf', 'ksq3'], [('ps', s)])
            self.rstd(rs3[:], self.psum[0:64, 0:G], 64, [('ps', 0), ('ps', 1)], ['krs3'])
            self.rope_apply(kr[:, 0, :], kr[:, 1, :], rs3, gk, tabs, kro[:], ['krr', 'krs3', 'gk'], 'kro')
            self.dma('sp', self.krot_s[:, p0:p0 + G], kro[:], ['kro'], ['krot_s'])
            P.barrier()

    def rope_apply(self, a, b, rs, gvec, tabs, out, in_keys, okey):
        cosT, sinT, t1, t2 = tabs
        self.stt('dve', t1[:], a, gvec[0:64, 1:2], rs[:], ALU.mult, ALU.mult, in_keys, ['rt1'])
        self.tt('dve', t1[:], t1[:], cosT[:], ALU.mult, ['rt1', 'rp_cos'], ['rt1'])
        self.stt('dve', t2[:], b, gvec[0:64, 2:3], rs[:], ALU.mult, ALU.mult, in_keys, ['rt2'])
        self.tt('dve', t2[:], t2[:], sinT[:], ALU.mult, ['rt2', 'rp_sin'], ['rt2'])
        self.tt('dve', out, t1[:], t2[:], ALU.add, ['rt1', 'rt2'], [okey])

    def stage_attn(self, l, W, pv, gq, gi):
        P = self.P
        p0 = gi * G
        o0 = (gi - 2) * G
        scale = 192.0 ** -0.5
        with ExitStack() as es:
            qn = self.sb(es, "qn", [128, 8, G], BF16)
            qr = self.sb(es, "qr", [64, 8, G], BF16)
            with ExitStack() as e2:
                tabs = self.rope_tables(e2, p0)
                cq = self.sb(e2, "cq", [128, 4, G], F32)
                sq = self.sb(e2, "qsq", [128, G], F32)
                rs = self.sb(e2, "qrs", [128, G], F32)
                cqn = self.sb(e2, "cqn", [128, 4, G], BF16)
                wq = self.sb(e2, "wq", [128, 4, 1536], BF16)
                wqp = self.sb(e2, "wqp", [128, 4, 8, 64], BF16)
                wsrc = W['w_q_up'][l].rearrange("(c p) f -> p c f", p=128)
                self.wload(wq[:], wsrc, 'wq')
                w3 = W['w_q_up'][l].rearrange("(c p) (h e) -> p c h e", p=128, e=192)
                for c in range(4):
                    self.wload(wqp[:, c, :, 0:32], w3[:, c, :, 160:192], 'wqp')
                    self.wload(wqp[:, c, :, 32:64], w3[:, c, :, 128:160], 'wqp')
                self.dma('sp', cq[:], self.cq_s[:, o0:o0 + G].rearrange("(c p) t -> p c t", p=128), ['cq_s'], ['cq'])
                for c in range(4):
                    self.act(sq[:], cq[:, c, :], AF.Square, ['cq'], ['qsq'])
                    for s in range(NSB):
                        self.mm(self.ps(s), self.onesf[:], sq[:, s * 512:(s + 1) * 512], c == 0, c == 3, ['onesf', 'qsq'], [('ps', s)])
                self.rstd(rs[:], self.psum[:, 0:G], 512, [('ps', 0), ('ps', 1)], ['qrs'])
                for c in range(4):
                    self.stt('dve', cqn[:, c, :], cq[:, c, :], pv['qng'][:, c:c + 1], rs[:], ALU.mult, ALU.mult, ['cq', 'qrs', 'qng'], ['cqn'])
                qf = self.sb(e2, "qf", [128, G], F32)
                sq2 = self.sb(e2, "qsq2", [128, G], F32)
                rs2 = self.sb(e2, "qrs2", [128, G], F32)
                qa = self.sb(e2, "qa", [64, G], F32)
                qb = self.sb(e2, "qb", [64, G], F32)
                sq3 = self.sb(e2, "qsq3", [64, G], F32)
                rs3 = self.sb(e2, "qrs3", [64, G], F32)
                for h in range(8):
                    for s in range(NSB):
                        for c in range(4):
                            self.mm(self.ps(s), wq[:, c, h * 192:h * 192 + 128], cqn[:, c, s * 512:(s + 1) * 512], c == 0, c == 3,
                                    ['wq', 'cqn'], [('ps', s)])
                    rb = [('ps', 0), ('ps', 1)]
                    self.act(qf[:], self.psum[:, 0:G], AF.Copy, rb, ['qf'])
                    self.act(sq2[:], self.psum[:, 0:G], AF.Square, rb, ['qsq2'])
                    for s in range(NSB):
                        self.mm(self.ps(2 + s), self.onesf[:], sq2[:, s * 512:(s + 1) * 512], True, True, ['onesf', 'qsq2'], [('ps', 2 + s)])
                    self.rstd(rs2[:], self.psum[:, 1024:1024 + G], 128, [('ps', 2), ('ps', 3)], ['qrs2'])
                    self.stt('dve', qn[:, h, :], qf[:], gq[:, 0:1], rs2[:], ALU.mult, ALU.mult, ['qf', 'qrs2', 'gq'], [('qn', h)])
                    for s in range(NSB):
                        for c in range(4):
                            self.mm(self.ps(4 + s)[0:64, :], wq[:, c, h * 192 + 128:h * 192 + 192], cqn[:, c, s * 512:(s + 1) * 512], c == 0, c == 3,
                                    ['wq', 'cqn'], [('ps', 4 + s)])
                        for c in range(4):
                            self.mm(self.ps(6 + s)[0:64, :], wqp[:, c, h, :], cqn[:, c, s * 512:(s + 1) * 512], c == 0, c == 3,
                                    ['wqp', 'cqn'], [('ps', 6 + s)])
                    self.act(qa[:], self.psum[0:64, 4 * 512:4 * 512 + G], AF.Copy, [('ps', 4), ('ps', 5)], ['qa'])
                    self.act(qb[:], self.psum[0:64, 6 * 512:6 * 512 + G], AF.Copy, [('ps', 6), ('ps', 7)], ['qb'])
                    self.act(sq3[:], qa[:], AF.Square, ['qa'], ['qsq3'])
                    for s in range(NSB):
                        self.mm(self.ps(4 + s)[0:64, :], self.onesf[0:64, 0:64], sq3[:, s * 512:(s + 1) * 512], True, True, ['onesf', 'qsq3'], [('ps', 4 + s)])
                    self.rstd(rs3[:], self.psum[0:64, 4 * 512:4 * 512 + G], 64, [('ps', 4), ('ps', 5)], ['qrs3'])
                    self.rope_apply(qa[:], qb[:], rs3, gq, tabs, qr[:, h, :], ['qa', 'qb', 'qrs3', 'gq'], ('qr', h))
                P.barrier()
            nkeys = p0 + G
            nkb_all = nkeys // 128
            krT = self.sb(es, "krT", [64, S_ALL], BF16)
            self.dma('sp', krT[:, 0:nkeys], self.krot_s[:, 0:nkeys], ['krot_s'], ['krT'])
            knT = [self.sb(es, "knT%d" % i, [128, S_ALL], BF16) for i in range(2)]
            vT = [self.sb(es, "vT%d" % i, [128, 32, 128], BF16) for i in range(2)]
            pT = [self.sb(es, "pT%d" % i, [128, 512], BF16) for i in range(3)]
            rden = self.sb(es, "rden", [128, 512], F32)
            oo = [self.sb(es, "oo%d" % i, [128, 512], BF16) for i in range(2)]
            pT = pT + [self.sb(es, "pT3", [128, 512], BF16)]
            NR = 4
            LA = 2

            def load_head(h):
                hi = h % 2
                self.dma('sp', knT[hi][:, 0:nkeys], self.kn_s[h, :, 0:nkeys], ['kn_s'], [('knT', hi)])
                self.dma('sp', vT[hi][:, 0:nkb_all, :], self.v_s[0:nkeys, h * 128:(h + 1) * 128].rearrange("(kb p) d -> p kb d", p=128),
                         ['v_s'], [('vT', hi)])

            its = []
            itn = 0
            for h in range(8):
                for qs in range(NSB):
                    q0 = p0 + qs * 512
                    nkb = (q0 + 512) // 128
                    bO = 4 + (itn % 2) * 2
                    for kb in range(nkb):
                        j = kb - q0 // 128
                        c0 = max(0, j) * 128
                        its.append(dict(h=h, hi=h % 2, qs=qs, qo=qs * 512, kb=kb, nkb=nkb, j=j, c0=c0, n=512 - c0, bO=bO, bD=bO + 1,
                                        oi=itn % 2, first_of_head=(qs == 0 and kb == 0)))
                    itn += 1

            def emit_S(i):
                d = its[i]
                h, hi, kb, c0, n, qo = d['h'], d['hi'], d['kb'], d['c0'], d['n'], d['qo']
                if d['first_of_head'] and h == 0:
                    load_head(0)
                r = i % NR
                self.mm(self.ps(r, n, c0), knT[hi][:, kb * 128:(kb + 1) * 128], qn[:, h, qo + c0:qo + 512], True, False,
                        [('knT', hi), ('qn', h)], [('ps', r)])
                self.mm(self.ps(r, n, c0), krT[:, kb * 128:(kb + 1) * 128], qr[:, h, qo + c0:qo + 512], False, True,
                        ['krT', ('qr', h)], [('ps', r)])
                bias = self.c_cbias if kb < 16 else self.c_zero
                self.act(pT[r][:, c0:512], self.ps(r, n, c0), AF.Exp, [('ps', r), 'cst'], [('pT', r)], scale=scale, bias=bias)
                if d['j'] >= 0:
                    self.tt('pool', pT[r][:, c0:c0 + 128], pT[r][:, c0:c0 + 128], self.trib[:], ALU.mult, [('pT', r), 'trib'], [('pT', r)])

            def emit_PV(i):
                d = its[i]
                h, hi, kb, c0, n, qo, nkb = d['h'], d['hi'], d['kb'], d['c0'], d['n'], d['qo'], d['nkb']
                bO, bD = d['bO'], d['bD']
                r = i % NR
                if d['first_of_head'] and h + 1 < 8:
                    load_head(h + 1)
                self.mm(self.ps(bO, n, c0), vT[hi][:, kb, :], pT[r][:, c0:512], kb == 0, kb == nkb - 1, [('vT', hi), ('pT', r)], [('ps', bO)])
                self.mm(self.ps(bD, n, c0), self.onesb[:], pT[r][:, c0:512], kb == 0, kb == nkb - 1, ['onesb', ('pT', r)], [('ps', bD)])
                if kb == nkb - 1:
                    self.recip(rden[:], self.ps(bD), [('ps', bD)], ['rden'])
                    o = oo[d['oi']]; ok = ('oo', d['oi'])
                    self.tt('dve', o[:], self.ps(bO), rden[:], ALU.mult, [('ps', bO), 'rden'], [ok])
                    self.dma('sp', self.o_s[h * 128:(h + 1) * 128, o0 + qo:o0 + qo + 512], o[:], [ok], ['o_s'])

            for idx in range(len(its) + LA):
                if idx < len(its):
                    emit_S(idx)
                if idx - LA >= 0:
                    emit_PV(idx - LA)
            P.barrier()

    def stage_mlstm(self, l, W, pv, mcw, bif, wif, gi):
        P = self.P
        own = gi >= 2
        p0 = gi * G
        o0 = (gi - 2) * G
        with ExitStack() as es:
            qT = self.sb(es, "qT", [128, 8, G], BF16)
            kT = self.sb(es, "kT", [128, 8, G], BF16)
            kTM = self.sb(es, "kTM", [128, NT, 1024], BF16)
            vp = self.sb(es, "vp", [128, NT, 4, 257], BF16)
            gsc = self.sb(es, "gsc", [128, NT, 16], F32)
            with ExitStack() as eb:
                xcb = self.sb(eb, "xcb", [128, 8, G], BF16)
                xmbf = self.sb(eb, "xmbf", [128, 8, G], BF16)
                with ExitStack() as e1:
                    xmb = self.sb(e1, "xmb", [128, 8, 3 + G], F32)
                    xcr = [self.sb(e1, "xc%d" % i, [128, G], F32) for i in range(2)]
                    self.dma('sp', xmb[:], self.xm_s[:, p0:p0 + 3 + G].rearrange("(c p) t -> p c t", p=128), ['xm_s'], ['xmb'])
                    if gi == 2:
                        self.ts('dve', xmb[:, :, 0:3], xmb[:, :, 0:3], self.c_flag, None, ALU.mult, None, ['xmb', 'cst'], ['xmb'])
                    for c in range(8):
                        xc = xcr[c % 2]; xk = ('xc', c % 2)
                        self.ts('dve', xc[:], xmb[:, c, 0:G], mcw[:, c, 0:1], pv['mcb'][:, c:c + 1], ALU.mult, ALU.add, ['xmb', 'mcw', 'mcb'], [xk])
                        for j in range(1, 4):
                            self.stt('dve', xc[:], xmb[:, c, j:j + G], mcw[:, c, j:j + 1], xc[:], ALU.mult, ALU.add, ['xmb', 'mcw', xk], [xk])
                        self.act(xc[:], xc[:], AF.Silu, [xk], [xk])
                        self.cp('pool', xcb[:, c, :], xc[:], [xk], [('xcb', c)])
                        self.cp('pool', xmbf[:, c, :], xmb[:, c, 3:3 + G], ['xmb'], [('xmbf', c)])
                        if own:
                            self.dma('sp', self.xc_s[c * 128:(c + 1) * 128, o0:o0 + G], xc[:], [xk], ['xc_s'])
                    P.barrier()
                with ExitStack() as e2:
                    vTf = self.sb(e2, "vTf", [128, 8, G], BF16)
                    wm = self.sb(e2, "wm", [128, 3, 8, 256], BF16)
                    vf = [self.sb(e2, "vf%d" % i, [128, 1024], F32) for i in range(2)]
                    gt = self.sb(e2, "gt", [128, NT, 16], F32)
                    for i, nm in enumerate(('w_mq', 'w_mk', 'w_mv')):
                        self.wload(wm[:, i, :, :], W[nm][l].rearrange("h (c p) e -> p (h c) e", p=128), ('wm', i))
                    nb = 0
                    for (wi, srcb, skey, dst, dkey) in ((0, xcb, 'xcb', qT, 'qT'), (1, xcb, 'xcb', kT, 'kT'), (2, xmbf, 'xmbf', vTf, 'vTf')):
                        for hh in range(4):
                            for ec in range(2):
                                b = (nb % 4) * 2; nb += 1
                                for s in range(NSB):
                                    for dc in range(2):
                                        self.mm(self.ps(b + s), wm[:, wi, hh * 2 + dc, ec * 128:(ec + 1) * 128], srcb[:, hh * 2 + dc, s * 512:(s + 1) * 512],
                                                dc == 0, dc == 1, [('wm', wi), (skey, hh * 2 + dc)], [('ps', b + s)])
                                if nb % 2 == 0:
                                    self.act(dst[:, hh * 2 + ec, :], self.psum[:, b * 512:b * 512 + G], AF.Copy, [('ps', b), ('ps', b + 1)], [(dkey, hh * 2 + ec)])
                                else:
                                    self.cp('dve', dst[:, hh * 2 + ec, :], self.psum[:, b * 512:b * 512 + G], [('ps', b), ('ps', b + 1)], [(dkey, hh * 2 + ec)])
                    allq = [('qT', i) for i in range(8)]
                    allk = [('kT', i) for i in range(8)]
                    allv = [('vTf', i) for i in range(8)]
                    for t in range(NT):
                        tsl = slice(t * 128, (t + 1) * 128)
                        b = (t % 2) * 2
                        for hh in range(4):
                            for dc in range(2):
                                self.mm(self.ps(b + hh // 2, 256, (hh % 2) * 256), xcb[:, hh * 2 + dc, tsl], wm[:, 1, hh * 2 + dc, :], dc == 0, dc == 1,
                                        [('wm', 1), ('xcb', hh * 2 + dc)], [('ps', b + hh // 2)])
                        self.act(kTM[:, t, :], self.psum[:, b * 512:b * 512 + 1024], AF.Copy, [('ps', b), ('ps', b + 1)], [('kTM', t)])
                        b2 = 4
                        for hh in range(4):
                            for dc in range(2):
                                self.mm(self.ps(b2 + hh // 2, 256, (hh % 2) * 256), xmbf[:, hh * 2 + dc, tsl], wm[:, 2, hh * 2 + dc, :], dc == 0, dc == 1,
                                        [('wm', 2), ('xmbf', hh * 2 + dc)], [('ps', b2 + hh // 2)])
                        vfi = vf[t % 2]; vk = ('vf', t % 2)
                        self.cp('dve', vfi[:], self.psum[:, b2 * 512:b2 * 512 + 1024], [('ps', b2), ('ps', b2 + 1)], [vk])
                        bg = 6 + (t % 2)
                        gk_ = ('ps', bg)
                        for i, (srcT, keys) in enumerate(((qT, allq), (kT, allk), (vTf, allv))):
                            for c in range(8):
                                self.mm(self.ps(bg, 8, 0), srcT[:, c, tsl], wif[:, i * 8 + c, :], i == 0 and c == 0, i == 2 and c == 7,
                                        ['wif', keys[c]], [gk_])
                        g = gt[:, t, :]
                        gkey = ('gt', t)
                        self.tt('dve', g[:, 0:8], self.ps(bg, 8, 0), bif[:], ALU.add, [gk_, 'bif'], [gkey])
                        self.act(g[:, 8:12], g[:, 4:8], AF.Exp, [gkey], [gkey], scale=-1.0)
                        self.act(g[:, 8:12], g[:, 8:12], AF.Ln, [gkey], [gkey], bias=1.0)
                        self.ts('dve', g[:, 8:12], g[:, 8:12], -1.0, None, ALU.mult, None, [gkey], [gkey])
                        self.mm(self.ps(bg, 4, 16), self.trif, g[:, 8:12], True, True, ['cst', gkey], [gk_])
                        self.mm(self.ps(bg, 4, 32), self.onesf[:], g[:, 8:12], True, True, ['onesf', gkey], [gk_])
                        sc = gsc[:, t, :]
                        skey = ('gsc', t)
                        self.tt('dve', g[:, 12:16], g[:, 0:4], self.ps(bg, 4, 16), ALU.subtract, [gkey, gk_], [gkey])
                        self.act(sc[:, 0:4], g[:, 12:16], AF.Exp, [gkey], [skey])
                        self.ts('dve', sc[:, 0:4], sc[:, 0:4], 1.0 / 16, None, ALU.mult, None, [skey], [skey])
                        self.act(sc[:, 4:8], self.ps(bg, 4, 16), AF.Exp, [gk_], [skey])
                        self.act(sc[:, 8:12], self.ps(bg, 4, 32), AF.Exp, [gk_], [skey])
                        for hh in range(4):
                            self.ts('dve', vp[:, t, hh, 0:256], vfi[:, hh * 256:(hh + 1) * 256], sc[:, hh:hh + 1], None, ALU.mult, None, [vk, skey], [('vp', t)])
                        self.cp('dve', vp[:, t, :, 256:257], sc[:, 0:4].rearrange("p (h o) -> p h o", o=1), [skey], [('vp', t)])
                    P.barrier()
            with ExitStack() as e3:
                if own:
                    sT = [self.sb(e3, "sT%d" % i, [128, 128], BF16) for i in range(2)]
                    nd = [self.sb(e3, "nd%d" % i, [128, 257], F32) for i in range(2)]
                    hst = self.sb(e3, "hst", [128, NT * 4, 8], F32)
                    hh_ = [self.sb(e3, "hh%d" % i, [128, 256], F32) for i in range(2)]
                    junk = self.sb(e3, "mjunk", [128, 256], F32)
                    zt = self.sb(e3, "zt", [128, 8, G], F32)
                    xcs = self.sb(e3, "xcs", [128, 8, G], F32)
                    hno = self.sb(e3, "hno", [128, 8, G], BF16)
                    self.dma('sp', zt[:], self.z_s[:, o0:o0 + G].rearrange("(c p) t -> p c t", p=128), ['z_s'], ['zt'])
                    self.dma('sp', xcs[:], self.xc_s[:, o0:o0 + G].rearrange("(c p) t -> p c t", p=128), ['xc_s'], [('xcs', c) for c in range(8)])
                    for c in range(8):
                        self.ts('pool', xcs[:, c, :], xcs[:, c, :], pv['skp'][:, c:c + 1], None, ALU.mult, None, [('xcs', c), 'skp'], [('xcs', c)])
                    self.memset('dve', hst[:], 0.0, ['hst'])
                it = 0
                pending_tail = None
                for t in range(NT):
                    tsl = slice(t * 128, (t + 1) * 128)
                    sc = gsc[:, t, :]
                    skey = ('gsc', t)
                    for hh in range(4):
                        i2 = it % 2; it += 1
                        if own:
                            bS = 0
                            for dc in range(2):
                                self.mm(self.ps(bS, 128, i2 * 128), kT[:, hh * 2 + dc, tsl], qT[:, hh * 2 + dc, tsl], dc == 0, dc == 1,
                                        [('kT', hh * 2 + dc), ('qT', hh * 2 + dc)], [('psS', i2)])
                            self.tt('dve', sT[i2][:], self.ps(bS, 128, i2 * 128), self.trif, ALU.mult, [('psS', i2), 'cst'], [('sT', i2)])
                            bN = 1 + i2
                            for dc in range(2):
                                self.mm(self.ps(bN, 257), qT[:, hh * 2 + dc, tsl], self.Cbf[:, hh * 2 + dc, :], dc == 0, False,
                                        [('qT', hh * 2 + dc), ('Cbf', hh)], [('ps', bN)])
                            self.mm(self.ps(bN, 257), sT[i2][:], vp[:, t, hh, :], False, True, [('sT', i2), ('vp', t)], [('ps', bN)])
                            n_ = nd[i2]; nk = ('nd', i2)
                            self.act(n_[:], self.ps(bN, 257), AF.Copy, [('ps', bN), skey], [nk], scale=sc[:, 4 + hh:5 + hh])
                            hs = hst[:, t * 4 + hh, :]
                            hk = ('hst', t * 4 + hh)
                            self.act(hs[:, 0:1], n_[:, 256:257], AF.Abs, [nk, 'hst'], [hk])
                            self.ts('dve', hs[:, 0:1], hs[:, 0:1], 1.0, None, ALU.max, None, [hk], [hk])
                            self.recip(hs[:, 1:2], hs[:, 0:1], [hk], [hk])
                            hv = hh_[i2]; hvk = ('hh', i2)
                            self.act(hv[:], n_[:, 0:256], AF.Copy, [nk, hk], [hvk, hk], scale=hs[:, 1:2], accum_out=hs[:, 2:3])
                            self.act(junk[:], hv[:], AF.Square, [hvk, hk], ['mjunk', hk], accum_out=hs[:, 3:4])
                            self.ts('dve', hs[:, 4:5], hs[:, 2:3], 1.0 / 256, None, ALU.mult, None, [hk], [hk])
                            self.tt('dve', hs[:, 5:6], hs[:, 4:5], hs[:, 4:5], ALU.mult, [hk], [hk])
                            self.stt('dve', hs[:, 6:7], hs[:, 3:4], 1.0 / 256, hs[:, 5:6], ALU.mult, ALU.subtract, [hk], [hk])
                            self.ts('dve', hs[:, 6:7], hs[:, 6:7], EPS, None, ALU.add, None, [hk], [hk])
                            self.act(hs[:, 6:7], hs[:, 6:7], AF.Sqrt, [hk], [hk])
                            self.recip(hs[:, 7:8], hs[:, 6:7], [hk], [hk])
                            self.ts('dve', hv[:], hv[:], hs[:, 4:5], hs[:, 7:8], ALU.subtract, ALU.mult, [hvk, hk], [hvk])
                            def tail(hv=hv, hvk=hvk, i2=i2, hh=hh, tsl=tsl):
                                bT = 7
                                for dc in range(2):
                                    self.tr(self.ps(bT, 128, (i2 * 2 + dc) * 128), hv[:, dc * 128:(dc + 1) * 128], self.idf, [hvk, 'cst'], [('psT', i2)])
                                for dc in range(2):
                                    c = hh * 2 + dc
                                    self.stt('dve', xcs[:, c, tsl], self.ps(bT, 128, (i2 * 2 + dc) * 128), pv['gng'][:, c:c + 1], xcs[:, c, tsl],
                                             ALU.mult, ALU.add, [('psT', i2), 'gng', ('xcs', c)], [('xcs', c)])
                                    self.tt('pool', hno[:, c, tsl], xcs[:, c, tsl], zt[:, c, tsl], ALU.mult, [('xcs', c), 'zt'], [('hno', c)])
                            new_tail = tail
                        for dc in range(2):
                            c = hh * 2 + dc
                            bU = 3 + (it % 2) * 2 + dc
                            self.mm(self.ps(bU, 257), kTM[:, t, hh * 256 + dc * 128:hh * 256 + (dc + 1) * 128], vp[:, t, hh, :], True, True,
                                    [('kTM', t), ('vp', t)], [('ps', bU)])
                            self.ts('dve', self.Cst[:, c, :], self.Cst[:, c, :], sc[:, 8 + hh:9 + hh], None, ALU.mult, None, [('Cst', hh), skey], [('Cst', hh)])
                            self.stt('dve', self.Cst[:, c, :], self.ps(bU, 257), sc[:, 8 + hh:9 + hh], self.Cst[:, c, :], ALU.mult, ALU.add,
                                     [('ps', bU), ('Cst', hh), skey], [('Cst', hh)])
                            self.act(self.Cbf[:, c, :], self.Cst[:, c, :], AF.Copy, [('Cst', hh)], [('Cbf', hh)])
                        if own:
                            if pending_tail is not None:
                                pending_tail()
                            pending_tail = new_tail
                if own and pending_tail is not None:
                    pending_tail()
                if gi == 1:
                    for hh in range(4):
                        for dc in range(2):
                            c = hh * 2 + dc
                            self.ts('dve', self.Cst[:, c, :], self.Cst[:, c, :], self.c_flag, None, ALU.mult, None, [('Cst', hh), 'cst'], [('Cst', hh)])
                            self.act(self.Cbf[:, c, :], self.Cst[:, c, :], AF.Copy, [('Cst', hh)], [('Cbf', hh)])
                if own:
                    for c in range(8):
                        self.dma('sp', self.hn_s[c * 128:(c + 1) * 128, o0:o0 + G], hno[:, c, :], [('hno', c)], ['hn_s'])
                P.barrier()

    def stage_merge(self, l, W, pv, gi, hT, x_src):
        P = self.P
        o0 = (gi - 2) * G
        wv = W['w_in'][l].rearrange("(c p) f -> p c f", p=128)
        wouts = [W[n][l].rearrange("(c p) f -> p c f", p=128) for n in ('w_conv_out', 'w_mla_out', 'w_mlstm_out')]
        srcs = [self.cb_s, self.o_s, self.hn_s]
        with ExitStack() as es:
            mg = self.sb(es, "mg", [128, 16, G], BF16)
            with ExitStack() as e2:
                br = [self.sb(e2, "br%d" % j, [128, 8, G], BF16) for j in range(3)]
                for j in range(3):
                    self.dma('sp', br[j][:], srcs[j][:, o0:o0 + G].rearrange("(c p) t -> p c t", p=128), [('cb_s', 'o_s', 'hn_s')[j]], [('br', j)])
                wg = [self.sb(e2, "wg%d" % i, [128, 16, 256], BF16) for i in range(4)]
                wy = [self.sb(e2, "wy%d" % i, [128, 8, 256], BF16) for i in range(4)]
                sg = [self.sb(e2, "sg%d" % i, [128, 512], F32) for i in range(2)]
                macc = [self.sb(e2, "macc%d" % i, [128, 512], F32) for i in range(2)]
                tmp = [self.sb(e2, "mtmp%d" % i, [128, 512], F32) for i in range(2)]
                nw = 0
                nb = 0
                ns = 0
                for d0 in range(0, DM, 256):
                    wgs = []
                    for j in range(3):
                        i = nw % 4; nw += 1
                        self.wload(wg[i][:], wv[:, :, 4928 + j * DM + d0:4928 + j * DM + d0 + 256], ('wg', i))
                        self.wload(wy[i][:], wouts[j][:, :, d0:d0 + 256], ('wy', i))
                        wgs.append(i)
                    for dd in range(2):
                        dc = d0 // 128 + dd
                        for s in range(NSB):
                            ssl = slice(s * 512, (s + 1) * 512)
                            ma = macc[ns % 2]; mk = ('macc', ns % 2); ns += 1
                            for j in range(3):
                                i = wgs[j]
                                bG = (nb % 4) * 2; bY = bG + 1; nb += 1
                                for k in range(16):
                                    self.mm(self.ps(bG), wg[i][:, k, dd * 128:(dd + 1) * 128], hT[:, k, ssl], k == 0, k == 15,
                                            [('wg', i), ('hT', s)], [('ps', bG)])
                                for k in range(8):
                                    self.mm(self.ps(bY), wy[i][:, k, dd * 128:(dd + 1) * 128], br[j][:, k, ssl], k == 0, k == 7,
                                            [('wy', i), ('br', j)], [('ps', bY)])
                                sgi = sg[nb % 2]; sk = ('sg', nb % 2)
                                self.act(sgi[:], self.ps(bG), AF.Sigmoid, [('ps', bG), 'bgate'], [sk], bias=pv['bgate'][:, j * 16 + dc:j * 16 + dc + 1])
                                if j == 0:
                                    self.tt('dve', ma[:], sgi[:], self.ps(bY), ALU.mult, [sk, ('ps', bY)], [mk])
                                else:
                                    tm = tmp[j % 2]; tk = ('mtmp', j % 2)
                                    self.tt('dve', tm[:], sgi[:], self.ps(bY), ALU.mult, [sk, ('ps', bY)], [tk])
                                    if j == 1:
                                        self.tt('dve', ma[:], ma[:], tm[:], ALU.add, [mk, tk], [mk])
                                    else:
                                        self.tt('dve', mg[:, dc, ssl], ma[:], tm[:], ALU.add, [mk, tk], [('mg', s)])
                P.barrier()
            self.resid_proj(es, mg, 'mg', 16, W['w_mix_out'][l], x_src, o0, self.x1_s, o0, G)
            P.barrier()

    def resid_proj(self, es, aT, akey, nk, w_d, x_src, xr0, x_dst, dr0, ntok, kchunk=16):
        wv = w_d.rearrange("(c p) f -> p c f", p=128)
        nkb = (nk + kchunk - 1) // kchunk
        wb = [self.sb(es, "rw%d" % i, [128, kchunk, 512], BF16) for i in range(3)]
        ntl = ntok // 128
        assert ntl <= 8
        xt = [self.sb(es, "rx%d" % i, [128, 512], F32) for i in range(ntl)]
        nw = 0
        for fc in range(4):
            fsl = slice(fc * 512, (fc + 1) * 512)
            for t in range(ntl):
                self.dma('sp', xt[t][:], x_src[xr0 + t * 128:xr0 + (t + 1) * 128, fsl], (), [('rx', t)])
            for kb in range(nkb):
                k0 = kb * kchunk
                kn = min(kchunk, nk - k0)
                i = nw % 3; nw += 1
                self.wload(wb[i][:, 0:kn, :], wv[:, k0:k0 + kn, fsl], ('rw', i))
                for t in range(ntl):
                    for k in range(kn):
                        self.mm(self.ps(t), aT[:, k0 + k, t * 128:(t + 1) * 128], wb[i][:, k, :], (k0 + k) == 0, (k0 + k) == nk - 1,
                                [('rw', i), (akey, t // 4)], [('ps', t)])
            for t in range(ntl):
                self.tt('dve', xt[t][:], xt[t][:], self.ps(t), ALU.add, [('rx', t), ('ps', t)], [('rx', t)])
                self.dma('sp', x_dst[dr0 + t * 128:dr0 + (t + 1) * 128, fsl], xt[t][:], [('rx', t)], ['xdst'])

    def stage_xattn(self, l, W, pv, kx, vx, gi):
        P = self.P
        o0 = (gi - 2) * G
        scale = 128.0 ** -0.5
        with ExitStack() as es:
            ox = self.sb(es, "ox", [128, 4, G], BF16)
            with ExitStack() as e1:
                hT = self.sb(e1, "hTx", [128, 16, G], BF16)
                with ExitStack() as e2:
                    self.norm_transpose(e2, self.x1_s, o0, NT, W['norm_x'][l], hT, 'hTx')
                    P.barrier()
                wq = self.sb(e1, "wxq", [128, 16, 512], BF16)
                self.wload(wq[:], W['w_xq'][l].rearrange("(c p) f -> p c f", p=128), 'wxq')
                qx = self.sb(e1, "qx", [128, 4, G], BF16)
                qf = self.sb(e1, "xqf", [128, G], F32)
                sq = self.sb(e1, "xsq", [128, G], F32)
                rs = self.sb(e1, "xrs", [128, G], F32)
                for h in range(4):
                    b = (h % 2) * 2
                    for s in range(NSB):
                        for k in range(16):
                            self.mm(self.ps(b + s), wq[:, k, h * 128:(h + 1) * 128], hT[:, k, s * 512:(s + 1) * 512], k == 0, k == 15,
                                    ['wxq', ('hTx', s)], [('ps', b + s)])
                    rb = [('ps', b), ('ps', b + 1)]
                    self.act(qf[:], self.psum[:, b * 512:b * 512 + G], AF.Copy, rb, ['xqf'])
                    self.act(sq[:], self.psum[:, b * 512:b * 512 + G], AF.Square, rb, ['xsq'])
                    for s in range(NSB):
                        self.mm(self.ps(4 + s), self.onesf[:], sq[:, s * 512:(s + 1) * 512], True, True, ['onesf', 'xsq'], [('ps', 4 + s)])
                    self.rstd(rs[:], self.psum[:, 4 * 512:4 * 512 + G], 128, [('ps', 4), ('ps', 5)], ['xrs'])
                    self.stt('dve', qx[:, h, :], qf[:], pv['xgq'][:, 0:1], rs[:], ALU.mult, ALU.mult, ['xqf', 'xrs', 'xgq'], [('qx', h)])
                pT = [self.sb(e1, "xpT%d" % i, [128, 512], BF16) for i in range(3)]
                rden = self.sb(e1, "xrden", [128, 512], F32)
                npt = 0
                it = 0
                for h in range(4):
                    for s in range(NSB):
                        ssl = slice(s * 512, (s + 1) * 512)
                        bO = 4 + (it % 2) * 2; bD = bO + 1; it += 1
                        for mc in range(2):
                            bS = npt % 3
                            pi = npt % 3; npt += 1
                            self.mm(self.ps(bS), kx[:, h, mc * 128:(mc + 1) * 128], qx[:, h, ssl], True, True, ['kx', ('qx', h)], [('ps', bS)])
                            self.act(pT[pi][:], self.ps(bS), AF.Exp, [('ps', bS)], [('xpT', pi)], scale=scale)
                            self.mm(self.ps(bO), vx[:, mc, h * 128:(h + 1) * 128], pT[pi][:], mc == 0, mc == 1, ['vx', ('xpT', pi)], [('ps', bO)])
                            self.mm(self.ps(bD), self.onesb[:], pT[pi][:], mc == 0, mc == 1, ['onesb', ('xpT', pi)], [('ps', bD)])
                        self.recip(rden[:], self.ps(bD), [('ps', bD)], ['xrden'])
                        self.tt('dve', ox[:, h, ssl], self.ps(bO), rden[:], ALU.mult, [('ps', bO), 'xrden'], [('ox', s)])
                P.barrier()
            self.resid_proj(es, ox, 'ox', 4, W['w_xo'][l], self.x1_s, o0, self.x2_s, o0, G)
            P.barrier()

    def stage_ffn(self, l, W, gi, x_out):
        P = self.P
        NJ = FFN_H // 128
        wv = W['w_ffn_in'][l].rearrange("(c p) f -> p c f", p=128)
        for half in range(G // 512):
            o0 = (gi - 2) * G + half * 512
            with ExitStack() as es:
                aT = self.sb(es, "aT", [128, NJ, 512], BF16)
                with ExitStack() as e1:
                    hT = self.sb(e1, "hTf", [128, 16, 512], BF16)
                    with ExitStack() as e2:
                        self.norm_transpose(e2, self.x2_s, o0, 4, W['norm_ffn'][l], hT, 'hTf')
                        P.barrier()
                    wg = [self.sb(e1, "fwg%d" % i, [128, 16, 512], BF16) for i in range(2)]
                    wu = [self.sb(e1, "fwu%d" % i, [128, 16, 512], BF16) for i in range(2)]
                    sg = [self.sb(e1, "fsg%d" % i, [128, 512], F32) for i in range(2)]
                    nb = 0
                    for jb in range(NJ // 4):
                        i = jb % 2
                        self.wload(wg[i][:], wv[:, :, jb * 512:(jb + 1) * 512], ('fwg', i))
                        self.wload(wu[i][:], wv[:, :, FFN_H + jb * 512:FFN_H + (jb + 1) * 512], ('fwu', i))
                        for jj in range(4):
                            j = jb * 4 + jj
                            bG = (nb % 4) * 2; bU = bG + 1; nb += 1
                            for k in range(16):
                                self.mm(self.ps(bG), wg[i][:, k, jj * 128:(jj + 1) * 128], hT[:, k, :], k == 0, k == 15, [('fwg', i), ('hTf', 0)], [('ps', bG)])
                            for k in range(16):
                                self.mm(self.ps(bU), wu[i][:, k, jj * 128:(jj + 1) * 128], hT[:, k, :], k == 0, k == 15, [('fwu', i), ('hTf', 0)], [('ps', bU)])
                            s = sg[nb % 2]; sk = ('fsg', nb % 2)
                            self.act(s[:], self.ps(bG), AF.Silu, [('ps', bG)], [sk])
                            self.tt('dve', aT[:, j, :], s[:], self.ps(bU), ALU.mult, [sk, ('ps', bU)], [('aT', 0)])
                    P.barrier()
                self.resid_proj(es, aT, 'aT', NJ, W['w_ffn_out'][l], self.x2_s, o0, x_out, o0, 512, kchunk=22)
                P.barrier()


WNAMES = ['norm_mix', 'w_in', 'b_gate', 'conv_dw', 'conv_dw_b', 'conv_ln_g', 'conv_ln_b', 'w_conv_out',
          'mla_q_norm', 'mla_kv_norm', 'w_q_up', 'w_kv_up', 'mla_g_q', 'mla_g_k', 'w_mla_out',
          'mlstm_conv_w', 'mlstm_conv_b', 'w_mq', 'w_mk', 'w_mv', 'w_if', 'b_if', 'mlstm_gn_g', 'mlstm_skip',
          'w_mlstm_out', 'w_mix_out', 'norm_x', 'norm_mem', 'w_xq', 'w_xkv', 'xattn_g_q', 'xattn_g_k', 'w_xo',
          'norm_ffn', 'w_ffn_in', 'w_ffn_out']


def build_program(shapes, layers, debug=()):
    nc = bass.Bass("TRN2", target_bir_lowering=False)
    B = Builder(nc, layers, debug)
    W = {}
    for n in WNAMES:
        shp = [len(layers)] + list(shapes[n][1:])
        W[n] = nc.dram_tensor(n, shp, F32, kind="ExternalInput").ap()
    x_own = nc.dram_tensor("x_own", [S_OWN, DM], F32, kind="ExternalInput").ap()
    x_ctx = nc.dram_tensor("x_ctx", [S_OWN, DM], F32, kind="ExternalInput").ap()
    y = nc.dram_tensor("y", [S_OWN, DM], F32, kind="ExternalOutput").ap()
    B.setup()
    if len(layers) == 1:
        B.layer(0, W, x_own, x_ctx, y)
    else:
        xl0 = nc.dram_tensor("xl0", [S_OWN, DM], F32).ap()
        ctx1 = nc.dram_tensor("ctx1", [S_OWN, DM], F32).ap()
        CH = 256
        bin_ = [nc.dram_tensor("ccin%d" % i, [CH, DM], F32).ap() for i in range(2)]
        bout = [nc.dram_tensor("ccout%d" % i, [2 * CH, DM], F32).ap() for i in range(2)]
        B.layer(0, W, x_own, x_ctx, xl0)
        rg = [[0, 1], [2, 3], [4, 5], [6, 7]]
        for i in range(S_OWN // CH):
            b = i % 2
            B.P.add('sp', lambda e, i=i, b=b: e.dma_start(out=bin_[b], in_=xl0[i * CH:(i + 1) * CH, :]), (), [('ccin', b)], dma=True)
            B.P.add('pool', lambda e, b=b: e.collective_compute("AllGather", ALU.bypass, replica_groups=rg, ins=[bin_[b]], outs=[bout[b]]),
                    [('ccin', b)], [('ccout', b)], dma=True, cc=True)
            B.P.add('sp', lambda e, i=i, b=b: e.dma_start(out=ctx1[i * CH:(i + 1) * CH, :], in_=bout[b][0:CH, :]), [('ccout', b)], ['ctx1'], dma=True)
        B.P.barrier()
        B.layer(1, W, xl0, ctx1, y)
    B.P.barrier()
    es = ExitStack()
    B.P.emit(nc, es)
    es.close()
    B.gs.close()
    return nc


def make_cst(half):
    c = np.zeros((128, 512), np.float32)
    c[:, 0:128] = np.eye(128, dtype=np.float32)
    r = np.arange(128)
    c[:, 128:256] = (r[None, :] >= r[:, None]).astype(np.float32)
    inv = (10000.0 ** (-(np.arange(32, dtype=np.float64)) / 32.0)) / (2 * math.pi)
    c[0:64, 256] = np.concatenate([inv, inv]).astype(np.float32)
    c[0:32, 257] = -1.0
    c[32:64, 257] = 1.0
    c[:, 258] = 1.0 if half == 1 else 0.0
    c[:, 259] = 0.0 if half == 1 else -30000.0
    c[:, 260] = 0.0
    return c


_PROG_CACHE = {}


def run_layer(l, inputs, x_cur, debug=(), cores=None):
    shapes = {n: inputs[n].shape for n in WNAMES}
    key = ('layer', tuple(debug))
    if key not in _PROG_CACHE:
        _PROG_CACHE[key] = build_program(shapes, [0], debug)
    nc = _PROG_CACHE[key]
    if cores is None:
        cores = list(range(8))
    wl = {n: np.ascontiguousarray(inputs[n][l:l + 1]) for n in WNAMES}
    in_maps = []
    for c in cores:
        b, half = c // 2, c % 2
        m = dict(wl)
        m['x_own'] = np.ascontiguousarray(x_cur[b, half * S_OWN:(half + 1) * S_OWN])
        m['x_ctx'] = np.ascontiguousarray(x_cur[b, 0:S_OWN])
        pos = inputs['positions'][b].astype(np.int32)
        if half == 1:
            pa = pos
        else:
            pa = np.concatenate([pos[0:S_OWN], pos[0:S_OWN]])
        m['pos'] = np.ascontiguousarray(pa[None, :])
        m['mem'] = np.ascontiguousarray(inputs['mem'][b])
        m['cst'] = make_cst(half)
        in_maps.append(m)
    res = run_bass_kernel_spmd(nc, in_maps, core_ids=list(range(len(cores))))
    return res.results


def run_fused(inputs, cores=None):
    shapes = {n: inputs[n].shape for n in WNAMES}
    key = ('fused',)
    if key not in _PROG_CACHE:
        _PROG_CACHE[key] = build_program(shapes, [0, 1])
    nc = _PROG_CACHE[key]
    if cores is None:
        cores = list(range(8))
    wl = {n: np.ascontiguousarray(inputs[n], dtype=np.float32) for n in WNAMES}
    x = inputs['x']
    in_maps = []
    for c in cores:
        b, half = c // 2, c % 2
        m = dict(wl)
        m['x_own'] = np.ascontiguousarray(x[b, half * S_OWN:(half + 1) * S_OWN], dtype=np.float32)
        m['x_ctx'] = np.ascontiguousarray(x[b, 0:S_OWN], dtype=np.float32)
        pos = inputs['positions'][b].astype(np.int32)
        if half == 1:
            pa = pos
        else:
            pa = np.concatenate([pos[0:S_OWN], pos[0:S_OWN]])
        m['pos'] = np.ascontiguousarray(pa[None, :])
        m['mem'] = np.ascontiguousarray(inputs['mem'][b], dtype=np.float32)
        m['cst'] = make_cst(half)
        in_maps.append(m)
    res = run_bass_kernel_spmd(nc, in_maps, core_ids=list(range(len(cores))))
    return res.results


def kernel(**inputs):
    inputs = {k: np.asarray(v) for k, v in inputs.items()}
    outs = run_fused(inputs)
    x = inputs['x']
    out = np.empty(x.shape, np.float32)
    for c in range(8):
        b, half = c // 2, c % 2
        out[b, half * S_OWN:(half + 1) * S_OWN] = outs[c]['y']
    return out
```

```python
import math
from contextlib import ExitStack

import numpy as np
import concourse.bass as bass
import concourse.mybir as mybir
from concourse.bass_utils import run_bass_kernel_spmd

F32 = mybir.dt.float32
BF16 = mybir.dt.bfloat16
I32 = mybir.dt.int32
AF = mybir.ActivationFunctionType
ALU = mybir.AluOpType

ENGS = ['pe', 'act', 'dve', 'pool', 'sp']

DM = 2048
S_OWN = 2048
S_ALL = 4096
G = 1024
NT = G // 128
NSB = G // 512
EPS = 1e-6
FFN_H = 5632
IN_COLS = 11072
DEPTH = 2


class Op:
    __slots__ = ('eng', 'fn', 'dma', 'deps_c', 'deps_d', 'idx', 'signal', 'count', 'sem', 'val', 'cc')


class Prog:
    def __init__(self):
        self.ops = {e: [] for e in ENGS}
        self.last_w = {}
        self.rd_c = {}
        self.rd_d = {}

    def add(self, eng, fn, reads=(), writes=(), dma=False, cc=False):
        o = Op()
        o.eng = eng; o.fn = fn; o.dma = dma; o.cc = cc
        o.idx = len(self.ops[eng]); o.signal = False
        o.count = 0; o.sem = None; o.val = 0
        dc = {}
        dd = set()

        def need(p):
            if p.dma:
                dd.add(p)
            else:
                if p.eng == 'pe' and eng == 'pe' and not dma:
                    return
                if dc.get(p.eng, -1) < p.idx:
                    dc[p.eng] = p.idx
        for k in reads:
            p = self.last_w.get(k)
            if p is not None:
                need(p)
        for k in writes:
            p = self.last_w.get(k)
            if p is not None:
                need(p)
            for (pe, pidx) in self.rd_c.get(k, {}).items():
                need(self.ops[pe][pidx])
            for r in self.rd_d.get(k, ()):
                need(r)
        o.deps_c = dc
        o.deps_d = dd
        self.ops[eng].append(o)
        for k in reads:
            if dma:
                self.rd_d.setdefault(k, set()).add(o)
            else:
                self.rd_c.setdefault(k, {})[eng] = o.idx
        for k in writes:
            self.last_w[k] = o
            self.rd_c[k] = {}
            self.rd_d[k] = set()
        return o

    def barrier(self):
        o = Op()
        o.eng = 'sp'; o.fn = (lambda e: e.nop()); o.dma = False; o.cc = False
        o.idx = len(self.ops['sp']); o.signal = False
        o.count = 0; o.sem = None; o.val = 0
        o.deps_c = {e: len(self.ops[e]) - 1 for e in ENGS if len(self.ops[e]) > 0}
        start = getattr(self, '_bar_idx', {e: 0 for e in ENGS})
        dd = set()
        for e in ENGS:
            for p in self.ops[e][start[e]:]:
                if p.dma:
                    dd.add(p)
        o.deps_d = dd
        self.ops['sp'].append(o)
        for e in ENGS:
            if e == 'sp':
                continue
            q = Op()
            q.eng = e; q.fn = (lambda en: en.nop()); q.dma = False; q.cc = False
            q.idx = len(self.ops[e]); q.signal = False
            q.count = 0; q.sem = None; q.val = 0
            q.deps_c = {'sp': o.idx}
            q.deps_d = set()
            self.ops[e].append(q)
        self._bar_idx = {e: len(self.ops[e]) for e in ENGS}
        self.last_w = {}
        self.rd_c = {}
        self.rd_d = {}

    def emit(self, nc, es, npool=20):
        ops = self.ops
        for e in ENGS:
            for o in ops[e]:
                for (pe, pidx) in o.deps_c.items():
                    ops[pe][pidx].signal = True
        sems = {e: es.enter_context(nc.semaphore("s_" + e)) for e in ENGS}
        for e in ENGS:
            c = 0
            for o in ops[e]:
                if (not o.dma) and o.signal:
                    c += 1
                    o.count = c
        for e in ENGS:
            nd = sum(1 for o in ops[e] if o.dma)
            if nd == 0:
                continue
            n = min(npool, nd)
            pool = [es.enter_context(nc.semaphore("d_%s_%d" % (e, i))) for i in range(n)]
            i = 0
            ccsem = None
            ncc = 0
            for o in ops[e]:
                if o.dma and o.cc:
                    if ccsem is None:
                        ccsem = es.enter_context(nc.semaphore("cc_%s" % e))
                    ncc += 1
                    o.sem = ccsem
                    o.val = ncc
                elif o.dma:
                    o.sem = pool[i % n]
                    o.val = 16 * (i // n + 1)
                    i += 1
        block = es.enter_context(nc.Block())

        def run(engname, eng):
            waited = {}
            for o in ops[engname]:
                waits = []
                for (pe, pidx) in o.deps_c.items():
                    waits.append((sems[pe], ops[pe][pidx].count))
                for d in o.deps_d:
                    waits.append((d.sem, d.val))
                if o.dma and o.cc and o.val > 1:
                    waits.append((o.sem, o.val - 1))
                elif o.dma and (not o.cc) and o.val > 16:
                    waits.append((o.sem, o.val - 16))
                waits.sort(key=lambda sv: -sv[1])
                for (s, v) in waits:
                    key = id(s)
                    if waited.get(key, 0) >= v:
                        continue
                    eng.wait_ge(s, v)
                    waited[key] = v
                ins = o.fn(eng)
                if o.dma and o.cc:
                    ins.then_inc(o.sem)
                elif o.dma:
                    ins.then_inc(o.sem, 16)
                elif o.signal:
                    ins.then_inc(sems[engname], 1)

        @block.sync
        def _(e):
            run('sp', e)

        @block.tensor
        def _(e):
            run('pe', e)

        @block.scalar
        def _(e):
            run('act', e)

        @block.vector
        def _(e):
            run('dve', e)

        @block.gpsimd
        def _(e):
            run('pool', e)


class Builder:
    def __init__(self, nc, layers, debug=()):
        self.nc = nc
        self.P = Prog()
        self.layers = layers
        self.debug = set(debug)
        self.uid = 0
        self.gs = ExitStack()

    def dma(self, eng, out, in_, r, w, **kw):
        self.P.add(eng, lambda e: e.dma_start(out=out, in_=in_, **kw), r, w, dma=True)

    def act(self, out, in_, func, r, w, **kw):
        self.P.add('act', lambda e: e.activation(out=out, in_=in_, func=func, **kw), r, w)

    def mm(self, out, lhsT, rhs, start, stop, r, w):
        self.P.add('pe', lambda e: e.matmul(out, lhsT=lhsT, rhs=rhs, start=start, stop=stop), r, w)

    def tr(self, out, in_, ident, r, w):
        self.P.add('pe', lambda e: e.transpose(out=out, in_=in_, identity=ident), r, w)

    def tt(self, eng, out, in0, in1, op, r, w):
        self.P.add(eng, lambda e: e.tensor_tensor(out=out, in0=in0, in1=in1, op=op), r, w)

    def ts(self, eng, out, in0, s1, s2, op0, op1, r, w):
        if op1 is None:
            self.P.add(eng, lambda e: e.tensor_scalar(out=out, in0=in0, scalar1=s1, scalar2=None, op0=op0), r, w)
        else:
            self.P.add(eng, lambda e: e.tensor_scalar(out=out, in0=in0, scalar1=s1, scalar2=s2, op0=op0, op1=op1), r, w)

    def stt(self, eng, out, in0, scalar, in1, op0, op1, r, w):
        self.P.add(eng, lambda e: e.scalar_tensor_tensor(out=out, in0=in0, scalar=scalar, in1=in1, op0=op0, op1=op1), r, w)

    def cp(self, eng, out, in_, r, w):
        self.P.add(eng, lambda e: e.tensor_copy(out=out, in_=in_), r, w)

    def memset(self, eng, ap, val, w):
        self.P.add(eng, lambda e: e.memset(ap, val), (), w)

    def recip(self, out, in_, r, w):
        self.P.add('dve', lambda e: e.reciprocal(out=out, in_=in_), r, w)

    def sb(self, es, name, shape, dt):
        self.uid += 1
        return es.enter_context(self.nc.sbuf_tensor("%s_%d" % (name, self.uid), shape, dt))

    def dram(self, name, shape, dt):
        kind = "ExternalOutput" if name in self.debug else "Internal"
        if kind == "Internal":
            return self.nc.dram_tensor(name, shape, dt).ap()
        return self.nc.dram_tensor(name, shape, dt, kind=kind).ap()

    def rstd(self, out, in_, n, r, w):
        self.ts('dve', out, in_, 1.0 / n, EPS, ALU.mult, ALU.add, r, w)
        self.act(out, out, AF.Sqrt, w, w)
        self.recip(out, out, w, w)

    def setup(self):
        nc = self.nc
        g = self.gs
        self.cst_d = nc.dram_tensor("cst", [128, 512], F32, kind="ExternalInput").ap()
        self.pos_d = nc.dram_tensor("pos", [1, S_ALL], I32, kind="ExternalInput").ap()
        self.mem_d = nc.dram_tensor("mem", [256, DM], F32, kind="ExternalInput").ap()
        self.cst = self.sb(g, "cst", [128, 512], F32)
        self.idb = self.sb(g, "idb", [128, 128], BF16)
        self.trib = self.sb(g, "trib", [128, 128], BF16)
        self.onesf = self.sb(g, "onesf", [128, 128], F32)
        self.onesb = self.sb(g, "onesb", [128, 128], BF16)
        self.psum = g.enter_context(nc.psum_tensor("ps", [128, 4096], F32))
        self.dma('sp', self.cst[:], self.cst_d, (), ['cst'])
        self.cp('dve', self.idb[:], self.cst[:, 0:128], ['cst'], ['idb'])
        self.cp('dve', self.trib[:], self.cst[:, 128:256], ['cst'], ['trib'])
        self.memset('dve', self.onesf[:], 1.0, ['onesf'])
        self.memset('dve', self.onesb[:], 1.0, ['onesb'])
        self.idf = self.cst[:, 0:128]
        self.trif = self.cst[:, 128:256]
        self.c_if2pi = self.cst[0:64, 256:257]
        self.c_sign = self.cst[0:64, 257:258]
        self.c_flag = self.cst[:, 258:259]
        self.c_cbias = self.cst[:, 259:260]
        self.c_zero = self.cst[:, 260:261]
        self.Cst = self.sb(g, "Cst", [128, 8, 257], F32)
        self.Cbf = self.sb(g, "Cbf", [128, 8, 257], BF16)
        self.u_s = self.dram("u_s", [1024, S_ALL], F32)
        self.cq_s = self.dram("cq_s", [512, S_OWN], F32)
        self.ckv_s = self.dram("ckv_s", [256, S_ALL], F32)
        self.kr_s = self.dram("kr_s", [2, 64, S_ALL], F32)
        self.xm_s = self.dram("xm_s", [1024, 3 + S_ALL], F32)
        self.z_s = self.dram("z_s", [1024, S_OWN], F32)
        self.xc_s = self.dram("xc_s", [1024, S_OWN], F32)
        self.cb_s = self.dram("cb_s", [1024, S_OWN], BF16)
        self.o_s = self.dram("o_s", [1024, S_OWN], BF16)
        self.hn_s = self.dram("hn_s", [1024, S_OWN], BF16)
        self.kn_s = self.dram("kn_s", [8, 128, S_ALL], BF16)
        self.v_s = self.dram("v_s", [S_ALL, 1024], BF16)
        self.krot_s = self.dram("krot_s", [64, S_ALL], BF16)
        self.x1_s = self.dram("x1_s", [S_OWN, DM], F32)
        self.x2_s = self.dram("x2_s", [S_OWN, DM], F32)
        self.cos_s = self.dram("cos_s", [64, S_ALL], F32)
        self.sin_s = self.dram("sin_s", [64, S_ALL], F32)
        self.P.barrier()
        with ExitStack() as es:
            for gi in range(4):
                cosT, sinT = self.rope_compute(es, gi * G)
                self.dma('sp', self.cos_s[:, gi * G:(gi + 1) * G], cosT[:], ['rp_cos'], ['cos_s'])
                self.dma('sp', self.sin_s[:, gi * G:(gi + 1) * G], sinT[:], ['rp_sin'], ['sin_s'])
            self.P.barrier()

    def ps(self, bank, n=512, off=0):
        return self.psum[:, bank * 512 + off: bank * 512 + off + n]

    def norm_transpose(self, es, src, row0, ntiles, gain_d, hT, hkey):
        gn = self.sb(es, "gain", [128, DM], F32)
        self.dma('sp', gn[:], gain_d.partition_broadcast(128), (), ['gain'])
        xts = [self.sb(es, "xt%d" % i, [128, DM], F32) for i in range(2)]
        hbs = [self.sb(es, "hb%d" % i, [128, DM], BF16) for i in range(2)]
        junk = self.sb(es, "junk", [128, DM], F32)
        st = self.sb(es, "nst", [128, 2 * ntiles], F32)
        self.memset('dve', st[:], 0.0, ['nst'])
        def stage1(t):
            xt = xts[t % 2]; hb = hbs[t % 2]
            kx = ('xt', t % 2); kh = ('hb', t % 2)
            self.dma('sp', xt[:], src[row0 + t * 128: row0 + (t + 1) * 128, :], (), [kx])
            self.act(junk[:], xt[:], AF.Square, [kx, 'nst'], ['junk', 'nst'], accum_out=st[:, 2 * t:2 * t + 1])
            self.rstd(st[:, 2 * t + 1:2 * t + 2], st[:, 2 * t:2 * t + 1], DM, ['nst'], ['nst'])
            self.stt('dve', hb[:], xt[:], st[:, 2 * t + 1:2 * t + 2], gn[:], ALU.mult, ALU.mult, [kx, 'nst', 'gain'], [kh])

        def stage2(t):
            hb = hbs[t % 2]
            kh = ('hb', t % 2)
            pb = (t % 2) * 2
            pv = self.psum[:, pb * 512:(pb + 2) * 512].bitcast(BF16)
            for c in range(16):
                self.tr(pv[:, c * 128:(c + 1) * 128], hb[:, c * 128:(c + 1) * 128], self.idb[:], [kh, 'idb'], [('ps', pb), ('ps', pb + 1)])
            outv = hT[:, :, t * 128:(t + 1) * 128]
            inv = pv.rearrange("p (c t) -> p c t", c=16)
            if t % 2 == 0:
                self.act(outv, inv, AF.Copy, [('ps', pb), ('ps', pb + 1)], [(hkey, t // 4)])
            else:
                self.cp('dve', outv, inv, [('ps', pb), ('ps', pb + 1)], [(hkey, t // 4)])

        for t in range(ntiles + 1):
            if t < ntiles:
                stage1(t)
            if t >= 1:
                stage2(t - 1)

    def wload(self, dst, src, key):
        self.dma('pool', dst, src, (), [key])

    def proj_fm(self, wt, wkey, fsl, M, nk, rhs_fn, rkeys, bank0, nsb):
        for s in range(nsb):
            for k in range(nk):
                self.mm(self.ps(bank0 + s)[0:M, :], wt[:, k, fsl:fsl + M], rhs_fn(k, s), k == 0, k == nk - 1,
                        [wkey] + rkeys(k, s), [('ps', bank0 + s)])

    def layer(self, l, W, x_own, x_ctx, x_out):
        P = self.P
        nc = self.nc
        with ExitStack() as ls:
            def col(name, src, nch, rows=128):
                t = self.sb(ls, name, [rows, nch], F32)
                self.dma('sp', t[:], src.rearrange("(c p) -> p c", p=rows), (), [name], allow_slow_non_contiguous=True)
                return t
            pv = {}
            pv['bgate'] = col('bgate', W['b_gate'][l], 48)
            pv['cdb'] = col('cdb', W['conv_dw_b'][l], 8)
            pv['clg'] = col('clg', W['conv_ln_g'][l], 8)
            pv['clb'] = col('clb', W['conv_ln_b'][l], 8)
            pv['qng'] = col('qng', W['mla_q_norm'][l], 4)
            pv['kvg'] = col('kvg', W['mla_kv_norm'][l], 2)
            pv['mcb'] = col('mcb', W['mlstm_conv_b'][l], 8)
            pv['gng'] = col('gng', W['mlstm_gn_g'][l], 8)
            pv['skp'] = col('skp', W['mlstm_skip'][l], 8)
            pv['xgq'] = col('xgq', W['xattn_g_q'][l], 1)
            pv['xgk'] = col('xgk', W['xattn_g_k'][l], 1)
            cdw = self.sb(ls, 'cdw', [128, 8, 31], F32)
            for c in range(8):
                self.dma('sp', cdw[:, c, :], W['conv_dw'][l][:, c * 128:(c + 1) * 128].rearrange("j p -> p j"), (), ['cdw'], allow_slow_non_contiguous=True)
            mcw = self.sb(ls, 'mcw', [128, 8, 4], F32)
            for c in range(8):
                self.dma('sp', mcw[:, c, :], W['mlstm_conv_w'][l][:, c * 128:(c + 1) * 128].rearrange("j p -> p j"), (), ['mcw'], allow_slow_non_contiguous=True)
            gq = self.sb(ls, 'gq', [128, 4], F32)
            gk = self.sb(ls, 'gk', [128, 4], F32)
            for (t, nm, key) in ((gq, 'mla_g_q', 'gq'), (gk, 'mla_g_k', 'gk')):
                src = W[nm][l]
                self.dma('sp', t[:, 0:1], src[0:128].rearrange("(p o) -> p o", o=1), (), [key], allow_slow_non_contiguous=True)
                self.dma('sp', t[0:64, 1:2], src[128:192].rearrange("(p o) -> p o", o=1), (), [key], allow_slow_non_contiguous=True)
                self.dma('sp', t[0:32, 2:3], src[160:192].rearrange("(p o) -> p o", o=1), (), [key], allow_slow_non_contiguous=True)
                self.dma('sp', t[32:64, 2:3], src[128:160].rearrange("(p o) -> p o", o=1), (), [key], allow_slow_non_contiguous=True)
            bif = self.sb(ls, 'bif', [128, 8], F32)
            self.dma('sp', bif[:], W['b_if'][l].partition_broadcast(128), (), ['bif'])
            wif = self.sb(ls, 'wif', [128, 24, 8], BF16)
            self.wload(wif[:], W['w_if'][l].rearrange("(c p) e -> p c e", p=128), 'wif')
            kx = self.sb(ls, 'kx', [128, 4, 256], BF16)
            vx = self.sb(ls, 'vx', [128, 2, 512], BF16)
            self.stage_mem(l, W, pv, kx, vx)
            self.memset('dve', self.Cst[:], 0.0, ['Cst'])
            self.memset('dve', self.Cbf[:], 0.0, ['Cbf'])
            with ExitStack() as es:
                z = self.sb(es, "zpad", [128, 8, 3], F32)
                self.memset('dve', z[:], 0.0, ['zpad'])
                self.dma('sp', self.xm_s[:, 0:3].rearrange("(c p) j -> p c j", p=128), z[:], ['zpad'], ['xm_s'])
                P.barrier()

            for gi in range(4):
                own = gi >= 2
                p0 = gi * G
                o0 = (gi - 2) * G
                src = x_own if own else x_ctx
                r0 = o0 if own else p0
                with ExitStack() as gsx:
                    hT = self.sb(gsx, "hT", [128, 16, G], BF16)
                    with ExitStack() as es:
                        self.norm_transpose(es, src, r0, NT, W['norm_mix'][l], hT, 'hT')
                        P.barrier()
                    self.stage_proj(l, W, gi, hT)
                    if own:
                        self.stage_conv(l, W, pv, cdw, gi)
                    self.stage_kv(l, W, pv, gk, gi)
                    if own:
                        self.stage_attn(l, W, pv, gq, gi)
                    self.stage_mlstm(l, W, pv, mcw, bif, wif, gi)
                    if own:
                        self.stage_merge(l, W, pv, gi, hT, src)
                if own:
                    self.stage_xattn(l, W, pv, kx, vx, gi)
                    self.stage_ffn(l, W, gi, x_out)
            P.barrier()

    def stage_mem(self, l, W, pv, kx, vx):
        P = self.P
        with ExitStack() as es:
            mT = self.sb(es, "mT", [128, 16, 256], BF16)
            with ExitStack() as es2:
                self.norm_transpose(es2, self.mem_d, 0, 2, W['norm_mem'][l], mT, 'mT')
                P.barrier()
            wv = W['w_xkv'][l].rearrange("(c p) f -> p c f", p=128)
            wk = self.sb(es, "wxkv", [128, 16, 1024], BF16)
            self.wload(wk[:, :, 0:512], wv[:, :, 0:512], 'wxkv0')
            self.wload(wk[:, :, 512:1024], wv[:, :, 512:1024], 'wxkv1')
            kf = self.sb(es, "kf", [128, 256], F32)
            sq = self.sb(es, "sq", [128, 256], F32)
            rs = self.sb(es, "rs", [128, 256], F32)
            for h in range(4):
                b = h % 2
                for k in range(16):
                    self.mm(self.ps(b, 256), wk[:, k, h * 128:(h + 1) * 128], mT[:, k, :], k == 0, k == 15,
                            ['wxkv0', ('mT', 0)], [('ps', b)])
                self.act(kf[:], self.ps(b, 256), AF.Copy, [('ps', b)], ['kf'])
                self.act(sq[:], self.ps(b, 256), AF.Square, [('ps', b)], ['sq'])
                self.mm(self.ps(2 + b, 256), self.onesf[:], sq[:], True, True, ['onesf', 'sq'], [('ps', 2 + b)])
                self.rstd(rs[:], self.ps(2 + b, 256), 128, [('ps', 2 + b)], ['rs'])
                self.stt('dve', kx[:, h, :], kf[:], pv['xgk'][:, 0:1], rs[:], ALU.mult, ALU.mult, ['kf', 'rs', 'xgk'], ['kx'])
            for mc in range(2):
                b = 4 + mc
                for k in range(16):
                    self.mm(self.ps(b), mT[:, k, mc * 128:(mc + 1) * 128], wk[:, k, 512:1024], k == 0, k == 15,
                            ['wxkv1', ('mT', 0)], [('ps', b)])
                self.act(vx[:, mc, :], self.ps(b), AF.Copy, [('ps', b)], ['vx'])
            P.barrier()

    def stage_proj(self, l, W, gi, hT):
        P = self.P
        own = gi >= 2
        p0 = gi * G
        o0 = (gi - 2) * G
        wv = W['w_in'][l].rearrange("(c p) f -> p c f", p=128)
        with ExitStack() as es:
            wb = [self.sb(es, "wb%d" % i, [128, 16, 512], BF16) for i in range(3)]
            stg = [self.sb(es, "stg%d" % i, [128, G], F32) for i in range(4)]
            sig = [self.sb(es, "sig%d" % i, [128, G], F32) for i in range(2)]
            cnt = {'w': 0, 's': 0, 'b': 0, 'g': 0}

            def nextw():
                i = cnt['w'] % 3; cnt['w'] += 1
                return wb[i], ('wb', i)

            def nexts():
                i = cnt['s'] % 4; cnt['s'] += 1
                return stg[i], ('stg', i)

            def nextb():
                b = (cnt['b'] % 4) * 2; cnt['b'] += 1
                return b

            def rhs_fn(k, s):
                return hT[:, k, s * 512:(s + 1) * 512]

            def rkeys(k, s):
                return [('hT', s)]

            def simple_seg(c0, width, M, dst_fn, func):
                for f0 in range(0, width, 512):
                    fw = min(512, width - f0)
                    wt, wkey = nextw()
                    self.wload(wt[:, :, 0:fw], wv[:, :, c0 + f0:c0 + f0 + fw], wkey)
                    for j in range(0, fw, M):
                        b = nextb()
                        self.proj_fm(wt, wkey, j, M, 16, rhs_fn, rkeys, b, NSB)
                        st_, skey = nexts()
                        self.act(st_[0:M, :], self.psum[0:M, b * 512:b * 512 + G], func, [('ps', b), ('ps', b + 1)], [skey])
                        dst_fn(f0 + j, st_, skey)

            if own:
                for f0 in range(0, 1024, 512):
                    wa, ka = nextw()
                    self.wload(wa[:], wv[:, :, f0:f0 + 512], ka)
                    wg, kg = nextw()
                    self.wload(wg[:], wv[:, :, 1024 + f0:1024 + f0 + 512], kg)
                    for j in range(0, 512, 128):
                        ba = nextb()
                        self.proj_fm(wa, ka, j, 128, 16, rhs_fn, rkeys, ba, NSB)
                        bg = nextb()
                        self.proj_fm(wg, kg, j, 128, 16, rhs_fn, rkeys, bg, NSB)
                        i = cnt['g'] % 2; cnt['g'] += 1
                        self.act(sig[i][:], self.psum[:, bg * 512:bg * 512 + G], AF.Sigmoid, [('ps', bg), ('ps', bg + 1)], [('sig', i)])
                        st_, skey = nexts()
                        self.tt('dve', st_[:], self.psum[:, ba * 512:ba * 512 + G], sig[i][:], ALU.mult,
                                [('ps', ba), ('ps', ba + 1), ('sig', i)], [skey])
                        ch = (f0 + j) // 128
                        self.dma('sp', self.u_s[ch * 128:(ch + 1) * 128, p0:p0 + G], st_[:], [skey], ['u_s'])
                simple_seg(2048, 512, 128,
                           lambda f, st_, sk: self.dma('sp', self.cq_s[f:f + 128, o0:o0 + G], st_[:], [sk], ['cq_s']), AF.Copy)
                simple_seg(3904, 1024, 128,
                           lambda f, st_, sk: self.dma('sp', self.z_s[f:f + 128, o0:o0 + G], st_[:], [sk], ['z_s']), AF.Silu)
            elif gi == 1:
                for f0 in range(0, 1024, 512):
                    wa, ka = nextw()
                    self.wload(wa[:], wv[:, :, f0:f0 + 512], ka)
                    wg, kg = nextw()
                    self.wload(wg[:], wv[:, :, 1024 + f0:1024 + f0 + 512], kg)
                    for j in range(0, 512, 128):
                        ba = nextb()
                        bg = nextb()
                        for k in range(16):
                            self.mm(self.ps(ba, 128), wa[:, k, j:j + 128], hT[:, k, G - 128:G], k == 0, k == 15, [ka, ('hT', 1)], [('ps', ba)])
                        for k in range(16):
                            self.mm(self.ps(bg, 128), wg[:, k, j:j + 128], hT[:, k, G - 128:G], k == 0, k == 15, [kg, ('hT', 1)], [('ps', bg)])
                        i = cnt['g'] % 2; cnt['g'] += 1
                        self.act(sig[i][:, 0:128], self.ps(bg, 128), AF.Sigmoid, [('ps', bg)], [('sig', i)], scale=1.0)
                        st_, skey = nexts()
                        self.stt('dve', st_[:, 0:128], self.ps(ba, 128), self.c_flag, sig[i][:, 0:128], ALU.mult, ALU.mult,
                                 [('ps', ba), ('sig', i), 'cst'], [skey])
                        ch = (f0 + j) // 128
                        self.dma('sp', self.u_s[ch * 128:(ch + 1) * 128, p0 + G - 128:p0 + G], st_[:, 0:128], [skey], ['u_s'])
            simple_seg(2560, 256, 128,
                       lambda f, st_, sk: self.dma('sp', self.ckv_s[f:f + 128, p0:p0 + G], st_[:], [sk], ['ckv_s']), AF.Copy)
            wt, wkey = nextw()
            self.wload(wt[:, :, 0:64], wv[:, :, 2816:2880], wkey)
            self.wload(wt[:, :, 64:96], wv[:, :, 2848:2880], wkey)
            self.wload(wt[:, :, 96:128], wv[:, :, 2816:2848], wkey)
            for v_ in range(2):
                b = nextb()
                self.proj_fm(wt, wkey, v_ * 64, 64, 16, rhs_fn, rkeys, b, NSB)
                st_, skey = nexts()
                self.act(st_[0:64, :], self.psum[0:64, b * 512:b * 512 + G], AF.Copy, [('ps', b), ('ps', b + 1)], [skey])
                self.dma('sp', self.kr_s[v_, :, p0:p0 + G], st_[0:64, :], [skey], ['kr_s'])
            simple_seg(2880, 1024, 128,
                       lambda f, st_, sk: self.dma('sp', self.xm_s[f:f + 128, 3 + p0:3 + p0 + G], st_[:], [sk], ['xm_s']), AF.Copy)
            P.barrier()

    def stage_conv(self, l, W, pv, cdw, gi):
        P = self.P
        p0 = gi * G
        o0 = (gi - 2) * G
        with ExitStack() as es:
            ub = [self.sb(es, "ub%d" % i, [128, 30 + G], F32) for i in range(2)]
            cv = self.sb(es, "cv", [128, 8, G], F32)
            sq = [self.sb(es, "csq%d" % i, [128, G], F32) for i in range(2)]
            mean = self.sb(es, "mean", [128, G], F32)
            rs = self.sb(es, "crs", [128, G], F32)
            tmp = [self.sb(es, "ctmp%d" % i, [128, G], F32) for i in range(2)]
            cbo = [self.sb(es, "cbo%d" % i, [128, G], BF16) for i in range(2)]
            PE_CHUNKS = (5, 6, 7)
            order = [5, 0, 6, 1, 7, 2, 3, 4]
            ubp = [self.sb(es, "ubp%d" % i, [128, 30 + G], F32) for i in range(2)]
            dg = [self.sb(es, "dg%d" % i, [128, 31, 128], F32) for i in range(2)]
            npe = 0
            for oi, c in enumerate(order):
                first = (oi == 0); last = (oi == len(order) - 1)
                if c in PE_CHUNKS:
                    u = ubp[npe % 2]; uk = ('ubp', npe % 2)
                    d = dg[npe % 2]; dk = ('dg', npe % 2)
                    npe += 1
                    self.dma('sp', u[:], self.u_s[c * 128:(c + 1) * 128, p0 - 30:p0 + G], ['u_s'], [uk])
                    for j in range(31):
                        self.act(d[:, j, :], self.idf, AF.Copy, ['cst', 'cdw'], [dk], scale=cdw[:, c, j:j + 1])
                    for sbk in range(NSB):
                        for j in range(31):
                            self.mm(self.ps(4 + sbk), d[:, j, :], u[:, j + sbk * 512:j + sbk * 512 + 512], j == 0, j == 30,
                                    [dk, uk], [('ps', 4 + sbk)])
                    self.act(cv[:, c, :], self.psum[:, 4 * 512:4 * 512 + G], AF.Identity, [('ps', 4), ('ps', 5), 'cdb'], [('cv', c)],
                             bias=pv['cdb'][:, c:c + 1])
                else:
                    u = ub[c % 2]; uk = ('ub', c % 2)
                    self.dma('sp', u[:], self.u_s[c * 128:(c + 1) * 128, p0 - 30:p0 + G], ['u_s'], [uk])
                    self.ts('dve', cv[:, c, :], u[:, 0:G], cdw[:, c, 0:1], pv['cdb'][:, c:c + 1], ALU.mult, ALU.add,
                            [uk, 'cdw', 'cdb'], [('cv', c)])
                    for j in range(1, 31):
                        self.stt('dve', cv[:, c, :], u[:, j:j + G], cdw[:, c, j:j + 1], cv[:, c, :], ALU.mult, ALU.add,
                                 [uk, 'cdw', ('cv', c)], [('cv', c)])
                s = sq[oi % 2]; sk = ('csq', oi % 2)
                self.act(s[:], cv[:, c, :], AF.Square, [('cv', c)], [sk])
                for sbk in range(NSB):
                    self.mm(self.ps(sbk), self.onesf[:], cv[:, c, sbk * 512:(sbk + 1) * 512], first, last, ['onesf', ('cv', c)], [('ps', sbk)])
                    self.mm(self.ps(2 + sbk), self.onesf[:], s[:, sbk * 512:(sbk + 1) * 512], first, last, ['onesf', sk], [('ps', 2 + sbk)])
            r_s1 = [('ps', 0), ('ps', 1)]
            r_s2 = [('ps', 2), ('ps', 3)]
            self.act(mean[:], self.psum[:, 0:G], AF.Copy, r_s1, ['mean'], scale=1.0 / 1024)
            t0 = tmp[0]
            self.tt('dve', t0[:], mean[:], mean[:], ALU.mult, ['mean'], [('ctmp', 0)])
            self.stt('dve', rs[:], self.psum[:, 1024:1024 + G], 1.0 / 1024, t0[:], ALU.mult, ALU.subtract, r_s2 + [('ctmp', 0)], ['crs'])
            self.ts('dve', rs[:], rs[:], EPS, None, ALU.add, None, ['crs'], ['crs'])
            self.act(rs[:], rs[:], AF.Sqrt, ['crs'], ['crs'])
            self.recip(rs[:], rs[:], ['crs'], ['crs'])
            for c in range(8):
                t = tmp[c % 2]; tk = ('ctmp', c % 2)
                self.tt('dve', t[:], cv[:, c, :], mean[:], ALU.subtract, [('cv', c), 'mean'], [tk])
                self.tt('dve', t[:], t[:], rs[:], ALU.mult, [tk, 'crs'], [tk])
                o = cbo[c % 2]; ok = ('cbo', c % 2)
                self.act(o[:], t[:], AF.Silu, [tk, 'clg', 'clb'], [ok], scale=pv['clg'][:, c:c + 1], bias=pv['clb'][:, c:c + 1])
                self.dma('sp', self.cb_s[c * 128:(c + 1) * 128, o0:o0 + G], o[:], [ok], ['cb_s'])
            P.barrier()

    def rope_compute(self, es, p0):
        pi_ = self.sb(es, "rp_i", [64, G], I32)
        t = self.sb(es, "rp_t", [64, G], F32)
        kf = self.sb(es, "rp_k", [64, G], F32)
        cosT = self.sb(es, "rp_cos", [64, G], F32)
        sinT = self.sb(es, "rp_sin", [64, G], F32)
        self.dma('sp', pi_[:], self.pos_d[:, p0:p0 + G].partition_broadcast(64), (), ['rp_i'])
        self.cp('dve', t[:], pi_[:], ['rp_i'], ['rp_t'])
        self.ts('dve', t[:], t[:], self.c_if2pi, None, ALU.mult, None, ['rp_t', 'cst'], ['rp_t'])
        for which, dst, key in ((0, sinT, 'rp_sin'), (1, cosT, 'rp_cos')):
            if which == 1:
                self.ts('dve', t[:], t[:], 0.25, None, ALU.add, None, ['rp_t'], ['rp_t'])
            self.cp('dve', pi_[:], t[:], ['rp_t'], ['rp_i'])
            self.cp('dve', kf[:], pi_[:], ['rp_i'], ['rp_k'])
            self.tt('dve', kf[:], t[:], kf[:], ALU.subtract, ['rp_t', 'rp_k'], ['rp_k'])
            self.ts('dve', dst[:], kf[:], 0.5, None, ALU.is_gt, None, ['rp_k'], [key])
            self.tt('dve', kf[:], kf[:], dst[:], ALU.subtract, ['rp_k', key], ['rp_k'])
            self.act(dst[:], kf[:], AF.Sin, ['rp_k'], [key], scale=2 * math.pi)
        self.ts('dve', sinT[:], sinT[:], self.c_sign, None, ALU.mult, None, ['rp_sin', 'cst'], ['rp_sin'])
        return cosT, sinT

    def rope_tables(self, es, p0):
        cosT = self.sb(es, "rp_cos", [64, G], F32)
        sinT = self.sb(es, "rp_sin", [64, G], F32)
        self.dma('sp', cosT[:], self.cos_s[:, p0:p0 + G], (), ['rp_cos'])
        self.dma('sp', sinT[:], self.sin_s[:, p0:p0 + G], (), ['rp_sin'])
        t1 = self.sb(es, "rt1", [64, G], F32)
        t2 = self.sb(es, "rt2", [64, G], F32)
        return cosT, sinT, t1, t2

    def stage_kv(self, l, W, pv, gk, gi):
        P = self.P
        p0 = gi * G
        with ExitStack() as es:
            tabs = self.rope_tables(es, p0)
            ck = self.sb(es, "ck", [128, 2, G], F32)
            sq = self.sb(es, "ksq", [128, G], F32)
            rs = self.sb(es, "krs", [128, G], F32)
            ckn = self.sb(es, "ckn", [128, 2, G], BF16)
            wkv = self.sb(es, "wkv", [128, 2, 2048], BF16)
            wvv = self.sb(es, "wvv", [128, 2, 8, 128], BF16)
            self.wload(wkv[:], W['w_kv_up'][l].rearrange("(c p) f -> p c f", p=128), 'wkv')
            wv5 = W['w_kv_up'][l].rearrange("(c p) (h two d) -> p c h two d", p=128, two=2, d=128)
            for c in range(2):
                self.wload(wvv[:, c, :, :], wv5[:, c, :, 1, :], 'wvv')
            self.dma('sp', ck[:], self.ckv_s[:, p0:p0 + G].rearrange("(c p) t -> p c t", p=128), ['ckv_s'], ['ck'])
            for c in range(2):
                self.act(sq[:], ck[:, c, :], AF.Square, ['ck'], ['ksq'])
                for s in range(NSB):
                    self.mm(self.ps(s), self.onesf[:], sq[:, s * 512:(s + 1) * 512], c == 0, c == 1, ['onesf', 'ksq'], [('ps', s)])
            self.rstd(rs[:], self.psum[:, 0:G], 256, [('ps', 0), ('ps', 1)], ['krs'])
            for c in range(2):
                self.stt('dve', ckn[:, c, :], ck[:, c, :], pv['kvg'][:, c:c + 1], rs[:], ALU.mult, ALU.mult, ['ck', 'krs', 'kvg'], ['ckn'])
            kf = [self.sb(es, "kff%d" % i, [128, G], F32) for i in range(2)]
            sq2 = [self.sb(es, "ksq2%d" % i, [128, G], F32) for i in range(2)]
            rs2 = [self.sb(es, "krs2%d" % i, [128, G], F32) for i in range(2)]
            kno = [self.sb(es, "kno%d" % i, [128, G], BF16) for i in range(2)]
            for h in range(8):
                i = h % 2
                b = 2 + i * 2
                for s in range(NSB):
                    for c in range(2):
                        self.mm(self.ps(b + s), wkv[:, c, h * 256:h * 256 + 128], ckn[:, c, s * 512:(s + 1) * 512], c == 0, c == 1,
                                ['wkv', 'ckn'], [('ps', b + s)])
                rb = [('ps', b), ('ps', b + 1)]
                self.act(kf[i][:], self.psum[:, b * 512:b * 512 + G], AF.Copy, rb, [('kff', i)])
                self.act(sq2[i][:], self.psum[:, b * 512:b * 512 + G], AF.Square, rb, [('ksq2', i)])
                for s in range(NSB):
                    self.mm(self.ps(6 + s), self.onesf[:], sq2[i][:, s * 512:(s + 1) * 512], True, True, ['onesf', ('ksq2', i)], [('ps', 6 + s)])
                self.rstd(rs2[i][:], self.psum[:, 6 * 512:6 * 512 + G], 128, [('ps', 6), ('ps', 7)], [('krs2', i)])
                self.stt('dve', kno[i][:], kf[i][:], gk[:, 0:1], rs2[i][:], ALU.mult, ALU.mult, [('kff', i), ('krs2', i), 'gk'], [('kno', i)])
                self.dma('sp', self.kn_s[h, :, p0:p0 + G], kno[i][:], [('kno', i)], ['kn_s'])
            vo = [self.sb(es, "vo%d" % i, [128, 1024], BF16) for i in range(2)]
            for t in range(NT):
                i = t % 2
                b = 2 + i * 2
                for hb in range(2):
                    for c in range(2):
                        self.mm(self.ps(b + hb), ckn[:, c, t * 128:(t + 1) * 128], wvv[:, c, hb * 4:(hb + 1) * 4, :].rearrange("p h d -> p (h d)"),
                                c == 0, c == 1, ['wvv', 'ckn'], [('ps', b + hb)])
                self.act(vo[i][:], self.psum[:, b * 512:b * 512 + 1024], AF.Copy, [('ps', b), ('ps', b + 1)], [('vo', i)])
                self.dma('sp', self.v_s[p0 + t * 128:p0 + (t + 1) * 128, :], vo[i][:], [('vo', i)], ['v_s'])
            kr = self.sb(es, "krr", [64, 2, G], F32)
            self.dma('sp', kr[:], self.kr_s[:, :, p0:p0 + G].rearrange("v p t -> p v t"), ['kr_s'], ['krr'])
            sq3 = self.sb(es, "ksq3", [64, G], F32)
            rs3 = self.sb(es, "krs3", [64, G], F32)
            kro = self.sb(es, "kro", [64, G], BF16)
            self.act(sq3[:], kr[:, 0, :], AF.Square, ['krr'], ['ksq3'])
            for s in range(NSB):
                self.mm(self.ps(s)[0:64, :], self.onesf[0:64, 0:64], sq3[:, s * 512:(s + 1) * 512], True, True, ['onesf', 'ksq3'], [('ps', s)])
            self.rstd(rs3[:], self.psum[0:64, 0:G], 64, [('ps', 0), ('ps', 1)], ['krs3'])
            self.rope_apply(kr[:, 0, :], kr[:, 1, :], rs3, gk, tabs, kro[:], ['krr', 'krs3', 'gk'], 'kro')
            self.dma('sp', self.krot_s[:, p0:p0 + G], kro[:], ['kro'], ['krot_s'])
            P.barrier()

    def rope_apply(self, a, b, rs, gvec, tabs, out, in_keys, okey):
        cosT, sinT, t1, t2 = tabs
        self.stt('dve', t1[:], a, gvec[0:64, 1:2], rs[:], ALU.mult, ALU.mult, in_keys, ['rt1'])
        self.tt('dve', t1[:], t1[:], cosT[:], ALU.mult, ['rt1', 'rp_cos'], ['rt1'])
        self.stt('dve', t2[:], b, gvec[0:64, 2:3], rs[:], ALU.mult, ALU.mult, in_keys, ['rt2'])
        self.tt('dve', t2[:], t2[:], sinT[:], ALU.mult, ['rt2', 'rp_sin'], ['rt2'])
        self.tt('dve', out, t1[:], t2[:], ALU.add, ['rt1', 'rt2'], [okey])

    def stage_attn(self, l, W, pv, gq, gi):
        P = self.P
        p0 = gi * G
        o0 = (gi - 2) * G
        scale = 192.0 ** -0.5
        with ExitStack() as es:
            qn = self.sb(es, "qn", [128, 8, G], BF16)
            qr = self.sb(es, "qr", [64, 8, G], BF16)
            with ExitStack() as e2:
                tabs = self.rope_tables(e2, p0)
                cq = self.sb(e2, "cq", [128, 4, G], F32)
                sq = self.sb(e2, "qsq", [128, G], F32)
                rs = self.sb(e2, "qrs", [128, G], F32)
                cqn = self.sb(e2, "cqn", [128, 4, G], BF16)
                wq = self.sb(e2, "wq", [128, 4, 1536], BF16)
                wqp = self.sb(e2, "wqp", [128, 4, 8, 64], BF16)
                wsrc = W['w_q_up'][l].rearrange("(c p) f -> p c f", p=128)
                self.wload(wq[:], wsrc, 'wq')
                w3 = W['w_q_up'][l].rearrange("(c p) (h e) -> p c h e", p=128, e=192)
                for c in range(4):
                    self.wload(wqp[:, c, :, 0:32], w3[:, c, :, 160:192], 'wqp')
                    self.wload(wqp[:, c, :, 32:64], w3[:, c, :, 128:160], 'wqp')
                self.dma('sp', cq[:], self.cq_s[:, o0:o0 + G].rearrange("(c p) t -> p c t", p=128), ['cq_s'], ['cq'])
                for c in range(4):
                    self.act(sq[:], cq[:, c, :], AF.Square, ['cq'], ['qsq'])
                    for s in range(NSB):
                        self.mm(self.ps(s), self.onesf[:], sq[:, s * 512:(s + 1) * 512], c == 0, c == 3, ['onesf', 'qsq'], [('ps', s)])
                self.rstd(rs[:], self.psum[:, 0:G], 512, [('ps', 0), ('ps', 1)], ['qrs'])
                for c in range(4):
                    self.stt('dve', cqn[:, c, :], cq[:, c, :], pv['qng'][:, c:c + 1], rs[:], ALU.mult, ALU.mult, ['cq', 'qrs', 'qng'], ['cqn'])
                qf = self.sb(e2, "qf", [128, G], F32)
                sq2 = self.sb(e2, "qsq2", [128, G], F32)
                rs2 = self.sb(e2, "qrs2", [128, G], F32)
                qa = self.sb(e2, "qa", [64, G], F32)
                qb = self.sb(e2, "qb", [64, G], F32)
                sq3 = self.sb(e2, "qsq3", [64, G], F32)
                rs3 = self.sb(e2, "qrs3", [64, G], F32)
                for h in range(8):
                    for s in range(NSB):
                        for c in range(4):
                            self.mm(self.ps(s), wq[:, c, h * 192:h * 192 + 128], cqn[:, c, s * 512:(s + 1) * 512], c == 0, c == 3,
                                    ['wq', 'cqn'], [('ps', s)])
                    rb = [('ps', 0), ('ps', 1)]
                    self.act(qf[:], self.psum[:, 0:G], AF.Copy, rb, ['qf'])
                    self.act(sq2[:], self.psum[:, 0:G], AF.Square, rb, ['qsq2'])
                    for s in range(NSB):
                        self.mm(self.ps(2 + s), self.onesf[:], sq2[:, s * 512:(s + 1) * 512], True, True, ['onesf', 'qsq2'], [('ps', 2 + s)])
                    self.rstd(rs2[:], self.psum[:, 1024:1024 + G], 128, [('ps', 2), ('ps', 3)], ['qrs2'])
                    self.stt('dve', qn[:, h, :], qf[:], gq[:, 0:1], rs2[:], ALU.mult, ALU.mult, ['qf', 'qrs2', 'gq'], [('qn', h)])
                    for s in range(NSB):
                        for c in range(4):
                            self.mm(self.ps(4 + s)[0:64, :], wq[:, c, h * 192 + 128:h * 192 + 192], cqn[:, c, s * 512:(s + 1) * 512], c == 0, c == 3,
                                    ['wq', 'cqn'], [('ps', 4 + s)])
                        for c in range(4):
                            self.mm(self.ps(6 + s)[0:64, :], wqp[:, c, h, :], cqn[:, c, s * 512:(s + 1) * 512], c == 0, c == 3,
                                    ['wqp', 'cqn'], [('ps', 6 + s)])
                    self.act(qa[:], self.psum[0:64, 4 * 512:4 * 512 + G], AF.Copy, [('ps', 4), ('ps', 5)], ['qa'])
                    self.act(qb[:], self.psum[0:64, 6 * 512:6 * 512 + G], AF.Copy, [('ps', 6), ('ps', 7)], ['qb'])
                    self.act(sq3[:], qa[:], AF.Square, ['qa'], ['qsq3'])
                    for s in range(NSB):
                        self.mm(self.ps(4 + s)[0:64, :], self.onesf[0:64, 0:64], sq3[:, s * 512:(s + 1) * 512], True, True, ['onesf', 'qsq3'], [('ps', 4 + s)])
                    self.rstd(rs3[:], self.psum[0:64, 4 * 512:4 * 512 + G], 64, [('ps', 4), ('ps', 5)], ['qrs3'])
                    self.rope_apply(qa[:], qb[:], rs3, gq, tabs, qr[:, h, :], ['qa', 'qb', 'qrs3', 'gq'], ('qr', h))
                P.barrier()
            nkeys = p0 + G
            nkb_all = nkeys // 128
            krT = self.sb(es, "krT", [64, S_ALL], BF16)
            self.dma('sp', krT[:, 0:nkeys], self.krot_s[:, 0:nkeys], ['krot_s'], ['krT'])
            knT = [self.sb(es, "knT%d" % i, [128, S_ALL], BF16) for i in range(2)]
            vT = [self.sb(es, "vT%d" % i, [128, 32, 128], BF16) for i in range(2)]
            pT = [self.sb(es, "pT%d" % i, [128, 512], BF16) for i in range(3)]
            rden = self.sb(es, "rden", [128, 512], F32)
            oo = [self.sb(es, "oo%d" % i, [128, 512], BF16) for i in range(2)]
            pT = pT + [self.sb(es, "pT3", [128, 512], BF16)]
            NR = 4
            LA = 3

            def load_head(h):
                hi = h % 2
                self.dma('sp', knT[hi][:, 0:nkeys], self.kn_s[h, :, 0:nkeys], ['kn_s'], [('knT', hi)])
                self.dma('sp', vT[hi][:, 0:nkb_all, :], self.v_s[0:nkeys, h * 128:(h + 1) * 128].rearrange("(kb p) d -> p kb d", p=128),
                         ['v_s'], [('vT', hi)])

            its = []
            itn = 0
            for h in range(8):
                for qs in range(NSB):
                    q0 = p0 + qs * 512
                    nkb = (q0 + 512) // 128
                    bO = 4 + (itn % 2) * 2
                    for kb in range(nkb):
                        j = kb - q0 // 128
                        c0 = max(0, j) * 128
                        its.append(dict(h=h, hi=h % 2, qs=qs, qo=qs * 512, kb=kb, nkb=nkb, j=j, c0=c0, n=512 - c0, bO=bO, bD=bO + 1,
                                        oi=itn % 2, first_of_head=(qs == 0 and kb == 0)))
                    itn += 1

            def emit_S(i):
                d = its[i]
                h, hi, kb, c0, n, qo = d['h'], d['hi'], d['kb'], d['c0'], d['n'], d['qo']
                if d['first_of_head'] and h == 0:
                    load_head(0)
                r = i % NR
                self.mm(self.ps(r, n, c0), knT[hi][:, kb * 128:(kb + 1) * 128], qn[:, h, qo + c0:qo + 512], True, False,
                        [('knT', hi), ('qn', h)], [('ps', r)])
                self.mm(self.ps(r, n, c0), krT[:, kb * 128:(kb + 1) * 128], qr[:, h, qo + c0:qo + 512], False, True,
                        ['krT', ('qr', h)], [('ps', r)])
                bias = self.c_cbias if kb < 16 else self.c_zero
                self.act(pT[r][:, c0:512], self.ps(r, n, c0), AF.Exp, [('ps', r), 'cst'], [('pT', r)], scale=scale, bias=bias)
                if d['j'] >= 0:
                    self.tt('pool', pT[r][:, c0:c0 + 128], pT[r][:, c0:c0 + 128], self.trib[:], ALU.mult, [('pT', r), 'trib'], [('pT', r)])

            def emit_PV(i):
                d = its[i]
                h, hi, kb, c0, n, qo, nkb = d['h'], d['hi'], d['kb'], d['c0'], d['n'], d['qo'], d['nkb']
                bO, bD = d['bO'], d['bD']
                r = i % NR
                if d['first_of_head'] and h + 1 < 8:
                    load_head(h + 1)
                self.mm(self.ps(bO, n, c0), vT[hi][:, kb, :], pT[r][:, c0:512], kb == 0, kb == nkb - 1, [('vT', hi), ('pT', r)], [('ps', bO)])
                self.mm(self.ps(bD, n, c0), self.onesb[:], pT[r][:, c0:512], kb == 0, kb == nkb - 1, ['onesb', ('pT', r)], [('ps', bD)])
                if kb == nkb - 1:
                    self.recip(rden[:], self.ps(bD), [('ps', bD)], ['rden'])
                    o = oo[d['oi']]; ok = ('oo', d['oi'])
                    self.tt('dve', o[:], self.ps(bO), rden[:], ALU.mult, [('ps', bO), 'rden'], [ok])
                    self.dma('sp', self.o_s[h * 128:(h + 1) * 128, o0 + qo:o0 + qo + 512], o[:], [ok], ['o_s'])

            for idx in range(len(its) + LA):
                if idx < len(its):
                    emit_S(idx)
                if idx - LA >= 0:
                    emit_PV(idx - LA)
            P.barrier()

    def stage_mlstm(self, l, W, pv, mcw, bif, wif, gi):
        P = self.P
        own = gi >= 2
        p0 = gi * G
        o0 = (gi - 2) * G
        with ExitStack() as es:
            qT = self.sb(es, "qT", [128, 8, G], BF16)
            kT = self.sb(es, "kT", [128, 8, G], BF16)
            kTM = self.sb(es, "kTM", [128, NT, 1024], BF16)
            vp = self.sb(es, "vp", [128, NT, 4, 257], BF16)
            gsc = self.sb(es, "gsc", [128, NT, 16], F32)
            with ExitStack() as eb:
                xcb = self.sb(eb, "xcb", [128, 8, G], BF16)
                xmbf = self.sb(eb, "xmbf", [128, 8, G], BF16)
                with ExitStack() as e1:
                    xmb = self.sb(e1, "xmb", [128, 8, 3 + G], F32)
                    xcr = [self.sb(e1, "xc%d" % i, [128, G], F32) for i in range(2)]
                    self.dma('sp', xmb[:], self.xm_s[:, p0:p0 + 3 + G].rearrange("(c p) t -> p c t", p=128), ['xm_s'], ['xmb'])
                    if gi == 2:
                        self.ts('dve', xmb[:, :, 0:3], xmb[:, :, 0:3], self.c_flag, None, ALU.mult, None, ['xmb', 'cst'], ['xmb'])
                    for c in range(8):
                        xc = xcr[c % 2]; xk = ('xc', c % 2)
                        self.ts('dve', xc[:], xmb[:, c, 0:G], mcw[:, c, 0:1], pv['mcb'][:, c:c + 1], ALU.mult, ALU.add, ['xmb', 'mcw', 'mcb'], [xk])
                        for j in range(1, 4):
                            self.stt('dve', xc[:], xmb[:, c, j:j + G], mcw[:, c, j:j + 1], xc[:], ALU.mult, ALU.add, ['xmb', 'mcw', xk], [xk])
                        self.act(xc[:], xc[:], AF.Silu, [xk], [xk])
                        self.cp('pool', xcb[:, c, :], xc[:], [xk], [('xcb', c)])
                        self.cp('pool', xmbf[:, c, :], xmb[:, c, 3:3 + G], ['xmb'], [('xmbf', c)])
                        if own:
                            self.dma('sp', self.xc_s[c * 128:(c + 1) * 128, o0:o0 + G], xc[:], [xk], ['xc_s'])
                    P.barrier()
                with ExitStack() as e2:
                    vTf = self.sb(e2, "vTf", [128, 8, G], BF16)
                    wm = self.sb(e2, "wm", [128, 3, 8, 256], BF16)
                    vf = [self.sb(e2, "vf%d" % i, [128, 1024], F32) for i in range(2)]
                    gt = self.sb(e2, "gt", [128, NT, 16], F32)
                    for i, nm in enumerate(('w_mq', 'w_mk', 'w_mv')):
                        self.wload(wm[:, i, :, :], W[nm][l].rearrange("h (c p) e -> p (h c) e", p=128), ('wm', i))
                    nb = 0
                    for (wi, srcb, skey, dst, dkey) in ((0, xcb, 'xcb', qT, 'qT'), (1, xcb, 'xcb', kT, 'kT'), (2, xmbf, 'xmbf', vTf, 'vTf')):
                        for hh in range(4):
                            for ec in range(2):
                                b = (nb % 4) * 2; nb += 1
                                for s in range(NSB):
                                    for dc in range(2):
                                        self.mm(self.ps(b + s), wm[:, wi, hh * 2 + dc, ec * 128:(ec + 1) * 128], srcb[:, hh * 2 + dc, s * 512:(s + 1) * 512],
                                                dc == 0, dc == 1, [('wm', wi), (skey, hh * 2 + dc)], [('ps', b + s)])
                                if nb % 2 == 0:
                                    self.act(dst[:, hh * 2 + ec, :], self.psum[:, b * 512:b * 512 + G], AF.Copy, [('ps', b), ('ps', b + 1)], [(dkey, hh * 2 + ec)])
                                else:
                                    self.cp('dve', dst[:, hh * 2 + ec, :], self.psum[:, b * 512:b * 512 + G], [('ps', b), ('ps', b + 1)], [(dkey, hh * 2 + ec)])
                    allq = [('qT', i) for i in range(8)]
                    allk = [('kT', i) for i in range(8)]
                    allv = [('vTf', i) for i in range(8)]
                    for t in range(NT):
                        tsl = slice(t * 128, (t + 1) * 128)
                        b = (t % 2) * 2
                        for hh in range(4):
                            for dc in range(2):
                                self.mm(self.ps(b + hh // 2, 256, (hh % 2) * 256), xcb[:, hh * 2 + dc, tsl], wm[:, 1, hh * 2 + dc, :], dc == 0, dc == 1,
                                        [('wm', 1), ('xcb', hh * 2 + dc)], [('ps', b + hh // 2)])
                        self.act(kTM[:, t, :], self.psum[:, b * 512:b * 512 + 1024], AF.Copy, [('ps', b), ('ps', b + 1)], [('kTM', t)])
                        b2 = 4
                        for hh in range(4):
                            for dc in range(2):
                                self.mm(self.ps(b2 + hh // 2, 256, (hh % 2) * 256), xmbf[:, hh * 2 + dc, tsl], wm[:, 2, hh * 2 + dc, :], dc == 0, dc == 1,
                                        [('wm', 2), ('xmbf', hh * 2 + dc)], [('ps', b2 + hh // 2)])
                        vfi = vf[t % 2]; vk = ('vf', t % 2)
                        self.cp('dve', vfi[:], self.psum[:, b2 * 512:b2 * 512 + 1024], [('ps', b2), ('ps', b2 + 1)], [vk])
                        bg = 6 + (t % 2)
                        gk_ = ('ps', bg)
                        for i, (srcT, keys) in enumerate(((qT, allq), (kT, allk), (vTf, allv))):
                            for c in range(8):
                                self.mm(self.ps(bg, 8, 0), srcT[:, c, tsl], wif[:, i * 8 + c, :], i == 0 and c == 0, i == 2 and c == 7,
                                        ['wif', keys[c]], [gk_])
                        g = gt[:, t, :]
                        gkey = ('gt', t)
                        self.tt('dve', g[:, 0:8], self.ps(bg, 8, 0), bif[:], ALU.add, [gk_, 'bif'], [gkey])
                        self.act(g[:, 8:12], g[:, 4:8], AF.Exp, [gkey], [gkey], scale=-1.0)
                        self.act(g[:, 8:12], g[:, 8:12], AF.Ln, [gkey], [gkey], bias=1.0)
                        self.ts('dve', g[:, 8:12], g[:, 8:12], -1.0, None, ALU.mult, None, [gkey], [gkey])
                        self.mm(self.ps(bg, 4, 16), self.trif, g[:, 8:12], True, True, ['cst', gkey], [gk_])
                        self.mm(self.ps(bg, 4, 32), self.onesf[:], g[:, 8:12], True, True, ['onesf', gkey], [gk_])
                        sc = gsc[:, t, :]
                        skey = ('gsc', t)
                        self.tt('dve', g[:, 12:16], g[:, 0:4], self.ps(bg, 4, 16), ALU.subtract, [gkey, gk_], [gkey])
                        self.act(sc[:, 0:4], g[:, 12:16], AF.Exp, [gkey], [skey])
                        self.ts('dve', sc[:, 0:4], sc[:, 0:4], 1.0 / 16, None, ALU.mult, None, [skey], [skey])
                        self.act(sc[:, 4:8], self.ps(bg, 4, 16), AF.Exp, [gk_], [skey])
                        self.act(sc[:, 8:12], self.ps(bg, 4, 32), AF.Exp, [gk_], [skey])
                        for hh in range(4):
                            self.ts('dve', vp[:, t, hh, 0:256], vfi[:, hh * 256:(hh + 1) * 256], sc[:, hh:hh + 1], None, ALU.mult, None, [vk, skey], [('vp', t)])
                        self.cp('dve', vp[:, t, :, 256:257], sc[:, 0:4].rearrange("p (h o) -> p h o", o=1), [skey], [('vp', t)])
                    P.barrier()
            with ExitStack() as e3:
                if own:
                    sT = [self.sb(e3, "sT%d" % i, [128, 128], BF16) for i in range(2)]
                    nd = [self.sb(e3, "nd%d" % i, [128, 257], F32) for i in range(2)]
                    hst = self.sb(e3, "hst", [128, NT * 4, 8], F32)
                    hh_ = [self.sb(e3, "hh%d" % i, [128, 256], F32) for i in range(2)]
                    junk = self.sb(e3, "mjunk", [128, 256], F32)
                    zt = self.sb(e3, "zt", [128, 8, G], F32)
                    xcs = self.sb(e3, "xcs", [128, 8, G], F32)
                    hno = self.sb(e3, "hno", [128, 8, G], BF16)
                    self.dma('sp', zt[:], self.z_s[:, o0:o0 + G].rearrange("(c p) t -> p c t", p=128), ['z_s'], ['zt'])
                    self.dma('sp', xcs[:], self.xc_s[:, o0:o0 + G].rearrange("(c p) t -> p c t", p=128), ['xc_s'], [('xcs', c) for c in range(8)])
                    for c in range(8):
                        self.ts('pool', xcs[:, c, :], xcs[:, c, :], pv['skp'][:, c:c + 1], None, ALU.mult, None, [('xcs', c), 'skp'], [('xcs', c)])
                    self.memset('dve', hst[:], 0.0, ['hst'])
                it = 0
                pending_tail = None
                for t in range(NT):
                    tsl = slice(t * 128, (t + 1) * 128)
                    sc = gsc[:, t, :]
                    skey = ('gsc', t)
                    for hh in range(4):
                        i2 = it % 2; it += 1
                        if own:
                            bS = 0
                            for dc in range(2):
                                self.mm(self.ps(bS, 128, i2 * 128), kT[:, hh * 2 + dc, tsl], qT[:, hh * 2 + dc, tsl], dc == 0, dc == 1,
                                        [('kT', hh * 2 + dc), ('qT', hh * 2 + dc)], [('psS', i2)])
                            self.tt('dve', sT[i2][:], self.ps(bS, 128, i2 * 128), self.trif, ALU.mult, [('psS', i2), 'cst'], [('sT', i2)])
                            bN = 1 + i2
                            for dc in range(2):
                                self.mm(self.ps(bN, 257), qT[:, hh * 2 + dc, tsl], self.Cbf[:, hh * 2 + dc, :], dc == 0, False,
                                        [('qT', hh * 2 + dc), ('Cbf', hh)], [('ps', bN)])
                            self.mm(self.ps(bN, 257), sT[i2][:], vp[:, t, hh, :], False, True, [('sT', i2), ('vp', t)], [('ps', bN)])
                            n_ = nd[i2]; nk = ('nd', i2)
                            self.act(n_[:], self.ps(bN, 257), AF.Copy, [('ps', bN), skey], [nk], scale=sc[:, 4 + hh:5 + hh])
                            hs = hst[:, t * 4 + hh, :]
                            hk = ('hst', t * 4 + hh)
                            self.act(hs[:, 0:1], n_[:, 256:257], AF.Abs, [nk, 'hst'], [hk])
                            self.ts('dve', hs[:, 0:1], hs[:, 0:1], 1.0, None, ALU.max, None, [hk], [hk])
                            self.recip(hs[:, 1:2], hs[:, 0:1], [hk], [hk])
                            hv = hh_[i2]; hvk = ('hh', i2)
                            self.act(hv[:], n_[:, 0:256], AF.Copy, [nk, hk], [hvk, hk], scale=hs[:, 1:2], accum_out=hs[:, 2:3])
                            self.act(junk[:], hv[:], AF.Square, [hvk, hk], ['mjunk', hk], accum_out=hs[:, 3:4])
                            self.ts('dve', hs[:, 4:5], hs[:, 2:3], 1.0 / 256, None, ALU.mult, None, [hk], [hk])
                            self.tt('dve', hs[:, 5:6], hs[:, 4:5], hs[:, 4:5], ALU.mult, [hk], [hk])
                            self.stt('dve', hs[:, 6:7], hs[:, 3:4], 1.0 / 256, hs[:, 5:6], ALU.mult, ALU.subtract, [hk], [hk])
                            self.ts('dve', hs[:, 6:7], hs[:, 6:7], EPS, None, ALU.add, None, [hk], [hk])
                            self.act(hs[:, 6:7], hs[:, 6:7], AF.Sqrt, [hk], [hk])
                            self.recip(hs[:, 7:8], hs[:, 6:7], [hk], [hk])
                            self.ts('dve', hv[:], hv[:], hs[:, 4:5], hs[:, 7:8], ALU.subtract, ALU.mult, [hvk, hk], [hvk])
                            def tail(hv=hv, hvk=hvk, i2=i2, hh=hh, tsl=tsl):
                                bT = 7
                                for dc in range(2):
                                    self.tr(self.ps(bT, 128, (i2 * 2 + dc) * 128), hv[:, dc * 128:(dc + 1) * 128], self.idf, [hvk, 'cst'], [('psT', i2)])
                                for dc in range(2):
                                    c = hh * 2 + dc
                                    self.stt('dve', xcs[:, c, tsl], self.ps(bT, 128, (i2 * 2 + dc) * 128), pv['gng'][:, c:c + 1], xcs[:, c, tsl],
                                             ALU.mult, ALU.add, [('psT', i2), 'gng', ('xcs', c)], [('xcs', c)])
                                    self.tt('pool', hno[:, c, tsl], xcs[:, c, tsl], zt[:, c, tsl], ALU.mult, [('xcs', c), 'zt'], [('hno', c)])
                            new_tail = tail
                        for dc in range(2):
                            c = hh * 2 + dc
                            bU = 3 + (it % 2) * 2 + dc
                            self.mm(self.ps(bU, 257), kTM[:, t, hh * 256 + dc * 128:hh * 256 + (dc + 1) * 128], vp[:, t, hh, :], True, True,
                                    [('kTM', t), ('vp', t)], [('ps', bU)])
                            self.ts('dve', self.Cst[:, c, :], self.Cst[:, c, :], sc[:, 8 + hh:9 + hh], None, ALU.mult, None, [('Cst', hh), skey], [('Cst', hh)])
                            self.stt('dve', self.Cst[:, c, :], self.ps(bU, 257), sc[:, 8 + hh:9 + hh], self.Cst[:, c, :], ALU.mult, ALU.add,
                                     [('ps', bU), ('Cst', hh), skey], [('Cst', hh)])
                            self.act(self.Cbf[:, c, :], self.Cst[:, c, :], AF.Copy, [('Cst', hh)], [('Cbf', hh)])
                        if own:
                            if pending_tail is not None:
                                pending_tail()
                            pending_tail = new_tail
                if own and pending_tail is not None:
                    pending_tail()
                if gi == 1:
                    for hh in range(4):
                        for dc in range(2):
                            c = hh * 2 + dc
                            self.ts('dve', self.Cst[:, c, :], self.Cst[:, c, :], self.c_flag, None, ALU.mult, None, [('Cst', hh), 'cst'], [('Cst', hh)])
                            self.act(self.Cbf[:, c, :], self.Cst[:, c, :], AF.Copy, [('Cst', hh)], [('Cbf', hh)])
                if own:
                    for c in range(8):
                        self.dma('sp', self.hn_s[c * 128:(c + 1) * 128, o0:o0 + G], hno[:, c, :], [('hno', c)], ['hn_s'])
                P.barrier()

    def stage_merge(self, l, W, pv, gi, hT, x_src):
        P = self.P
        o0 = (gi - 2) * G
        wv = W['w_in'][l].rearrange("(c p) f -> p c f", p=128)
        wouts = [W[n][l].rearrange("(c p) f -> p c f", p=128) for n in ('w_conv_out', 'w_mla_out', 'w_mlstm_out')]
        srcs = [self.cb_s, self.o_s, self.hn_s]
        with ExitStack() as es:
            mg = self.sb(es, "mg", [128, 16, G], BF16)
            with ExitStack() as e2:
                br = [self.sb(e2, "br%d" % j, [128, 8, G], BF16) for j in range(3)]
                for j in range(3):
                    self.dma('sp', br[j][:], srcs[j][:, o0:o0 + G].rearrange("(c p) t -> p c t", p=128), [('cb_s', 'o_s', 'hn_s')[j]], [('br', j)])
                wg = [self.sb(e2, "wg%d" % i, [128, 16, 256], BF16) for i in range(4)]
                wy = [self.sb(e2, "wy%d" % i, [128, 8, 256], BF16) for i in range(4)]
                sg = [self.sb(e2, "sg%d" % i, [128, 512], F32) for i in range(2)]
                macc = [self.sb(e2, "macc%d" % i, [128, 512], F32) for i in range(2)]
                tmp = [self.sb(e2, "mtmp%d" % i, [128, 512], F32) for i in range(2)]
                nw = 0
                nb = 0
                ns = 0
                for d0 in range(0, DM, 256):
                    wgs = []
                    for j in range(3):
                        i = nw % 4; nw += 1
                        self.wload(wg[i][:], wv[:, :, 4928 + j * DM + d0:4928 + j * DM + d0 + 256], ('wg', i))
                        self.wload(wy[i][:], wouts[j][:, :, d0:d0 + 256], ('wy', i))
                        wgs.append(i)
                    for dd in range(2):
                        dc = d0 // 128 + dd
                        for s in range(NSB):
                            ssl = slice(s * 512, (s + 1) * 512)
                            ma = macc[ns % 2]; mk = ('macc', ns % 2); ns += 1
                            for j in range(3):
                                i = wgs[j]
                                bG = (nb % 4) * 2; bY = bG + 1; nb += 1
                                for k in range(16):
                                    self.mm(self.ps(bG), wg[i][:, k, dd * 128:(dd + 1) * 128], hT[:, k, ssl], k == 0, k == 15,
                                            [('wg', i), ('hT', s)], [('ps', bG)])
                                for k in range(8):
                                    self.mm(self.ps(bY), wy[i][:, k, dd * 128:(dd + 1) * 128], br[j][:, k, ssl], k == 0, k == 7,
                                            [('wy', i), ('br', j)], [('ps', bY)])
                                sgi = sg[nb % 2]; sk = ('sg', nb % 2)
                                self.act(sgi[:], self.ps(bG), AF.Sigmoid, [('ps', bG), 'bgate'], [sk], bias=pv['bgate'][:, j * 16 + dc:j * 16 + dc + 1])
                                if j == 0:
                                    self.tt('dve', ma[:], sgi[:], self.ps(bY), ALU.mult, [sk, ('ps', bY)], [mk])
                                else:
                                    tm = tmp[j % 2]; tk = ('mtmp', j % 2)
                                    self.tt('dve', tm[:], sgi[:], self.ps(bY), ALU.mult, [sk, ('ps', bY)], [tk])
                                    if j == 1:
                                        self.tt('dve', ma[:], ma[:], tm[:], ALU.add, [mk, tk], [mk])
                                    else:
                                        self.tt('dve', mg[:, dc, ssl], ma[:], tm[:], ALU.add, [mk, tk], [('mg', s)])
                P.barrier()
            self.resid_proj(es, mg, 'mg', 16, W['w_mix_out'][l], x_src, o0, self.x1_s, o0, G)
            P.barrier()

    def resid_proj(self, es, aT, akey, nk, w_d, x_src, xr0, x_dst, dr0, ntok, kchunk=16):
        wv = w_d.rearrange("(c p) f -> p c f", p=128)
        nkb = (nk + kchunk - 1) // kchunk
        wb = [self.sb(es, "rw%d" % i, [128, kchunk, 512], BF16) for i in range(3)]
        ntl = ntok // 128
        assert ntl <= 8
        xt = [self.sb(es, "rx%d" % i, [128, 512], F32) for i in range(ntl)]
        nw = 0
        for fc in range(4):
            fsl = slice(fc * 512, (fc + 1) * 512)
            for t in range(ntl):
                self.dma('sp', xt[t][:], x_src[xr0 + t * 128:xr0 + (t + 1) * 128, fsl], (), [('rx', t)])
            for kb in range(nkb):
                k0 = kb * kchunk
                kn = min(kchunk, nk - k0)
                i = nw % 3; nw += 1
                self.wload(wb[i][:, 0:kn, :], wv[:, k0:k0 + kn, fsl], ('rw', i))
                for t in range(ntl):
                    for k in range(kn):
                        self.mm(self.ps(t), aT[:, k0 + k, t * 128:(t + 1) * 128], wb[i][:, k, :], (k0 + k) == 0, (k0 + k) == nk - 1,
                                [('rw', i), (akey, t // 4)], [('ps', t)])
            for t in range(ntl):
                self.tt('dve', xt[t][:], xt[t][:], self.ps(t), ALU.add, [('rx', t), ('ps', t)], [('rx', t)])
                self.dma('sp', x_dst[dr0 + t * 128:dr0 + (t + 1) * 128, fsl], xt[t][:], [('rx', t)], ['xdst'])

    def stage_xattn(self, l, W, pv, kx, vx, gi):
        P = self.P
        o0 = (gi - 2) * G
        scale = 128.0 ** -0.5
        with ExitStack() as es:
            ox = self.sb(es, "ox", [128, 4, G], BF16)
            with ExitStack() as e1:
                hT = self.sb(e1, "hTx", [128, 16, G], BF16)
                with ExitStack() as e2:
                    self.norm_transpose(e2, self.x1_s, o0, NT, W['norm_x'][l], hT, 'hTx')
                    P.barrier()
                wq = self.sb(e1, "wxq", [128, 16, 512], BF16)
                self.wload(wq[:], W['w_xq'][l].rearrange("(c p) f -> p c f", p=128), 'wxq')
                qx = self.sb(e1, "qx", [128, 4, G], BF16)
                qf = self.sb(e1, "xqf", [128, G], F32)
                sq = self.sb(e1, "xsq", [128, G], F32)
                rs = self.sb(e1, "xrs", [128, G], F32)
                for h in range(4):
                    b = (h % 2) * 2
                    for s in range(NSB):
                        for k in range(16):
                            self.mm(self.ps(b + s), wq[:, k, h * 128:(h + 1) * 128], hT[:, k, s * 512:(s + 1) * 512], k == 0, k == 15,
                                    ['wxq', ('hTx', s)], [('ps', b + s)])
                    rb = [('ps', b), ('ps', b + 1)]
                    self.act(qf[:], self.psum[:, b * 512:b * 512 + G], AF.Copy, rb, ['xqf'])
                    self.act(sq[:], self.psum[:, b * 512:b * 512 + G], AF.Square, rb, ['xsq'])
                    for s in range(NSB):
                        self.mm(self.ps(4 + s), self.onesf[:], sq[:, s * 512:(s + 1) * 512], True, True, ['onesf', 'xsq'], [('ps', 4 + s)])
                    self.rstd(rs[:], self.psum[:, 4 * 512:4 * 512 + G], 128, [('ps', 4), ('ps', 5)], ['xrs'])
                    self.stt('dve', qx[:, h, :], qf[:], pv['xgq'][:, 0:1], rs[:], ALU.mult, ALU.mult, ['xqf', 'xrs', 'xgq'], [('qx', h)])
                pT = [self.sb(e1, "xpT%d" % i, [128, 512], BF16) for i in range(3)]
                rden = self.sb(e1, "xrden", [128, 512], F32)
                npt = 0
                it = 0
                for h in range(4):
                    for s in range(NSB):
                        ssl = slice(s * 512, (s + 1) * 512)
                        bO = 4 + (it % 2) * 2; bD = bO + 1; it += 1
                        for mc in range(2):
                            bS = npt % 3
                            pi = npt % 3; npt += 1
                            self.mm(self.ps(bS), kx[:, h, mc * 128:(mc + 1) * 128], qx[:, h, ssl], True, True, ['kx', ('qx', h)], [('ps', bS)])
                            self.act(pT[pi][:], self.ps(bS), AF.Exp, [('ps', bS)], [('xpT', pi)], scale=scale)
                            self.mm(self.ps(bO), vx[:, mc, h * 128:(h + 1) * 128], pT[pi][:], mc == 0, mc == 1, ['vx', ('xpT', pi)], [('ps', bO)])
                            self.mm(self.ps(bD), self.onesb[:], pT[pi][:], mc == 0, mc == 1, ['onesb', ('xpT', pi)], [('ps', bD)])
                        self.recip(rden[:], self.ps(bD), [('ps', bD)], ['xrden'])
                        self.tt('dve', ox[:, h, ssl], self.ps(bO), rden[:], ALU.mult, [('ps', bO), 'xrden'], [('ox', s)])
                P.barrier()
            self.resid_proj(es, ox, 'ox', 4, W['w_xo'][l], self.x1_s, o0, self.x2_s, o0, G)
            P.barrier()

    def stage_ffn(self, l, W, gi, x_out):
        P = self.P
        NJ = FFN_H // 128
        wv = W['w_ffn_in'][l].rearrange("(c p) f -> p c f", p=128)
        for half in range(G // 512):
            o0 = (gi - 2) * G + half * 512
            with ExitStack() as es:
                aT = self.sb(es, "aT", [128, NJ, 512], BF16)
                with ExitStack() as e1:
                    hT = self.sb(e1, "hTf", [128, 16, 512], BF16)
                    with ExitStack() as e2:
                        self.norm_transpose(e2, self.x2_s, o0, 4, W['norm_ffn'][l], hT, 'hTf')
                        P.barrier()
                    wg = [self.sb(e1, "fwg%d" % i, [128, 16, 512], BF16) for i in range(2)]
                    wu = [self.sb(e1, "fwu%d" % i, [128, 16, 512], BF16) for i in range(2)]
                    sg = [self.sb(e1, "fsg%d" % i, [128, 512], F32) for i in range(2)]
                    nb = 0
                    for jb in range(NJ // 4):
                        i = jb % 2
                        self.wload(wg[i][:], wv[:, :, jb * 512:(jb + 1) * 512], ('fwg', i))
                        self.wload(wu[i][:], wv[:, :, FFN_H + jb * 512:FFN_H + (jb + 1) * 512], ('fwu', i))
                        for jj in range(4):
                            j = jb * 4 + jj
                            bG = (nb % 4) * 2; bU = bG + 1; nb += 1
                            for k in range(16):
                                self.mm(self.ps(bG), wg[i][:, k, jj * 128:(jj + 1) * 128], hT[:, k, :], k == 0, k == 15, [('fwg', i), ('hTf', 0)], [('ps', bG)])
                            for k in range(16):
                                self.mm(self.ps(bU), wu[i][:, k, jj * 128:(jj + 1) * 128], hT[:, k, :], k == 0, k == 15, [('fwu', i), ('hTf', 0)], [('ps', bU)])
                            s = sg[nb % 2]; sk = ('fsg', nb % 2)
                            self.act(s[:], self.ps(bG), AF.Silu, [('ps', bG)], [sk])
                            self.tt('dve', aT[:, j, :], s[:], self.ps(bU), ALU.mult, [sk, ('ps', bU)], [('aT', 0)])
                    P.barrier()
                self.resid_proj(es, aT, 'aT', NJ, W['w_ffn_out'][l], self.x2_s, o0, x_out, o0, 512, kchunk=22)
                P.barrier()


WNAMES = ['norm_mix', 'w_in', 'b_gate', 'conv_dw', 'conv_dw_b', 'conv_ln_g', 'conv_ln_b', 'w_conv_out',
          'mla_q_norm', 'mla_kv_norm', 'w_q_up', 'w_kv_up', 'mla_g_q', 'mla_g_k', 'w_mla_out',
          'mlstm_conv_w', 'mlstm_conv_b', 'w_mq', 'w_mk', 'w_mv', 'w_if', 'b_if', 'mlstm_gn_g', 'mlstm_skip',
          'w_mlstm_out', 'w_mix_out', 'norm_x', 'norm_mem', 'w_xq', 'w_xkv', 'xattn_g_q', 'xattn_g_k', 'w_xo',
          'norm_ffn', 'w_ffn_in', 'w_ffn_out']


def build_program(shapes, layers, debug=()):
    nc = bass.Bass("TRN2", target_bir_lowering=False)
    B = Builder(nc, layers, debug)
    W = {}
    for n in WNAMES:
        shp = [len(layers)] + list(shapes[n][1:])
        W[n] = nc.dram_tensor(n, shp, F32, kind="ExternalInput").ap()
    x_own = nc.dram_tensor("x_own", [S_OWN, DM], F32, kind="ExternalInput").ap()
    x_ctx = nc.dram_tensor("x_ctx", [S_OWN, DM], F32, kind="ExternalInput").ap()
    y = nc.dram_tensor("y", [S_OWN, DM], F32, kind="ExternalOutput").ap()
    B.setup()
    if len(layers) == 1:
        B.layer(0, W, x_own, x_ctx, y)
    else:
        xl0 = nc.dram_tensor("xl0", [S_OWN, DM], F32).ap()
        ctx1 = nc.dram_tensor("ctx1", [S_OWN, DM], F32).ap()
        CH = 256
        bin_ = [nc.dram_tensor("ccin%d" % i, [CH, DM], F32).ap() for i in range(2)]
        bout = [nc.dram_tensor("ccout%d" % i, [2 * CH, DM], F32).ap() for i in range(2)]
        B.layer(0, W, x_own, x_ctx, xl0)
        rg = [[0, 1], [2, 3], [4, 5], [6, 7]]
        for i in range(S_OWN // CH):
            b = i % 2
            B.P.add('sp', lambda e, i=i, b=b: e.dma_start(out=bin_[b], in_=xl0[i * CH:(i + 1) * CH, :]), (), [('ccin', b)], dma=True)
            B.P.add('pool', lambda e, b=b: e.collective_compute("AllGather", ALU.bypass, replica_groups=rg, ins=[bin_[b]], outs=[bout[b]]),
                    [('ccin', b)], [('ccout', b)], dma=True, cc=True)
            B.P.add('sp', lambda e, i=i, b=b: e.dma_start(out=ctx1[i * CH:(i + 1) * CH, :], in_=bout[b][0:CH, :]), [('ccout', b)], ['ctx1'], dma=True)
        B.P.barrier()
        B.layer(1, W, xl0, ctx1, y)
    B.P.barrier()
    es = ExitStack()
    B.P.emit(nc, es)
    es.close()
    B.gs.close()
    return nc


def make_cst(half):
    c = np.zeros((128, 512), np.float32)
    c[:, 0:128] = np.eye(128, dtype=np.float32)
    r = np.arange(128)
    c[:, 128:256] = (r[None, :] >= r[:, None]).astype(np.float32)
    inv = (10000.0 ** (-(np.arange(32, dtype=np.float64)) / 32.0)) / (2 * math.pi)
    c[0:64, 256] = np.concatenate([inv, inv]).astype(np.float32)
    c[0:32, 257] = -1.0
    c[32:64, 257] = 1.0
    c[:, 258] = 1.0 if half == 1 else 0.0
    c[:, 259] = 0.0 if half == 1 else -30000.0
    c[:, 260] = 0.0
    return c


_PROG_CACHE = {}


def run_layer(l, inputs, x_cur, debug=(), cores=None):
    shapes = {n: inputs[n].shape for n in WNAMES}
    key = ('layer', tuple(debug))
    if key not in _PROG_CACHE:
        _PROG_CACHE[key] = build_program(shapes, [0], debug)
    nc = _PROG_CACHE[key]
    if cores is None:
        cores = list(range(8))
    wl = {n: np.ascontiguousarray(inputs[n][l:l + 1]) for n in WNAMES}
    in_maps = []
    for c in cores:
        b, half = c // 2, c % 2
        m = dict(wl)
        m['x_own'] = np.ascontiguousarray(x_cur[b, half * S_OWN:(half + 1) * S_OWN])
        m['x_ctx'] = np.ascontiguousarray(x_cur[b, 0:S_OWN])
        pos = inputs['positions'][b].astype(np.int32)
        if half == 1:
            pa = pos
        else:
            pa = np.concatenate([pos[0:S_OWN], pos[0:S_OWN]])
        m['pos'] = np.ascontiguousarray(pa[None, :])
        m['mem'] = np.ascontiguousarray(inputs['mem'][b])
        m['cst'] = make_cst(half)
        in_maps.append(m)
    res = run_bass_kernel_spmd(nc, in_maps, core_ids=list(range(len(cores))))
    return res.results


def run_fused(inputs, cores=None):
    shapes = {n: inputs[n].shape for n in WNAMES}
    key = ('fused',)
    if key not in _PROG_CACHE:
        _PROG_CACHE[key] = build_program(shapes, [0, 1])
    nc = _PROG_CACHE[key]
    if cores is None:
        cores = list(range(8))
    wl = {n: np.ascontiguousarray(inputs[n], dtype=np.float32) for n in WNAMES}
    x = inputs['x']
    in_maps = []
    for c in cores:
        b, half = c // 2, c % 2
        m = dict(wl)
        m['x_own'] = np.ascontiguousarray(x[b, half * S_OWN:(half + 1) * S_OWN], dtype=np.float32)
        m['x_ctx'] = np.ascontiguousarray(x[b, 0:S_OWN], dtype=np.float32)
        pos = inputs['positions'][b].astype(np.int32)
        if half == 1:
            pa = pos
        else:
            pa = np.concatenate([pos[0:S_OWN], pos[0:S_OWN]])
        m['pos'] = np.ascontiguousarray(pa[None, :])
        m['mem'] = np.ascontiguousarray(inputs['mem'][b], dtype=np.float32)
        m['cst'] = make_cst(half)
        in_maps.append(m)
    res = run_bass_kernel_spmd(nc, in_maps, core_ids=list(range(len(cores))))
    return res.results


def kernel(**inputs):
    inputs = {k: np.asarray(v) for k, v in inputs.items()}
    outs = run_fused(inputs)
    x = inputs['x']
    out = np.empty(x.shape, np.float32)
    for c in range(8):
        b, half = c // 2, c % 2
        out[b, half * S_OWN:(half + 1) * S_OWN] = outs[c]['y']
    return out
```
